# Optimizing a Trainium2 kernel written in Bass

```python
import math
import jax, jax.numpy as jnp
from jax import lax
import numpy as np

D_MODEL = 4096
BATCH = 1
SEQ = 16384
DEPTH = 4

D_MIX = D_MODEL
D_HGRN = D_MIX // 2
D_DIFF = D_MIX - D_HGRN
HG_HEAD_DIM = 128
HG_HEADS = D_HGRN // HG_HEAD_DIM
HG_CHUNK = 64
DA_HEAD_DIM = 128
DA_V_DIM = 2 * DA_HEAD_DIM
DA_HEADS = D_DIFF // DA_V_DIM
Q_BLOCK = 128
IN_COLS = 5 * D_HGRN + 3 * D_DIFF
D_FF = 2 * D_MODEL
CONV_WIDTH = 3
N_MOD = 6
EPS = 1e-6

kernel_name = "hymba_hgrn2_diffattn_convglu_adaln_encoder"


def rmsnorm(x, g):
    x32 = x.astype(jnp.float32)
    y = x32 * lax.rsqrt(jnp.mean(x32 * x32, axis=-1, keepdims=True) + EPS)
    return (y * g.astype(jnp.float32)).astype(x.dtype)


def modulate(h, shift, scale):
    return h * (1.0 + scale) + shift


def alibi_slopes(n_heads):
    return jnp.exp2(-8.0 * jnp.arange(1, n_heads + 1, dtype=jnp.float32) / n_heads)


def forget_log(z, lb):
    B, T, _ = z.shape
    z = z.reshape(B, T, HG_HEADS, HG_HEAD_DIM).astype(jnp.float32)
    lbh = lb.reshape(HG_HEADS, HG_HEAD_DIM)
    return jnp.log(lbh + (1.0 - lbh) * jax.nn.sigmoid(z))


def hgrn2_scan(q, v, logf):
    B, T, H, K = q.shape
    V = v.shape[-1]
    N = T // HG_CHUNK

    def to_chunks(a):
        return a.reshape(B, N, HG_CHUNK, H, a.shape[-1]).transpose(1, 0, 3, 2, 4)

    mask = jnp.tril(jnp.ones((HG_CHUNK, HG_CHUNK), dtype=bool))

    def step(S, inp):
        qc, vc, gc = inp
        kc = -jnp.expm1(gc)
        b = jnp.cumsum(gc, axis=2)
        bl = b[:, :, -1:, :]
        o_inter = jnp.einsum('bhtk,bhkv->bhtv', qc * jnp.exp(b), S)
        rel = jnp.where(mask[:, :, None], b[:, :, :, None, :] - b[:, :, None, :, :], -jnp.inf)
        a = jnp.einsum('bhtk,bhtsk,bhsk->bhts', qc, jnp.exp(rel), kc)
        o = o_inter + jnp.einsum('bhts,bhsv->bhtv', a, vc)
        S = jnp.exp(bl[:, :, 0, :])[..., None] * S + jnp.einsum('bhsk,bhsv->bhkv', kc * jnp.exp(bl - b), vc)
        return S, o

    S0 = jnp.zeros((B, H, K, V), jnp.float32)
    _, o = lax.scan(step, S0, (to_chunks(q), to_chunks(v), to_chunks(logf)))
    return o.transpose(1, 0, 3, 2, 4).reshape(B, T, H, V)


def hgrn2_group(hq, hf_f, hf_b, hi, hg, lb_f, lb_b, norm_g):
    B, T, _ = hq.shape
    shp = (B, T, HG_HEADS, HG_HEAD_DIM)
    q = hq.reshape(shp).astype(jnp.float32) * (HG_HEAD_DIM ** -0.5)
    v = hi.reshape(shp).astype(jnp.float32)
    o_fwd = hgrn2_scan(q, v, forget_log(hf_f, lb_f))
    o_bwd = hgrn2_scan(q[:, ::-1], v[:, ::-1], forget_log(hf_b, lb_b)[:, ::-1])[:, ::-1]
    o = rmsnorm(o_fwd + o_bwd, norm_g) * jax.nn.silu(hg.reshape(shp).astype(jnp.float32))
    return o.reshape(B, T, D_HGRN).astype(hq.dtype)


def diff_attention_group(dq, dk, dv, lam_params, norm_g, layer_idx):
    B, T, _ = dq.shape
    q = dq.reshape(B, T, DA_HEADS, 2, DA_HEAD_DIM).astype(jnp.float32) * (DA_HEAD_DIM ** -0.5)
    k = dk.reshape(B, T, DA_HEADS, 2, DA_HEAD_DIM).astype(jnp.float32)
    v = dv.reshape(B, T, DA_HEADS, DA_V_DIM).astype(jnp.float32)
    lam_init = 0.8 - 0.6 * math.exp(-0.3 * layer_idx)
    lp = lam_params.astype(jnp.float32)
    lam = jnp.exp(jnp.sum(lp[0] * lp[1])) - jnp.exp(jnp.sum(lp[2] * lp[3])) + lam_init
    slopes = alibi_slopes(DA_HEADS)
    nb = T // Q_BLOCK
    pos = jnp.arange(T, dtype=jnp.int32)
    qb = q.reshape(B, nb, Q_BLOCK, DA_HEADS, 2, DA_HEAD_DIM).transpose(1, 0, 3, 4, 2, 5)
    kt = k.transpose(0, 2, 3, 1, 4)
    vt = v.transpose(0, 2, 1, 3)
    qpos = pos.reshape(nb, Q_BLOCK)

    def block(args):
        qblk, tq = args
        s = jnp.einsum('bhjqd,bhjkd->bhjqk', qblk, kt)
        dist = jnp.abs(tq[:, None] - pos[None, :]).astype(jnp.float32)
        s = s - slopes[None, :, None, None, None] * dist
        p = jax.nn.softmax(s, axis=-1)
        a = p[:, :, 0] - lam * p[:, :, 1]
        return jnp.einsum('bhqk,bhkv->bhqv', a, vt)

    o = lax.map(block, (qb, qpos))
    o = o.transpose(1, 0, 3, 2, 4).reshape(B, T, DA_HEADS, DA_V_DIM)
    o = rmsnorm(o, norm_g) * (1.0 - lam_init)
    return o.reshape(B, T, D_DIFF).astype(dq.dtype)


def token_mixer(h, w_in_l, lb_f, lb_b, hg_norm_g_l, lam_l, da_norm_g_l, w_out_l, layer_idx):
    proj = h @ w_in_l
    splits = [D_HGRN, 2 * D_HGRN, 3 * D_HGRN, 4 * D_HGRN, 5 * D_HGRN,
              5 * D_HGRN + D_DIFF, 5 * D_HGRN + 2 * D_DIFF]
    hq, hf_f, hf_b, hi, hg, dq, dk, dv = jnp.split(proj, splits, axis=-1)
    o_hgrn = hgrn2_group(hq, hf_f, hf_b, hi, hg, lb_f, lb_b, hg_norm_g_l)
    o_diff = diff_attention_group(dq, dk, dv, lam_l, da_norm_g_l, layer_idx)
    return jnp.concatenate([o_hgrn, o_diff], axis=-1) @ w_out_l


def conv_glu(h, w_up_l, cw, cb, w_down_l):
    u = h @ w_up_l
    up = jnp.pad(u, ((0, 0), (1, 1), (0, 0)))
    u = up[:, :-2] * cw[0] + up[:, 1:-1] * cw[1] + up[:, 2:] * cw[2] + cb
    gate, val = jnp.split(u, 2, axis=-1)
    return (jax.nn.silu(gate) * val) @ w_down_l


def setup_inputs(seed: int = 0) -> dict:
    key = jax.random.key(seed)
    ks = jax.random.split(key, 18)

    def nrm(k, shape, scale):
        return jax.random.normal(k, shape, jnp.float32) * scale

    return {
        "x": nrm(ks[0], (BATCH, SEQ, D_MODEL), 1.0),
        "c": nrm(ks[1], (BATCH, D_MODEL), 1.0),
        "w_ada": nrm(ks[2], (D_MODEL, N_MOD * D_MODEL), 0.5 * D_MODEL ** -0.5),
        "b_ada": nrm(ks[3], (N_MOD * D_MODEL,), 0.01),
        "ada_table": nrm(ks[4], (DEPTH, N_MOD, D_MODEL), 0.1),
        "norm1_g": 1.0 + nrm(ks[5], (DEPTH, D_MODEL), 0.02),
        "w_in": nrm(ks[6], (DEPTH, D_MODEL, IN_COLS), D_MODEL ** -0.5),
        "hg_lb_logits": nrm(ks[7], (2, DEPTH, D_HGRN), 0.5),
        "hg_norm_g": 1.0 + nrm(ks[8], (DEPTH, HG_HEAD_DIM), 0.02),
        "da_lambda": nrm(ks[9], (DEPTH, 4, DA_HEAD_DIM), 0.1),
        "da_norm_g": 1.0 + nrm(ks[10], (DEPTH, DA_V_DIM), 0.02),
        "w_out": nrm(ks[11], (DEPTH, D_MIX, D_MODEL), D_MIX ** -0.5),
        "norm2_g": 1.0 + nrm(ks[12], (DEPTH, D_MODEL), 0.02),
        "w_up": nrm(ks[13], (DEPTH, D_MODEL, 2 * D_FF), D_MODEL ** -0.5),
        "conv_w": nrm(ks[14], (DEPTH, CONV_WIDTH, 2 * D_FF), CONV_WIDTH ** -0.5),
        "conv_b": nrm(ks[15], (DEPTH, 2 * D_FF), 0.02),
        "w_down": nrm(ks[16], (DEPTH, D_FF, D_MODEL), D_FF ** -0.5),
        "final_g": 1.0 + nrm(ks[17], (D_MODEL,), 0.02),
    }


def reference(x, c, w_ada, b_ada, ada_table, norm1_g, w_in, hg_lb_logits, hg_norm_g, da_lambda,
              da_norm_g, w_out, norm2_g, w_up, conv_w, conv_b, w_down, final_g):
    B, T, D = x.shape
    mod = (jax.nn.silu(c) @ w_ada + b_ada).reshape(B, N_MOD, D)
    p = jax.nn.softmax(hg_lb_logits.astype(jnp.float32), axis=1)
    lb = jnp.cumsum(p, axis=1)
    lb = lb - lb[:, :1]
    for l in range(DEPTH):
        m = mod + ada_table[l]
        shift1, scale1, gate1 = m[:, 0, None, :], m[:, 1, None, :], m[:, 2, None, :]
        shift2, scale2, gate2 = m[:, 3, None, :], m[:, 4, None, :], m[:, 5, None, :]
        h = modulate(rmsnorm(x, norm1_g[l]), shift1, scale1)
        x = x + gate1 * token_mixer(h, w_in[l], lb[0, l], lb[1, l], hg_norm_g[l], da_lambda[l],
                                    da_norm_g[l], w_out[l], l)
        h = modulate(rmsnorm(x, norm2_g[l]), shift2, scale2)
        x = x + gate2 * conv_glu(h, w_up[l], conv_w[l], conv_b[l], w_down[l])
    return rmsnorm(x, final_g)
```

```python
import contextlib
import math
import numpy as np
import ml_dtypes
import concourse.bass as bass
import concourse.mybir as mybir
from concourse.bass_utils import run_bass_kernel_spmd

F32 = mybir.dt.float32
BF16 = mybir.dt.bfloat16
AF = mybir.ActivationFunctionType
ALU = mybir.AluOpType
AX = mybir.AxisListType

D = 4096
KC = 32
T = 16384
NCORE = 8
TL = T // NCORE
DEPTH = 4
DFF = 8192
EPS = 1e-6
NVEC = 40
HCH = 32
DBG_PREP = [0]
VQ = ['sp']


class Op:
    __slots__ = ("q", "fn", "reads", "writes", "dma", "sem", "deps", "signal", "count", "idx", "prev")


class Prog:
    QUEUES = ("pe", "dve", "act", "pool", "sp")
    RING = 8
    _scnt = [0]

    def __init__(self, nc, strict=True):
        self.nc = nc
        self.ops = []
        self.strict = strict

    def add(self, q, fn, reads=(), writes=(), dma=False):
        o = Op()
        o.q, o.fn, o.reads, o.writes, o.dma = q, fn, tuple(reads), tuple(writes), dma
        o.sem = None
        o.deps, o.signal, o.count, o.idx, o.prev = [], False, 0, len(self.ops), 0
        self.ops.append(o)
        return o

    def dma(self, q, out, in_, reads, writes, **kw):
        return self.add(q, lambda e: e.dma_start(out=out, in_=in_, **kw), reads, writes, dma=True)

    def emit(self):
        nc = self.nc
        ops = self.ops
        last_w = {}
        readers = {}
        for o in ops:
            deps = set()
            for k in o.reads:
                if k in last_w:
                    deps.add(last_w[k])
            for k in o.writes:
                if k in last_w:
                    deps.add(last_w[k])
                for r in readers.get(k, ()):
                    deps.add(r)
            deps.discard(o.idx)
            o.deps = sorted(deps)
            for k in o.writes:
                last_w[k] = o.idx
                readers[k] = []
            for k in o.reads:
                readers.setdefault(k, []).append(o.idx)
        for o in ops:
            if o.dma:
                o.signal = True
            for d in o.deps:
                a = ops[d]
                if a.dma or a.q != o.q or (self.strict and a.q != "pe"):
                    a.signal = True
        counts = {}
        nd = {}
        for o in ops:
            if not o.signal:
                continue
            if o.dma:
                k = nd.get(o.q, 0)
                nd[o.q] = k + 1
                name = ("dq", o.q, k % self.RING)
            else:
                name = ("eng", o.q)
            o.sem = name
            o.prev = counts.get(name, 0)
            counts[name] = o.prev + (16 if o.dma else 1)
            o.count = counts[name]
        with contextlib.ExitStack() as st:
            sems = {}
            for i, name in enumerate(counts):
                Prog._scnt[0] += 1
                sems[name] = st.enter_context(nc.semaphore("s%d" % Prog._scnt[0]))
            block = st.enter_context(nc.Block())
            engs = {"pe": block.tensor, "dve": block.vector, "act": block.scalar,
                    "pool": block.gpsimd, "sp": block.sync}
            for q in self.QUEUES:
                qops = [o for o in ops if o.q == q]
                if not qops and q != "sp":
                    continue

                def body(e, qops=qops, q=q):
                    seen = {}

                    def wait(name, v):
                        if v > 0 and seen.get(name, 0) < v:
                            e.wait_ge(sems[name], v)
                            seen[name] = v
                    for o in qops:
                        need = {}
                        for d in o.deps:
                            a = ops[d]
                            if not a.signal:
                                continue
                            if (not a.dma) and a.q == q and (q == "pe" or not self.strict):
                                continue
                            if need.get(a.sem, 0) < a.count:
                                need[a.sem] = a.count
                        for nm, v in need.items():
                            wait(nm, v)
                        if o.dma:
                            wait(o.sem, o.prev)
                        ins = o.fn(e)
                        if o.signal:
                            ins.then_inc(sems[o.sem], 16 if o.dma else 1)
                    if q == "sp":
                        for name, c in counts.items():
                            if name[0] == "dq":
                                wait(name, c)
                engs[q](body)


class Stage:
    _cnt = [0]

    def __init__(self, nc, name):
        Stage._cnt[0] += 1
        self.nc, self.name = nc, "%s%d" % (name, Stage._cnt[0])
        self.st = contextlib.ExitStack()
        self.P = Prog(nc)
        self.n = 0

    def sb(self, shape, dt=F32, name=None):
        self.n += 1
        return self.st.enter_context(self.nc.sbuf_tensor("%s_%s%d" % (self.name, name or "t", self.n), list(shape), dt))

    def ps(self, shape=(128, 512), dt=F32, name=None):
        self.n += 1
        return self.st.enter_context(self.nc.psum_tensor("%s_%s%d" % (self.name, name or "p", self.n), list(shape), dt))

    def finish(self):
        self.P.emit()
        self.st.close()


def stage_params(nc, vecs, w_ada, lbl, idf_d, par, lbo):
    S = Stage(nc, "par")
    P = S.P
    V = S.sb([128, 10, 128])
    VT = S.sb([128, NVEC, 32])
    idf = S.sb([128, 128])
    sc = S.sb([128, 32])
    modT = S.sb([128, 6, 32])
    mm = S.sb([128, 6, 32])
    outp = S.sb([128, DEPTH * 6 + 1, 32])
    wsl = [S.sb([128, KC, 512]) for _ in range(2)]
    LR = S.sb([128, 128])
    LT = S.sb([128, 2, DEPTH, 16])
    EX = S.sb([128, 2, DEPTH, 16])
    mx = S.sb([128, 2, 16])
    sm = S.sb([128, 2, 16])
    lbs = S.sb([128, 2, 2, DEPTH, 16])
    pst = [S.ps() for _ in range(2)]
    psm = S.ps()

    P.dma("sp", idf[:], idf_d, [], ["idf"])
    P.dma("sp", V[:], vecs.rearrange("(i r) p -> r i p", r=128), [], ["V"])
    P.dma("sp", LR[:], lbl, [], ["LR"])
    for i in range(10):
        b = i % 2
        P.add("pe", lambda e, i=i, b=b: e.transpose(out=pst[b][:, 0:128], in_=V[:, i, :], identity=idf[:]),
              ["V", "idf"], [("pst", b)])
        P.add("dve", lambda e, i=i, b=b: e.tensor_copy(
            out=VT[:, 4 * i:4 * i + 4, :], in_=pst[b][:, 0:128].rearrange("p (v c) -> p v c", c=32)),
            [("pst", b)], ["VT"])
    P.add("act", lambda e: e.activation(out=sc[:], in_=VT[:, 0, :], func=AF.Silu), ["VT"], ["sc"])
    wv = w_ada.rearrange("(c p) n -> p c n", p=128)
    NSL = (6 * D) // 512
    for s in range(NSL):
        b = s % 2
        P.dma("sp", wsl[b][:], wv[:, :, s * 512:(s + 1) * 512], [], [("wsl", b)])
        for g in range(4):
            col = s * 4 + g
            for c in range(KC):
                P.add("pe", lambda e, b=b, g=g, c=c, col=col: e.matmul(
                    psm[:, col:col + 1], lhsT=wsl[b][:, c, g * 128:(g + 1) * 128], rhs=sc[:, c:c + 1],
                    start=(c == 0), stop=(c == KC - 1)), [("wsl", b), "sc"], ["psm"])
    P.add("dve", lambda e: e.tensor_tensor(out=modT[:], in0=psm[:, 0:192].rearrange("p (j c) -> p j c", c=32),
                                           in1=VT[:, 1:7, :], op=ALU.add), ["psm", "VT"], ["modT"])
    for l in range(DEPTH):
        P.add("dve", lambda e, l=l: e.tensor_tensor(out=mm[:], in0=modT[:], in1=VT[:, 7 + 6 * l:13 + 6 * l, :], op=ALU.add),
              ["modT", "VT", "outp"], ["mm"])
        P.add("dve", lambda e, l=l: e.scalar_tensor_tensor(out=outp[:, 6 * l + 0, :], in0=mm[:, 1, :], scalar=1.0,
                                                           in1=VT[:, 31 + l, :], op0=ALU.add, op1=ALU.mult), ["mm", "VT"], ["outp"])
        P.add("dve", lambda e, l=l: e.tensor_copy(out=outp[:, 6 * l + 1, :], in_=mm[:, 0, :]), ["mm"], ["outp"])
        P.add("dve", lambda e, l=l: e.tensor_copy(out=outp[:, 6 * l + 2, :], in_=mm[:, 2, :]), ["mm"], ["outp"])
        P.add("dve", lambda e, l=l: e.scalar_tensor_tensor(out=outp[:, 6 * l + 3, :], in0=mm[:, 4, :], scalar=1.0,
                                                           in1=VT[:, 35 + l, :], op0=ALU.add, op1=ALU.mult), ["mm", "VT"], ["outp"])
        P.add("dve", lambda e, l=l: e.tensor_copy(out=outp[:, 6 * l + 4, :], in_=mm[:, 3, :]), ["mm"], ["outp"])
        P.add("dve", lambda e, l=l: e.tensor_copy(out=outp[:, 6 * l + 5, :], in_=mm[:, 5, :]), ["mm"], ["outp"])
    P.add("dve", lambda e: e.tensor_copy(out=outp[:, 6 * DEPTH, :], in_=VT[:, 39, :]), ["VT"], ["outp"])
    P.dma("sp", par, outp[:], ["outp"], ["par"])
    P.add("pe", lambda e: e.transpose(out=pst[0][:, 0:128], in_=LR[:], identity=idf[:]), ["LR", "idf", ("pst", 0)], [("pst", 0)])
    P.add("dve", lambda e: e.tensor_copy(out=LT[:], in_=pst[0][:, 0:128].rearrange("p (d l h) -> p d l h", d=2, l=DEPTH)),
          [("pst", 0)], ["LT"])
    P.add("dve", lambda e: e.tensor_tensor(out=mx[:], in0=LT[:, :, 0, :], in1=LT[:, :, 1, :], op=ALU.max), ["LT"], ["mx"])
    for l in (2, 3):
        P.add("dve", lambda e, l=l: e.tensor_tensor(out=mx[:], in0=mx[:], in1=LT[:, :, l, :], op=ALU.max), ["LT", "mx"], ["mx"])
    for l in range(DEPTH):
        P.add("dve", lambda e, l=l: e.tensor_tensor(out=EX[:, :, l, :], in0=LT[:, :, l, :], in1=mx[:], op=ALU.subtract),
              ["LT", "mx"], ["EX"])
    P.add("act", lambda e: e.activation(out=EX[:], in_=EX[:], func=AF.Exp), ["EX"], ["EX"])
    P.add("dve", lambda e: e.tensor_tensor(out=sm[:], in0=EX[:, :, 0, :], in1=EX[:, :, 1, :], op=ALU.add), ["EX"], ["sm"])
    for l in (2, 3):
        P.add("dve", lambda e, l=l: e.tensor_tensor(out=sm[:], in0=sm[:], in1=EX[:, :, l, :], op=ALU.add), ["EX", "sm"], ["sm"])
    P.add("dve", lambda e: e.reciprocal(out=sm[:], in_=sm[:]), ["sm"], ["sm"])
    for l in range(DEPTH):
        P.add("dve", lambda e, l=l: e.tensor_tensor(out=EX[:, :, l, :], in0=EX[:, :, l, :], in1=sm[:], op=ALU.mult),
              ["EX", "sm"], ["EX"])
    P.add("dve", lambda e: e.memset(lbs[:, 0, :, 0, :], 0.0), [], ["lbs"])
    P.add("dve", lambda e: e.tensor_copy(out=lbs[:, 0, :, 1, :], in_=EX[:, :, 1, :]), ["EX", "lbs"], ["lbs"])
    for l in (2, 3):
        P.add("dve", lambda e, l=l: e.tensor_tensor(out=lbs[:, 0, :, l, :], in0=lbs[:, 0, :, l - 1, :], in1=EX[:, :, l, :], op=ALU.add),
              ["EX", "lbs"], ["lbs"])
    P.add("dve", lambda e: e.tensor_scalar(out=lbs[:, 1], in0=lbs[:, 0], scalar1=-1.0, scalar2=1.0, op0=ALU.mult, op1=ALU.add),
          ["lbs"], ["lbs"])
    P.dma("sp", lbo, lbs[:], ["lbs"], ["lbo"])
    S.finish()


def stage_pre(nc, x, xT, idf_d, ntok=TL):
    S = Stage(nc, "pre")
    P = S.P
    idf = S.sb([128, 128])
    xin = [S.sb([128, D]) for _ in range(2)]
    xo = [S.sb([128, 4, 128]) for _ in range(4)]
    pst = [S.ps() for _ in range(4)]
    P.dma("sp", idf[:], idf_d, [], ["idf"])
    xTv = xT.rearrange("(c p) t -> p c t", p=128)
    k = 0
    for tb in range(ntok // 128):
        b = tb % 2
        P.dma("sp", xin[b][:], x[tb * 128:(tb + 1) * 128, :], [], [("xin", b)])
        for cg in range(KC // 4):
            r = k % 4
            k += 1
            for j in range(4):
                c = cg * 4 + j
                P.add("pe", lambda e, b=b, c=c, j=j, r=r: e.transpose(out=pst[r][:, j * 128:(j + 1) * 128],
                                                                      in_=xin[b][:, c * 128:(c + 1) * 128], identity=idf[:]),
                      [("xin", b), "idf"], [("pst", r)])
            eng = "act" if (k % 2) else "dve"
            if eng == "act":
                P.add("act", lambda e, r=r: e.copy(out=xo[r][:], in_=pst[r][:].rearrange("p (j t) -> p j t", j=4)),
                      [("pst", r)], [("xo", r)])
            else:
                P.add("dve", lambda e, r=r: e.tensor_copy(out=xo[r][:], in_=pst[r][:].rearrange("p (j t) -> p j t", j=4)),
                      [("pst", r)], [("xo", r)])
            P.dma("pool", xTv[:, cg * 4:cg * 4 + 4, tb * 128:(tb + 1) * 128], xo[r][:], [("xo", r)], [("xT", tb, cg)])
    S.finish()


def stage_norm(nc, xT, par, ia, ib, out, idf_d=None, final=False, ntok=TL, TT=512, out_off=0):
    S = Stage(nc, "nrm")
    P = S.P
    A = S.sb([128, 32])
    B = S.sb([128, 32])
    ones = S.sb([128, 128])
    xt = [S.sb([128, KC, TT]) for _ in range(2)]
    sq = [S.sb([128, TT]) for _ in range(3)]
    rstd = S.sb([128, TT])
    tmp = [S.sb([128, TT]) for _ in range(3)]
    pss = S.ps()
    P.dma("sp", A[:], par[:, ia, :], [], ["A"])
    if not final:
        P.dma("sp", B[:], par[:, ib, :], [], ["B"])
        ho = [S.sb([128, KC, TT], BF16) for _ in range(2)]
        ov = out.rearrange("(c p) t -> p c t", p=128)
    else:
        idf = S.sb([128, 128])
        P.dma("sp", idf[:], idf_d, [], ["idf"])
        yo = [S.sb([128, TT]) for _ in range(3)]
        pst = [S.ps() for _ in range(3)]
        yrow = [S.sb([128, D]) for _ in range(2)]
    P.add("dve", lambda e: e.memset(ones[:], 1.0 / D), [], ["ones"])
    epst = S.sb([128, 1])
    P.add("dve", lambda e: e.memset(epst[:], EPS), [], ["eps"])
    xv = xT.rearrange("(c p) t -> p c t", p=128)
    nt = ntok // TT
    for tt in range(nt):
        b = tt % 2
        P.dma("sp", xt[b][:], xv[:, :, tt * TT:(tt + 1) * TT], [], [("xt", b)])
        for c in range(KC):
            r = c % 3
            P.add("act", lambda e, b=b, c=c, r=r: e.activation(out=sq[r][:], in_=xt[b][:, c, :], func=AF.Square),
                  [("xt", b)], [("sq", r)])
            P.add("pe", lambda e, c=c, r=r: e.matmul(pss[:, 0:TT], lhsT=ones[:], rhs=sq[r][:], start=(c == 0), stop=(c == KC - 1)),
                  [("sq", r), "ones"], ["pss"])
        P.add("act", lambda e: e.activation(out=rstd[:], in_=pss[:, 0:TT], func=AF.Sqrt, bias=epst[:, 0:1], scale=1.0),
              ["pss", "eps"], ["rstd"])
        P.add("dve", lambda e: e.reciprocal(out=rstd[:], in_=rstd[:]), ["rstd"], ["rstd"])
        if not final:
            for c in range(KC):
                r = c % 3
                P.add("dve", lambda e, b=b, c=c, r=r: e.scalar_tensor_tensor(out=tmp[r][:], in0=xt[b][:, c, :], scalar=A[:, c:c + 1],
                                                                             in1=rstd[:], op0=ALU.mult, op1=ALU.mult),
                      [("xt", b), "A", "rstd"], [("tmp", r)])
                P.add("act", lambda e, b=b, c=c, r=r: e.activation(out=ho[b][:, c, :], in_=tmp[r][:], func=AF.Identity,
                                                                   bias=B[:, c:c + 1], scale=1.0),
                      [("tmp", r), "B"], [("ho", b)])
            P.dma("pool", ov[:, :, out_off + tt * TT:out_off + (tt + 1) * TT], ho[b][:], [("ho", b)], [("out", tt)])
        else:
            for c in range(KC):
                r = c % 3
                P.add("dve", lambda e, b=b, c=c, r=r: e.scalar_tensor_tensor(out=yo[r][:], in0=xt[b][:, c, :], scalar=A[:, c:c + 1],
                                                                             in1=rstd[:], op0=ALU.mult, op1=ALU.mult),
                      [("xt", b), "A", "rstd"], [("yo", r)])
                for tb in range(TT // 128):
                    P.add("pe", lambda e, r=r, tb=tb: e.transpose(out=pst[r][:, tb * 128:(tb + 1) * 128],
                                                                  in_=yo[r][:, tb * 128:(tb + 1) * 128], identity=idf[:]),
                          [("yo", r), "idf"], [("pst", r)])
                P.add("act", lambda e, r=r: e.copy(out=tmp[r][:], in_=pst[r][:, 0:TT]), [("pst", r)], [("tmp", r)])
                for tb in range(TT // 128):
                    t0 = tt * TT + tb * 128
                    P.dma("pool", out[t0:t0 + 128, c * 128:(c + 1) * 128], tmp[r][:, tb * 128:(tb + 1) * 128],
                          [("tmp", r)], [("out", tt, c, tb)])
    S.finish()


def stage_inproj(nc, hT_all, w, projT, vtok, ntok=T, TT=512):
    S = Stage(nc, "inp")
    P = S.P
    NCOL = 2048
    W = S.sb([128, KC, NCOL], BF16)
    ht = [S.sb([128, KC, TT], BF16) for _ in range(2)]
    og = [S.sb([128, 4, TT], BF16) for _ in range(2)]
    vo = [S.sb([128, TT // 128, 256], BF16) for _ in range(2)]
    ps = [S.ps() for _ in range(4)]
    vtv = vtok.rearrange("(n p) d -> p n d", p=128)
    wv = w.rearrange("(c p) n -> p c n", p=128)
    for blk in range(8):
        P.dma("pool", W[:, :, blk * 256:(blk + 1) * 256], wv[:, :, blk * 256:(blk + 1) * 256], [], [("W", blk)])
    hv = hT_all.rearrange("(c p) t -> p c t", p=128)
    pv = projT.rearrange("(g p) t -> p g t", p=128)
    qscale = 128.0 ** -0.5
    k = 0
    for tt in range(ntok // TT):
        b = tt % 2
        P.dma("sp", ht[b][:], hv[:, :, tt * TT:(tt + 1) * TT], [], [("ht", b)])
        for cg in range(14):
            r = k % 4
            k += 1
            for c in range(KC):
                P.add("pe", lambda e, b=b, c=c, cg=cg, r=r: e.matmul(ps[r][:, 0:TT], lhsT=W[:, c, cg * 128:(cg + 1) * 128],
                                                                     rhs=ht[b][:, c, :], start=(c == 0), stop=(c == KC - 1)),
                      [("W", cg // 2), ("ht", b)], [("ps", r)])
            ob = (k // 4) % 2 if False else ((tt * 4 + cg // 4) % 2)
            sc = qscale if cg in (0, 1, 10, 11) else 1.0
            if k % 2:
                P.add("act", lambda e, r=r, ob=ob, cg=cg, sc=sc: e.activation(out=og[ob][:, cg % 4, :], in_=ps[r][:, 0:TT],
                                                                             func=AF.Copy, scale=sc),
                      [("ps", r)], [("og", ob)])
            else:
                P.add("dve", lambda e, r=r, ob=ob, cg=cg, sc=sc: e.tensor_scalar(out=og[ob][:, cg % 4, :], in0=ps[r][:, 0:TT],
                                                                                scalar1=sc, scalar2=None, op0=ALU.mult),
                      [("ps", r)], [("og", ob)])
            if cg % 4 == 3 or cg == 13:
                g0 = (cg // 4) * 4
                ng = cg - g0 + 1
                P.dma("pool", pv[:, g0:g0 + ng, tt * TT:(tt + 1) * TT], og[ob][:, 0:ng, :], [("og", ob)], [("proj", tt, g0)])
        for tb in range(TT // 128):
            r = k % 4
            k += 1
            for c in range(KC):
                P.add("pe", lambda e, b=b, c=c, tb=tb, r=r: e.matmul(ps[r][:, 0:256], lhsT=ht[b][:, c, tb * 128:(tb + 1) * 128],
                                                                     rhs=W[:, c, 1792:2048], start=(c == 0), stop=(c == KC - 1)),
                      [("W", 7), ("ht", b)], [("ps", r)])
            P.add("dve", lambda e, r=r, b=b, tb=tb: e.tensor_copy(out=vo[b][:, tb, :], in_=ps[r][:, 0:256]), [("ps", r)], [("vo", b)])
        P.dma("pool", vtv[:, tt * (TT // 128):(tt + 1) * (TT // 128), :], vo[b][:], [("vo", b)], [("vtok", tt)])
    S.finish()


def attn_consts(j):
    m = 2.0 ** (-8.0 * (j + 1) / 8.0)
    p = np.arange(128, dtype=np.float64)[:, None]
    n = np.arange(128, dtype=np.float64)[None, :]
    BL = -m * (128.0 * n - p)
    BR = -m * (p + 128.0 * n + 1.0)
    f = np.arange(256, dtype=np.float64)[None, :]
    Dg = [-m * np.abs(f - p - 128.0 * a) for a in range(2)]
    qb = np.arange(2, dtype=np.float64)[None, :]
    fL = np.exp(-m * (p + 128.0 * qb))
    fR = np.exp(-m * (255.0 - p - 128.0 * qb))
    return np.concatenate([BL, BR, Dg[0], Dg[1], fL, fR], axis=1).astype(np.float32)


def stage_attn(nc, projT, vtok, acst_d, lam_d, gn_d, idb_d, oT_rows, lamc_d, ntok=T, dbg=None):
    S = Stage(nc, "att")
    P = S.P
    NKB = ntok // 128
    NQG = ntok // 256
    kT = S.sb([128, 2, ntok], BF16)
    Vt = S.sb([128, NKB, 258], BF16)
    acst = S.sb([128, 772])
    BLm = S.sb([128, 128])
    BRm = S.sb([128, 128])
    lam4 = S.sb([128, 4])
    gnb = S.sb([128, 256])
    idb = S.sb([128, 128], BF16)
    onesf = S.sb([128, 128])
    qt = [S.sb([128, 2, 256], BF16) for _ in range(3)]
    PT = [S.sb([128, 2, 256], BF16) for _ in range(4)]
    sd = [S.sb([128, 2, 256]) for _ in range(2)]
    Osb = S.sb([128, 4, 257])
    sqc = [S.sb([128, 512], BF16) for _ in range(2)]
    onesb = S.sb([128, 128], BF16)
    lhl = S.sb([128, 4], BF16)
    vin = [S.sb([128, 2, 512], BF16) for _ in range(2)]
    mxs = S.sb([128, 2, 2 * (ntok // 512)])
    sm = S.sb([128, 16])
    res = S.sb([128, 256])
    junk = S.sb([128, 256])
    yb = S.sb([128, 256], BF16)
    oTs = [S.sb([128, 2, 256], BF16) for _ in range(2)]
    Sp = [S.ps() for _ in range(3)]
    Op = [S.ps() for _ in range(4)]
    Tp = S.ps([128, 1024], BF16)

    qv = projT[1280:1536, :].rearrange("(m d) t -> d m t", d=128)
    kv = projT[1536:1792, :].rearrange("(m d) t -> d m t", d=128)
    lamc = S.sb([128, 2])
    P.dma("sp", lamc[:], lamc_d, [], ["lamc"])
    P.dma("sp", acst[:], acst_d, [], ["acst"])
    P.dma("sp", lam4[:], lam_d, [], ["lam4"])
    P.dma("sp", gnb[:], gn_d, [], ["gnb"])
    P.dma("sp", idb[:], idb_d, [], ["idb"])
    P.dma("sp", kT[:], kv, [], ["kT"])
    P.add("dve", lambda e: e.memset(onesf[:], 1.0), [], ["onesf"])
    P.add("dve", lambda e: e.memset(onesb[:], 1.0), [], ["onesf"])
    P.add("dve", lambda e: e.memset(sm[:], 0.0), [], ["sm"])
    P.add("dve", lambda e: e.memset(sm[:, 15:16], EPS), ["sm"], ["sm"])
    P.add("dve", lambda e: e.memset(Vt[:, :, 256:258], 1.0), [], ["Vt1"])
    P.add("dve", lambda e: e.tensor_scalar(out=gnb[:], in0=gnb[:], scalar1=lamc[:, 0:1], scalar2=None, op0=ALU.mult), ["gnb", "lamc"], ["gnb"])
    if DBG_PREP[0] == 1:
        S.finish()
        return
    nch = ntok // 512
    k = 0
    for which, src in ((0, qv), (1, kv)):
        for m in range(2):
            for ch in range(nch):
                b = k % 2
                k += 1
                if which == 0:
                    P.dma("sp", vin[b][:, 0, :], src[:, m, ch * 512:(ch + 1) * 512], [], [("vin", b)])
                    P.add("act", lambda e, b=b: e.activation(out=sqc[b][:], in_=vin[b][:, 0, :], func=AF.Square), [("vin", b)], [("sqc", b)])
                else:
                    P.add("act", lambda e, b=b, m=m, ch=ch: e.activation(out=sqc[b][:], in_=kT[:, m, ch * 512:(ch + 1) * 512], func=AF.Square),
                          ["kT"], [("sqc", b)])
                P.add("pe", lambda e, b=b: e.matmul(Sp[b][:, 0:512], lhsT=onesb[:], rhs=sqc[b][:], start=True, stop=True),
                      [("sqc", b), "onesf"], [("S", b)])
                P.add("dve", lambda e, b=b, which=which, m=m, ch=ch: e.reduce_max(out=mxs[:, which, m * nch + ch:m * nch + ch + 1],
                                                                                 in_=Sp[b][:, 0:512], axis=AX.X),
                      [("S", b)], ["mxs"])
    P.add("dve", lambda e: e.reduce_max(out=sm[:, 0:2], in_=mxs[:], axis=AX.X), ["mxs"], ["sm"])
    P.add("dve", lambda e: e.tensor_tensor(out=sm[:, 2:3], in0=sm[:, 0:1], in1=sm[:, 1:2], op=ALU.mult), ["sm"], ["sm"])
    P.add("act", lambda e: e.activation(out=sm[:, 3:4], in_=sm[:, 2:3], func=AF.Sqrt, scale=1.1), ["sm"], ["sm"])
    P.add("dve", lambda e: e.tensor_scalar(out=sm[:, 4:5], in0=sm[:, 3:4], scalar1=-1.0, scalar2=None, op0=ALU.mult), ["sm"], ["sm"])
    P.add("dve", lambda e: e.tensor_scalar(out=BLm[:], in0=acst[:, 0:128], scalar1=sm[:, 3:4], scalar2=None, op0=ALU.subtract),
          ["acst", "sm"], ["BLm"])
    P.add("dve", lambda e: e.tensor_scalar(out=BRm[:], in0=acst[:, 128:256], scalar1=sm[:, 3:4], scalar2=None, op0=ALU.subtract),
          ["acst", "sm"], ["BRm"])
    P.add("dve", lambda e: e.tensor_tensor(out=sm[:, 5:6], in0=lam4[:, 0:1], in1=lam4[:, 1:2], op=ALU.mult), ["lam4", "sm"], ["sm"])
    P.add("dve", lambda e: e.tensor_tensor(out=sm[:, 6:7], in0=lam4[:, 2:3], in1=lam4[:, 3:4], op=ALU.mult), ["lam4", "sm"], ["sm"])
    P.add("dve", lambda e: e.tensor_copy(out=lhl[:, 0:2], in_=sm[:, 5:7]), ["sm"], ["lhl"])
    P.add("dve", lambda e: e.tensor_tensor(out=sm[:, 9:11], in0=sm[:, 5:7], in1=lhl[:, 0:2], op=ALU.subtract), ["sm", "lhl"], ["sm"])
    P.add("dve", lambda e: e.tensor_copy(out=lhl[:, 2:4], in_=sm[:, 9:11]), ["sm", "lhl"], ["lhl"])
    P.add("pe", lambda e: e.matmul(Sp[2][:, 0:4], lhsT=onesb[:], rhs=lhl[:], start=True, stop=True), ["lhl", "onesf"], [("S", 2)])
    P.add("dve", lambda e: e.tensor_copy(out=sm[:, 9:11], in_=Sp[2][:, 2:4]), [("S", 2), "sm"], ["sm"])
    P.add("dve", lambda e: e.tensor_tensor(out=sm[:, 5:7], in0=Sp[2][:, 0:2], in1=sm[:, 9:11], op=ALU.add), [("S", 2), "sm"], ["sm"])
    P.add("act", lambda e: e.activation(out=sm[:, 9:11], in_=sm[:, 5:7], func=AF.Exp), ["sm"], ["sm"])
    P.add("dve", lambda e: e.tensor_tensor(out=sm[:, 8:9], in0=sm[:, 10:11], in1=sm[:, 9:10], op=ALU.subtract), ["sm"], ["sm"])
    P.add("dve", lambda e: e.tensor_scalar(out=sm[:, 8:9], in0=sm[:, 8:9], scalar1=lamc[:, 1:2], scalar2=None, op0=ALU.add), ["sm", "lamc"], ["sm"])
    if DBG_PREP[0] == 2:
        S.finish()
        return
    vtv = vtok.rearrange("(n p) d -> p n d", p=128)
    VG = max(1, NKB // 4)
    for g4 in range(NKB // VG):
        P.dma("sp", Vt[:, g4 * VG:(g4 + 1) * VG, 0:256], vtv[:, g4 * VG:(g4 + 1) * VG, :], [], [("Vt", g4)])
    ks = 0
    kp = 0
    for qg in (range(NQG) if dbg is None else dbg):
        qb_ = qg % 3
        P.dma("sp", qt[qb_][:], qv[:, :, qg * 256:(qg + 1) * 256], [], [("qt", qb_)])
        sides = []
        if qg > 0:
            sides.append(("L", list(range(0, 2 * qg))))
        sides.append(("C", [2 * qg, 2 * qg + 1]))
        if 2 * qg + 2 < NKB:
            sides.append(("R", list(range(2 * qg + 2, NKB))))
        for si, (side, kbs) in enumerate(sides):
            for ki, kb in enumerate(kbs):
                r = ks % 3
                ks += 1
                for m in range(2):
                    P.add("pe", lambda e, r=r, m=m, kb=kb, qb_=qb_: e.matmul(Sp[r][:, m * 256:(m + 1) * 256], lhsT=kT[:, m, kb * 128:(kb + 1) * 128],
                                                                           rhs=qt[qb_][:, m, :], start=True, stop=True),
                          ["kT", ("qt", qb_)], [("S", r)])
                r2 = kp % 4
                kp += 1
                if side == "L":
                    n = 2 * qg - kb
                    P.add("act", lambda e, r=r, r2=r2, n=n: e.activation(out=PT[r2][:].rearrange("p m q -> p (m q)"), in_=Sp[r][:, 0:512], func=AF.Exp,
                                                                         bias=BLm[:, n:n + 1], scale=1.0),
                          [("S", r), "BLm"], [("PT", r2)])
                elif side == "R":
                    n = kb - 2 * qg - 2
                    P.add("act", lambda e, r=r, r2=r2, n=n: e.activation(out=PT[r2][:].rearrange("p m q -> p (m q)"), in_=Sp[r][:, 0:512], func=AF.Exp,
                                                                         bias=BRm[:, n:n + 1], scale=1.0),
                          [("S", r), "BRm"], [("PT", r2)])
                else:
                    a = kb - 2 * qg
                    d_ = ki % 2
                    for m in range(2):
                        P.add("dve", lambda e, r=r, m=m, a=a, d_=d_: e.tensor_tensor(out=sd[d_][:, m, :], in0=Sp[r][:, m * 256:(m + 1) * 256],
                                                                                     in1=acst[:, 256 + a * 256:512 + a * 256], op=ALU.add),
                              [("S", r), "acst"], [("sd", d_)])
                    P.add("act", lambda e, r2=r2, d_=d_: e.activation(out=PT[r2][:].rearrange("p m q -> p (m q)"), in_=sd[d_][:].rearrange("p m q -> p (m q)"),
                                                                      func=AF.Exp, bias=sm[:, 4:5], scale=1.0),
                          [("sd", d_), "sm"], [("PT", r2)])
                for qb in range(2):
                    for m in range(2):
                        i = qb * 2 + m
                        P.add("pe", lambda e, i=i, r2=r2, m=m, qb=qb, kb=kb, ki=ki, nk=len(kbs): e.matmul(
                            Op[i][:, 0:257], lhsT=PT[r2][:, m, qb * 128:(qb + 1) * 128], rhs=Vt[:, kb, 0:257],
                            start=(ki == 0), stop=(ki == nk - 1)),
                            [("PT", r2), ("Vt", kb // VG), "Vt1"], [("O", i)])
            for i in range(4):
                qb = i // 2
                if si == 0:
                    if side == "L":
                        P.add("dve", lambda e, i=i, qb=qb: e.tensor_scalar(out=Osb[:, i, :], in0=Op[i][:, 0:257], scalar1=acst[:, 768 + qb:769 + qb],
                                                                           scalar2=None, op0=ALU.mult), [("O", i), "acst"], ["Osb"])
                    else:
                        P.add("dve", lambda e, i=i: e.tensor_copy(out=Osb[:, i, :], in_=Op[i][:, 0:257]), [("O", i)], ["Osb"])
                elif side == "C":
                    P.add("dve", lambda e, i=i: e.tensor_tensor(out=Osb[:, i, :], in0=Osb[:, i, :], in1=Op[i][:, 0:257], op=ALU.add),
                          [("O", i), "Osb"], ["Osb"])
                else:
                    P.add("dve", lambda e, i=i, qb=qb: e.scalar_tensor_tensor(out=Osb[:, i, :], in0=Op[i][:, 0:257], scalar=acst[:, 770 + qb:771 + qb],
                                                                              in1=Osb[:, i, :], op0=ALU.mult, op1=ALU.add),
                          [("O", i), "Osb", "acst"], ["Osb"])
        ob = qg % 2
        for qb in range(2):
            P.add("dve", lambda e, qb=qb: e.reciprocal(out=sm[:, 11:12], in_=Osb[:, 2 * qb, 256:257]), ["Osb", "sm"], ["sm"])
            P.add("dve", lambda e, qb=qb: e.reciprocal(out=sm[:, 12:13], in_=Osb[:, 2 * qb + 1, 256:257]), ["Osb", "sm"], ["sm"])
            P.add("dve", lambda e: e.tensor_tensor(out=sm[:, 12:13], in0=sm[:, 12:13], in1=sm[:, 8:9], op=ALU.mult), ["sm"], ["sm"])
            P.add("dve", lambda e, qb=qb: e.tensor_scalar(out=res[:], in0=Osb[:, 2 * qb, 0:256], scalar1=sm[:, 11:12], scalar2=None, op0=ALU.mult),
                  ["Osb", "sm"], ["res"])
            P.add("dve", lambda e, qb=qb: e.scalar_tensor_tensor(out=res[:], in0=Osb[:, 2 * qb + 1, 0:256], scalar=sm[:, 12:13], in1=res[:],
                                                                 op0=ALU.mult, op1=ALU.add), ["Osb", "sm", "res"], ["res"])
            P.add("dve", lambda e: e.memset(sm[:, 13:14], 0.0), ["sm"], ["sm"])
            P.add("act", lambda e: e.activation(out=junk[:], in_=res[:], func=AF.Square, accum_out=sm[:, 13:14]), ["res", "sm"], ["junk", "sm"])
            P.add("act", lambda e: e.activation(out=sm[:, 14:15], in_=sm[:, 13:14], func=AF.Sqrt, bias=sm[:, 15:16], scale=1.0 / 256.0),
                  ["sm"], ["sm"])
            P.add("dve", lambda e: e.reciprocal(out=sm[:, 14:15], in_=sm[:, 14:15]), ["sm"], ["sm"])
            P.add("dve", lambda e: e.scalar_tensor_tensor(out=yb[:], in0=res[:], scalar=sm[:, 14:15], in1=gnb[:], op0=ALU.mult, op1=ALU.mult),
                  ["res", "sm", "gnb"], ["yb"])
            for dc in range(2):
                P.add("pe", lambda e, dc=dc: e.transpose(out=Tp[:, 512 + dc * 128:512 + (dc + 1) * 128], in_=yb[:, dc * 128:(dc + 1) * 128], identity=idb[:]),
                      ["yb", "idb"], [("Tp", 2)])
            P.add("act", lambda e, ob=ob, qb=qb: e.copy(out=oTs[ob][:, :, qb * 128:(qb + 1) * 128],
                                                        in_=Tp[:, 512:768].rearrange("p (c q) -> p c q", c=2)), [("Tp", 2)], [("oTs", ob)])
        P.dma("pool", oT_rows.rearrange("(c p) t -> p c t", p=128)[:, :, qg * 256:(qg + 1) * 256], oTs[ob][:], [("oTs", ob)], [("oT", qg)])
    S.finish()


def hgrn_consts(SEG=512):
    cm = np.ones((128, SEG), np.float32)
    cm[:, 0::HCH] = 0.0
    s = np.arange(HCH)[:, None]
    t = np.arange(HCH)[None, :]
    mk = np.zeros((128, 2 * HCH), np.float32)
    mk[0:HCH, 0:HCH] = (s <= t)
    mk[0:HCH, HCH:2 * HCH] = (s >= t)
    return np.concatenate([cm, mk], axis=1)


def stage_hgrn(nc, projT, lbo_d, l, hcst_d, gnh_d, idb_d, o_scr, oT_rows, ntok=T, SEG=512):
    S = Stage(nc, "hg")
    P = S.P
    nseg = ntok // SEG
    nch = SEG // HCH
    hcst = S.sb([128, SEG + 2 * HCH])
    idb = S.sb([128, 128], BF16)
    lbt = S.sb([128, 2, 2, DEPTH, 16])
    noml = S.sb([128, 2, DEPTH, 16])
    gnh = S.sb([128, 1])
    onesn = S.sb([128, 128])
    epst = S.sb([128, 1])
    S32 = [S.sb([128, 128]) for _ in range(2)]
    Sbf = [S.sb([128, 128], BF16) for _ in range(2)]
    qin = [S.sb([128, 2, SEG], BF16) for _ in range(2)]
    zin = [S.sb([128, 2, SEG], BF16) for _ in range(2)]
    vin = [S.sb([128, 2, SEG], BF16) for _ in range(2)]
    gin = [S.sb([128, 2, SEG], BF16) for _ in range(2)]
    ofw = [S.sb([128, 2, SEG]) for _ in range(2)]
    sig = [S.sb([128, SEG]) for _ in range(2)]
    g = [S.sb([128, SEG]) for _ in range(2)]
    kc = [S.sb([128, SEG]) for _ in range(2)]
    bb = [S.sb([128, SEG]) for _ in range(2)]
    arg = [S.sb([128, SEG]) for _ in range(2)]
    cq = [S.sb([128, SEG]) for _ in range(2)]
    ck = [S.sb([128, SEG]) for _ in range(2)]
    ex = [S.sb([128, SEG]) for _ in range(2)]
    Qt = [S.sb([128, SEG], BF16) for _ in range(2)]
    Kt = [S.sb([128, SEG], BF16) for _ in range(2)]
    Kh = [S.sb([128, SEG], BF16) for _ in range(2)]
    ebl = [S.sb([128, nch]) for _ in range(2)]
    Am = [S.sb([HCH, HCH], BF16) for _ in range(4)]
    VK = [S.sb([HCH, 256], BF16) for _ in range(4)]
    osb = [S.sb([128, SEG]) for _ in range(2)]
    ysb = [S.sb([128, SEG]) for _ in range(2)]
    yb = [S.sb([128, SEG], BF16) for _ in range(2)]
    Ap = S.ps()
    Dp = S.ps()
    Tp = S.ps([128, 1024], BF16)
    Np = S.ps()
    Oa = [[S.ps() for _ in range(2)] for _ in range(2)]

    P.dma("sp", hcst[:], hcst_d, [], ["hcst"])
    P.dma("sp", idb[:], idb_d, [], ["idb"])
    P.dma("sp", lbt[:], lbo_d, [], ["lbt"])
    P.dma("sp", gnh[:], gnh_d, [], ["gnh"])
    P.add("dve", lambda e: e.tensor_scalar(out=noml[:], in0=lbt[:, 1], scalar1=-1.0, scalar2=None, op0=ALU.mult), ["lbt"], ["noml"])
    P.add("dve", lambda e: e.memset(onesn[:], 1.0 / 128.0), [], ["onesn"])
    P.add("dve", lambda e: e.memset(epst[:], EPS), [], ["eps"])
    cm = hcst[:, 0:SEG]
    pq = projT[0:256, :].rearrange("(a k) t -> k a t", k=128)
    pv = projT[768:1024, :].rearrange("(a k) t -> k a t", k=128)
    pg = projT[1024:1280, :].rearrange("(a k) t -> k a t", k=128)
    osv = o_scr.rearrange("(a k) t -> k a t", k=128)
    oTv = oT_rows.rearrange("(a k) t -> k a t", k=128)
    kk = 0
    for dirn in range(2):
        pz = projT[256 + 256 * dirn:512 + 256 * dirn, :].rearrange("(a k) t -> k a t", k=128)
        mask = hcst[0:HCH, SEG + dirn * HCH:SEG + (dirn + 1) * HCH]
        for a in range(2):
            P.add("dve", lambda e, a=a: e.memset(S32[a][:], 0.0), [("S32", a)], [("S32", a)])
            P.add("dve", lambda e, a=a: e.memset(Sbf[a][:], 0.0), [("Sbf", a)], [("Sbf", a)])
        segs = list(range(nseg)) if dirn == 0 else list(range(nseg - 1, -1, -1))
        for si, seg in enumerate(segs):
            sb_ = si % 2
            ts = slice(seg * SEG, (seg + 1) * SEG)
            P.dma("sp", qin[sb_][:], pq[:, :, ts], [], [("qin", sb_)])
            P.dma("sp", zin[sb_][:], pz[:, :, ts], [], [("zin", sb_)])
            P.dma("sp", vin[sb_][:], pv[:, :, ts], [], [("vin", sb_)])
            if dirn == 1:
                P.dma("sp", gin[sb_][:], pg[:, :, ts], [], [("gin", sb_)])
                P.dma("sp", ofw[sb_][:], osv[:, :, ts], [("oscr", seg)], [("ofw", sb_)])
            for a in range(2):
                lb_ap = lbt[:, 0, dirn, l, a:a + 1]
                oml_ap = lbt[:, 1, dirn, l, a:a + 1]
                noml_ap = noml[:, dirn, l, a:a + 1]
                P.add("act", lambda e, a=a, sb_=sb_: e.activation(out=sig[a][:], in_=zin[sb_][:, a, :], func=AF.Sigmoid), [("zin", sb_)], [("sig", a)])
                P.add("act", lambda e, a=a, lb_ap=lb_ap, oml_ap=oml_ap: e.activation(out=g[a][:], in_=sig[a][:], func=AF.Ln, bias=lb_ap, scale=oml_ap),
                      [("sig", a), "lbt"], [("g", a)])
                P.add("dve", lambda e, a=a, noml_ap=noml_ap, oml_ap=oml_ap: e.tensor_scalar(out=kc[a][:], in0=sig[a][:], scalar1=noml_ap, scalar2=oml_ap,
                                                                                          op0=ALU.mult, op1=ALU.add), [("sig", a), "noml", "lbt"], [("kc", a)])
                P.add("dve", lambda e, a=a: e.tensor_tensor_scan(out=bb[a][:], data0=cm, data1=g[a][:], initial=0.0, op0=ALU.mult, op1=ALU.add),
                      [("g", a), "hcst"], [("bb", a)])
                b3 = bb[a][:].rearrange("p (n c) -> p n c", c=HCH)
                P.add("dve", lambda e, a=a, b3=b3: e.tensor_tensor(out=arg[a][:].rearrange("p (n c) -> p n c", c=HCH),
                                                                   in0=b3[:, :, HCH - 1:HCH].to_broadcast([128, nch, HCH]), in1=b3, op=ALU.subtract),
                      [("bb", a)], [("arg", a)])
                P.add("act", lambda e, a=a, b3=b3: e.activation(out=ebl[a][:], in_=b3[:, :, HCH - 1], func=AF.Exp), [("bb", a)], [("ebl", a)])
                if dirn == 0:
                    cq_ap, ck_ap = bb[a], arg[a]
                    kq, kk_ = ("bb", a), ("arg", a)
                else:
                    P.add("dve", lambda e, a=a: e.tensor_tensor(out=cq[a][:], in0=arg[a][:], in1=g[a][:], op=ALU.add), [("arg", a), ("g", a)], [("cq", a)])
                    P.add("dve", lambda e, a=a: e.tensor_tensor(out=ck[a][:], in0=bb[a][:], in1=g[a][:], op=ALU.subtract), [("bb", a), ("g", a)], [("ck", a)])
                    cq_ap, ck_ap = cq[a], ck[a]
                    kq, kk_ = ("cq", a), ("ck", a)
                P.add("act", lambda e, a=a, cq_ap=cq_ap: e.activation(out=ex[0][:], in_=cq_ap[:], func=AF.Exp), [kq], [("ex", 0)])
                P.add("dve", lambda e, a=a, sb_=sb_: e.tensor_tensor(out=Qt[a][:], in0=qin[sb_][:, a, :], in1=ex[0][:], op=ALU.mult),
                      [("ex", 0), ("qin", sb_)], [("Qt", a)])
                P.add("act", lambda e, a=a, cq_ap=cq_ap: e.activation(out=ex[1][:], in_=cq_ap[:], func=AF.Exp, scale=-1.0), [kq], [("ex", 1)])
                P.add("dve", lambda e, a=a: e.tensor_tensor(out=Kt[a][:], in0=kc[a][:], in1=ex[1][:], op=ALU.mult), [("ex", 1), ("kc", a)], [("Kt", a)])
                P.add("act", lambda e, a=a, ck_ap=ck_ap: e.activation(out=ex[0][:], in_=ck_ap[:], func=AF.Exp), [kk_, ("ex", 0)], [("ex", 0)])
                P.add("dve", lambda e, a=a: e.tensor_tensor(out=Kh[a][:], in0=kc[a][:], in1=ex[0][:], op=ALU.mult), [("ex", 0), ("kc", a)], [("Kh", a)])
            chunks = list(range(nch)) if dirn == 0 else list(range(nch - 1, -1, -1))
            for ci, n in enumerate(chunks):
                cs = slice(n * HCH, (n + 1) * HCH)
                for a in range(2):
                    r = kk % 4
                    kk += 1
                    P.add("pe", lambda e, a=a, r=r, cs=cs: e.matmul(Ap[0:HCH, r * HCH:(r + 1) * HCH], lhsT=Kt[a][:, cs], rhs=Qt[a][:, cs], start=True, stop=True),
                          [("Kt", a), ("Qt", a)], [("Ap", r)])
                    P.add("dve", lambda e, r=r, mask=mask: e.tensor_tensor(out=Am[r][:], in0=Ap[0:HCH, r * HCH:(r + 1) * HCH], in1=mask, op=ALU.mult),
                          [("Ap", r), "hcst"], [("Am", r)])
                    P.add("pe", lambda e, a=a, r=r, cs=cs, sb_=sb_: e.transpose(out=Tp[0:HCH, r * 256:r * 256 + 128], in_=vin[sb_][:, a, cs], identity=idb[:]),
                          [("vin", sb_), "idb"], [("Tp", r)])
                    P.add("pe", lambda e, a=a, r=r, cs=cs: e.transpose(out=Tp[0:HCH, r * 256 + 128:r * 256 + 256], in_=Kh[a][:, cs], identity=idb[:]),
                          [("Kh", a), "idb"], [("Tp", r)])
                    P.add("act", lambda e, r=r: e.copy(out=VK[r][:], in_=Tp[0:HCH, r * 256:(r + 1) * 256]), [("Tp", r)], [("VK", r)])
                    P.add("pe", lambda e, a=a, cs=cs, sb_=sb_: e.matmul(Oa[a][sb_][:, cs], lhsT=Sbf[a][:], rhs=Qt[a][:, cs], start=True, stop=False),
                          [("Sbf", a), ("Qt", a)], [("Oa", a, sb_)])
                    P.add("pe", lambda e, a=a, r=r, cs=cs, sb_=sb_: e.matmul(Oa[a][sb_][:, cs], lhsT=VK[r][:, 0:128], rhs=Am[r][:], start=False, stop=True),
                          [("VK", r), ("Am", r)], [("Oa", a, sb_)])
                    P.add("pe", lambda e, r=r: e.matmul(Dp[:, r * 128:(r + 1) * 128], lhsT=VK[r][:, 128:256], rhs=VK[r][:, 0:128], start=True, stop=True),
                          [("VK", r)], [("Dp", r)])
                    P.add("dve", lambda e, a=a, r=r, n=n: e.scalar_tensor_tensor(out=S32[a][:], in0=S32[a][:], scalar=ebl[a][:, n:n + 1],
                                                                                in1=Dp[:, r * 128:(r + 1) * 128], op0=ALU.mult, op1=ALU.add),
                          [("S32", a), ("ebl", a), ("Dp", r)], [("S32", a)])
                    P.add("act", lambda e, a=a: e.copy(out=Sbf[a][:], in_=S32[a][:]), [("S32", a)], [("Sbf", a)])
            for a in range(2):
                if dirn == 0:
                    P.add("dve", lambda e, a=a, sb_=sb_: e.tensor_copy(out=osb[a][:], in_=Oa[a][sb_][:, 0:SEG]), [("Oa", a, sb_)], [("osb", a)])
                    P.dma("pool", osv[:, a, ts], osb[a][:], [("osb", a)], [("oscr", seg)])
                else:
                    P.add("dve", lambda e, a=a, sb_=sb_: e.tensor_tensor(out=osb[a][:], in0=Oa[a][sb_][:, 0:SEG], in1=ofw[sb_][:, a, :], op=ALU.add),
                          [("Oa", a, sb_), ("ofw", sb_)], [("osb", a)])
                    P.add("act", lambda e, a=a: e.activation(out=ysb[a][:], in_=osb[a][:], func=AF.Square), [("osb", a)], [("ysb", a)])
                    P.add("pe", lambda e, a=a: e.matmul(Np[:, 0:SEG], lhsT=onesn[:], rhs=ysb[a][:], start=True, stop=True), [("ysb", a), "onesn"], ["Np"])
                    P.add("act", lambda e, a=a: e.activation(out=ysb[a][:], in_=Np[:, 0:SEG], func=AF.Sqrt, bias=epst[:, 0:1], scale=1.0),
                          ["Np", "eps"], [("ysb", a)])
                    P.add("dve", lambda e, a=a: e.reciprocal(out=ysb[a][:], in_=ysb[a][:]), [("ysb", a)], [("ysb", a)])
                    P.add("dve", lambda e, a=a: e.tensor_tensor(out=osb[a][:], in0=osb[a][:], in1=ysb[a][:], op=ALU.mult), [("osb", a), ("ysb", a)], [("osb", a)])
                    P.add("act", lambda e, a=a, sb_=sb_: e.activation(out=ysb[a][:], in_=gin[sb_][:, a, :], func=AF.Silu), [("gin", sb_), ("ysb", a)], [("ysb", a)])
                    P.add("dve", lambda e, a=a: e.scalar_tensor_tensor(out=yb[a][:], in0=osb[a][:], scalar=gnh[:, 0:1], in1=ysb[a][:], op0=ALU.mult, op1=ALU.mult),
                          [("osb", a), ("ysb", a), "gnh"], [("yb", a)])
                    P.dma("pool", oTv[:, a, ts], yb[a][:], [("yb", a)], [("oT", seg, a)])
    S.finish()


def stage_outproj(nc, oT_tok, w_out, xT_in, xT_out, par, ig, ntok=TL, TT=512):
    S = Stage(nc, "op")
    P = S.P
    G = S.sb([128, 32])
    oTr = S.sb([128, KC, ntok], BF16)
    Wg = [S.sb([128, KC, 128], BF16) for _ in range(2)]
    xg = [S.sb([128, ntok]) for _ in range(2)]
    ps = [S.ps() for _ in range(4)]
    P.dma("sp", G[:], par[:, ig, :], [], ["G"])
    ov = oT_tok.rearrange("(c p) t -> p c t", p=128)
    for q4 in range(4):
        P.dma("sp", oTr[:, q4 * 8:(q4 + 1) * 8, :], ov[:, q4 * 8:(q4 + 1) * 8, :], [], [("oTr", q4)])
    wv = w_out.rearrange("(c p) n -> p c n", p=128)
    k = 0
    for cg in range(D // 128):
        b = cg % 2
        P.dma("pool", Wg[b][:], wv[:, :, cg * 128:(cg + 1) * 128], [], [("Wg", b)])
        P.dma("sp", xg[b][:], xT_in[cg * 128:(cg + 1) * 128, :], [], [("xg", b)])
        for tt in range(ntok // TT):
            r = k % 4
            k += 1
            for c in range(KC):
                P.add("pe", lambda e, b=b, c=c, r=r, tt=tt: e.matmul(ps[r][:, 0:TT], lhsT=Wg[b][:, c, :], rhs=oTr[:, c, tt * TT:(tt + 1) * TT],
                                                                     start=(c == 0), stop=(c == KC - 1)),
                      [("Wg", b), ("oTr", c // 8)], [("ps", r)])
            P.add("dve", lambda e, b=b, r=r, tt=tt, cg=cg: e.scalar_tensor_tensor(out=xg[b][:, tt * TT:(tt + 1) * TT], in0=ps[r][:, 0:TT], scalar=G[:, cg:cg + 1],
                                                                                  in1=xg[b][:, tt * TT:(tt + 1) * TT], op0=ALU.mult, op1=ALU.add),
                  [("ps", r), "G", ("xg", b)], [("xg", b)])
        P.dma("sp", xT_out[cg * 128:(cg + 1) * 128, :], xg[b][:], [("xg", b)], [("xTo", cg)])
    S.finish()


def stage_ffn_up(nc, h2T_halo, w_up, cwp_d, mT, ntok=TL, TT=512):
    S = Stage(nc, "up")
    P = S.P
    NH = ntok + 2
    h2 = S.sb([128, KC, NH], BF16)
    Wg = [S.sb([128, KC, 256], BF16) for _ in range(2)]
    u = S.sb([128, 2, NH])
    cwp = S.sb([128, 4, 128])
    HT = 1024
    tA = [S.sb([128, HT]) for _ in range(2)]
    sg = S.sb([128, HT])
    mo = [S.sb([128, ntok], BF16) for _ in range(2)]
    ps = [S.ps() for _ in range(4)]
    P.dma("sp", cwp[:], cwp_d, [], ["cwp"])
    hv = h2T_halo.rearrange("(c p) t -> p c t", p=128)
    for q4 in range(4):
        P.dma("sp", h2[:, q4 * 8:(q4 + 1) * 8, :], hv[:, q4 * 8:(q4 + 1) * 8, :], [], [("h2", q4)])
    wv = w_up.rearrange("(c p) n -> p c n", p=128)
    mv = mT.rearrange("(g p) t -> p g t", p=128)
    k = 0
    for jg in range(DFF // 128):
        b = jg % 2
        P.dma("pool", Wg[b][:, :, 0:128], wv[:, :, jg * 128:(jg + 1) * 128], [], [("Wg", b, 0)])
        P.dma("pool", Wg[b][:, :, 128:256], wv[:, :, DFF + jg * 128:DFF + (jg + 1) * 128], [], [("Wg", b, 1)])
        for gv in range(2):
            for ct in range(ntok // TT + 1):
                r = k % 4
                k += 1
                if ct < ntok // TT:
                    n = TT
                    rsl = slice(1 + ct * TT, 1 + (ct + 1) * TT)
                else:
                    n = 2
                    rsl = slice(0, NH, NH - 1)
                for c in range(KC):
                    P.add("pe", lambda e, b=b, c=c, r=r, gv=gv, n=n, rsl=rsl: e.matmul(ps[r][:, 0:n], lhsT=Wg[b][:, c, gv * 128:(gv + 1) * 128],
                                                                                       rhs=h2[:, c, rsl], start=(c == 0), stop=(c == KC - 1)),
                          [("Wg", b, gv), ("h2", c // 8)], [("ps", r)])
                if k % 2:
                    P.add("act", lambda e, r=r, gv=gv, n=n, rsl=rsl: e.copy(out=u[:, gv, rsl], in_=ps[r][:, 0:n]), [("ps", r)], [("u", gv)])
                else:
                    P.add("dve", lambda e, r=r, gv=gv, n=n, rsl=rsl: e.tensor_copy(out=u[:, gv, rsl], in_=ps[r][:, 0:n]), [("ps", r)], [("u", gv)])
        for th in range(ntok // HT):
            o = th * HT
            for gv in range(2):
                gi = gv * 64 + jg
                P.add("act", lambda e, gv=gv, gi=gi, o=o: e.activation(out=tA[gv][:], in_=u[:, gv, 1 + o:1 + o + HT], func=AF.Identity,
                                                                       bias=cwp[:, 3, gi:gi + 1], scale=cwp[:, 1, gi:gi + 1]),
                      [("u", gv), "cwp"], [("tA", gv)])
                P.add("dve", lambda e, gv=gv, gi=gi, o=o: e.scalar_tensor_tensor(out=tA[gv][:], in0=u[:, gv, o:o + HT], scalar=cwp[:, 0, gi:gi + 1],
                                                                                 in1=tA[gv][:], op0=ALU.mult, op1=ALU.add),
                      [("u", gv), "cwp", ("tA", gv)], [("tA", gv)])
                P.add("dve", lambda e, gv=gv, gi=gi, o=o: e.scalar_tensor_tensor(out=tA[gv][:], in0=u[:, gv, 2 + o:2 + o + HT], scalar=cwp[:, 2, gi:gi + 1],
                                                                                  in1=tA[gv][:], op0=ALU.mult, op1=ALU.add),
                      [("u", gv), "cwp", ("tA", gv)], [("tA", gv)])
            P.add("act", lambda e: e.activation(out=sg[:], in_=tA[0][:], func=AF.Silu), [("tA", 0)], ["sg"])
            P.add("dve", lambda e, b=b, o=o: e.tensor_tensor(out=mo[b][:, o:o + HT], in0=sg[:], in1=tA[1][:], op=ALU.mult),
                  ["sg", ("tA", 1)], [("mo", b)])
        P.dma("sp", mv[:, jg, :], mo[b][:], [("mo", b)], [("mT", jg)])
    S.finish()


def stage_ffn_down(nc, mT, w_down, xT_in, xT_out, par, ig, ntok=TL, TT=512):
    S = Stage(nc, "dn")
    P = S.P
    NK = DFF // 128
    G = S.sb([128, 32])
    mt = [S.sb([128, NK, TT], BF16) for _ in range(2)]
    Wg = [S.sb([128, NK, 128], BF16) for _ in range(2)]
    xg = [S.sb([128, TT]) for _ in range(3)]
    ps = [S.ps() for _ in range(4)]
    P.dma("sp", G[:], par[:, ig, :], [], ["G"])
    mv = mT.rearrange("(c p) t -> p c t", p=128)
    wv = w_down.rearrange("(c p) n -> p c n", p=128)
    k = 0
    for tt in range(ntok // TT):
        tb = tt % 2
        ts = slice(tt * TT, (tt + 1) * TT)
        for q4 in range(4):
            P.dma("sp", mt[tb][:, q4 * 16:(q4 + 1) * 16, :], mv[:, q4 * 16:(q4 + 1) * 16, ts], [("mT", g) for g in range(q4 * 16, (q4 + 1) * 16)],
                  [("mt", tb, q4)])
        for cg in range(D // 128):
            b = k % 2
            r = k % 4
            x3 = k % 3
            k += 1
            P.dma("pool", Wg[b][:], wv[:, :, cg * 128:(cg + 1) * 128], [], [("Wg", b)])
            P.dma("sp", xg[x3][:], xT_in[cg * 128:(cg + 1) * 128, ts], [], [("xg", x3)])
            for c in range(NK):
                P.add("pe", lambda e, b=b, c=c, r=r, tb=tb: e.matmul(ps[r][:, 0:TT], lhsT=Wg[b][:, c, :], rhs=mt[tb][:, c, :],
                                                                     start=(c == 0), stop=(c == NK - 1)),
                      [("Wg", b), ("mt", tb, c // 16)], [("ps", r)])
            P.add("dve", lambda e, r=r, x3=x3, cg=cg: e.scalar_tensor_tensor(out=xg[x3][:], in0=ps[r][:, 0:TT], scalar=G[:, cg:cg + 1],
                                                                             in1=xg[x3][:], op0=ALU.mult, op1=ALU.add),
                  [("ps", r), "G", ("xg", x3)], [("xg", x3)])
            P.dma("sp", xT_out[cg * 128:(cg + 1) * 128, ts], xg[x3][:], [("xg", x3)], [("xTo", cg, tt)])
    S.finish()


_PROGS = {}


def _dt(nc, n, s, d=F32, k="ExternalInput"):
    return nc.dram_tensor(n, list(s), d, kind=k).ap()


def prog_L0():
    if "L0" in _PROGS:
        return _PROGS["L0"]
    nc = bass.Bass("TRN2", target_bir_lowering=False)
    x = _dt(nc, "x", [TL, D]); vecs = _dt(nc, "vecs", [1280, 128]); w_ada = _dt(nc, "w_ada", [D, 6 * D])
    lbl = _dt(nc, "lbl", [128, 128]); idf = _dt(nc, "idf", [128, 128])
    par = _dt(nc, "par", [128, DEPTH * 6 + 1, 32], F32, "ExternalOutput")
    lbo = _dt(nc, "lbo", [128, 2, 2, DEPTH, 16], F32, "ExternalOutput")
    xT = _dt(nc, "xT", [D, TL], F32, "ExternalOutput")
    hT = _dt(nc, "hT", [D, TL], BF16, "ExternalOutput")
    stage_params(nc, vecs, w_ada, lbl, idf, par, lbo)
    stage_pre(nc, x, xT, idf)
    stage_norm(nc, xT, par, 0, 1, hT)
    _PROGS["L0"] = nc
    return nc


def prog_LB():
    if "LB" in _PROGS:
        return _PROGS["LB"]
    nc = bass.Bass("TRN2", target_bir_lowering=False)
    hT = _dt(nc, "hT", [D, T], BF16); w = _dt(nc, "w", [D, 2048]); acst = _dt(nc, "acst", [128, 772]); lam = _dt(nc, "lam", [128, 4])
    lamc = _dt(nc, "lamc", [128, 2]); gn = _dt(nc, "gn", [128, 256]); idb = _dt(nc, "idb", [128, 128], BF16)
    lbo = _dt(nc, "lbo", [128, 2, 2, DEPTH, 16]); hc = _dt(nc, "hc", [128, 512 + 64]); gnh = _dt(nc, "gnh", [128, 1])
    projT = _dt(nc, "projT", [2048, T], BF16, "Internal")
    oscr = _dt(nc, "oscr", [256, T], F32, "Internal")
    vtok = _dt(nc, "vtok", [T, 256], BF16, "Internal")
    oT = _dt(nc, "oT", [512, T], BF16, "ExternalOutput")
    stage_inproj(nc, hT, w, projT, vtok)
    stage_attn(nc, projT, vtok, acst, lam, gn, idb, oT[256:512, :], lamc)
    stage_hgrn(nc, projT, lbo, 0, hc, gnh, idb, oscr, oT[0:256, :])
    _PROGS["LB"] = nc
    return nc


def prog_LC1():
    if "LC1" in _PROGS:
        return _PROGS["LC1"]
    nc = bass.Bass("TRN2", target_bir_lowering=False)
    oT = _dt(nc, "oT", [D, TL], BF16); w_out = _dt(nc, "w_out", [D, D]); xT = _dt(nc, "xT", [D, TL]); par = _dt(nc, "par", [128, 8, 32])
    xTo = _dt(nc, "xTo", [D, TL], F32, "ExternalOutput")
    h2T = _dt(nc, "h2T", [D, TL], BF16, "ExternalOutput")
    stage_outproj(nc, oT, w_out, xT, xTo, par, 2)
    stage_norm(nc, xTo, par, 3, 4, h2T)
    _PROGS["LC1"] = nc
    return nc


def prog_LC2(final):
    key = "LC2f" if final else "LC2"
    if key in _PROGS:
        return _PROGS[key]
    nc = bass.Bass("TRN2", target_bir_lowering=False)
    h2 = _dt(nc, "h2", [D, TL + 2], BF16); w_up = _dt(nc, "w_up", [D, 2 * DFF]); cwp = _dt(nc, "cwp", [128, 4, 128])
    w_down = _dt(nc, "w_down", [DFF, D]); xT = _dt(nc, "xT", [D, TL]); par = _dt(nc, "par", [128, 8, 32])
    mT = _dt(nc, "mT", [DFF, TL], BF16, "Internal")
    xTo = _dt(nc, "xTo", [D, TL], F32, "ExternalOutput")
    stage_ffn_up(nc, h2, w_up, cwp, mT)
    stage_ffn_down(nc, mT, w_down, xT, xTo, par, 5)
    if final:
        idf = _dt(nc, "idf", [128, 128])
        y = _dt(nc, "y", [TL, D], F32, "ExternalOutput")
        stage_norm(nc, xTo, par, 6, None, y, idf_d=idf, final=True)
    else:
        hT = _dt(nc, "hT", [D, TL], BF16, "ExternalOutput")
        stage_norm(nc, xTo, par, 6, 7, hT)
    _PROGS[key] = nc
    return nc


def _run(nc, maps):
    return run_bass_kernel_spmd(nc, maps, core_ids=list(range(NCORE))).results


def kernel(x, c, w_ada, b_ada, ada_table, norm1_g, w_in, hg_lb_logits, hg_norm_g, da_lambda,
           da_norm_g, w_out, norm2_g, w_up, conv_w, conv_b, w_down, final_g):
    f32 = np.float32
    A = lambda a: np.ascontiguousarray(np.asarray(a))
    x = A(x); w_ada = A(w_ada); w_in = np.asarray(w_in); w_out = np.asarray(w_out); w_up = np.asarray(w_up); w_down = np.asarray(w_down)
    x2 = x.reshape(T, D)
    vecs = np.concatenate([np.asarray(c).reshape(1, D), np.asarray(b_ada).reshape(6, D), np.asarray(ada_table).reshape(24, D),
                           np.asarray(norm1_g), np.asarray(norm2_g), np.asarray(final_g).reshape(1, D)], 0).astype(f32).reshape(1280, 128)
    idf = np.eye(128, dtype=f32)
    idb = np.eye(128).astype(ml_dtypes.bfloat16)
    lbl4 = np.asarray(hg_lb_logits).reshape(2, DEPTH, 16, 128)
    hc = hgrn_consts()
    maps = []
    for j in range(NCORE):
        order = [2 * j, 2 * j + 1] + [h for h in range(16) if h not in (2 * j, 2 * j + 1)]
        maps.append({"x": A(x2[j * TL:(j + 1) * TL]), "vecs": vecs, "w_ada": w_ada,
                     "lbl": A(lbl4[:, :, order, :].reshape(128, 128)), "idf": idf})
    r0 = _run(prog_L0(), maps)
    par = r0[0]["par"]
    lbo = [r0[j]["lbo"] for j in range(NCORE)]
    xT = [r0[j]["xT"] for j in range(NCORE)]
    hT = [r0[j]["hT"] for j in range(NCORE)]
    del r0
    y = None
    for l in range(DEPTH):
        lam_init = 0.8 - 0.6 * math.exp(-0.3 * l)
        hT_all = np.concatenate(hT, axis=1)
        lamc = np.tile(np.array([[1.0 - lam_init, -lam_init]], f32), (128, 1))
        maps = []
        for j in range(NCORE):
            cols = np.concatenate([np.arange(g * 2048 + j * 256, g * 2048 + (j + 1) * 256) for g in range(5)] +
                                  [np.arange(10240 + g * 2048 + j * 256, 10240 + g * 2048 + (j + 1) * 256) for g in range(3)])
            lbo_l = np.repeat(lbo[j][:, :, :, l:l + 1, :], DEPTH, axis=3)
            maps.append({"hT": hT_all, "w": A(w_in[l][:, cols]), "acst": attn_consts(j), "lam": A(np.asarray(da_lambda)[l].T.astype(f32)),
                         "lamc": lamc, "gn": A(np.broadcast_to(np.asarray(da_norm_g)[l], (128, 256)).astype(f32)), "idb": idb,
                         "lbo": A(lbo_l), "hc": hc, "gnh": A(np.asarray(hg_norm_g)[l].reshape(128, 1).astype(f32))})
        rb = _run(prog_LB(), maps)
        oT_all = np.empty((D, T), dtype=ml_dtypes.bfloat16)
        for j in range(NCORE):
            oT_all[j * 256:(j + 1) * 256] = rb[j]["oT"][0:256]
            oT_all[2048 + j * 256:2048 + (j + 1) * 256] = rb[j]["oT"][256:512]
        del rb, hT_all
        nxt = par[:, 6 * (l + 1):6 * (l + 1) + 2, :] if l + 1 < DEPTH else np.repeat(par[:, 24:25, :], 2, axis=1)
        par_l = A(np.concatenate([par[:, 6 * l:6 * l + 6, :], nxt], axis=1))
        maps = [{"oT": A(oT_all[:, j * TL:(j + 1) * TL]), "w_out": A(w_out[l]), "xT": xT[j], "par": par_l} for j in range(NCORE)]
        rc = _run(prog_LC1(), maps)
        xT = [rc[j]["xTo"] for j in range(NCORE)]
        h2 = [rc[j]["h2T"] for j in range(NCORE)]
        del rc, oT_all
        zcol = np.zeros((D, 1), dtype=ml_dtypes.bfloat16)
        cwp = A(np.concatenate([np.asarray(conv_w)[l], np.asarray(conv_b)[l][None]], 0).astype(f32).reshape(4, 128, 128).transpose(2, 0, 1))
        maps = []
        for j in range(NCORE):
            left = h2[j - 1][:, -1:] if j > 0 else zcol
            right = h2[j + 1][:, :1] if j + 1 < NCORE else zcol
            m = {"h2": A(np.concatenate([left, h2[j], right], axis=1)), "w_up": A(w_up[l]), "cwp": cwp, "w_down": A(w_down[l]),
                 "xT": xT[j], "par": par_l}
            if l == DEPTH - 1:
                m["idf"] = idf
            maps.append(m)
        rd = _run(prog_LC2(l == DEPTH - 1), maps)
        xT = [rd[j]["xTo"] for j in range(NCORE)]
        if l == DEPTH - 1:
            y = np.concatenate([rd[j]["y"] for j in range(NCORE)], axis=0)
        else:
            hT = [rd[j]["hT"] for j in range(NCORE)]
        del rd
    return y.reshape(1, T, D).astype(f32)
```

```python
import contextlib
import math
import numpy as np
import ml_dtypes
import concourse.bass as bass
import concourse.mybir as mybir
from concourse.bass_utils import run_bass_kernel_spmd

F32 = mybir.dt.float32
BF16 = mybir.dt.bfloat16
AF = mybir.ActivationFunctionType
ALU = mybir.AluOpType
AX = mybir.AxisListType

D = 4096
KC = 32
T = 16384
NCORE = 8
TL = T // NCORE
DEPTH = 4
DFF = 8192
EPS = 1e-6
NVEC = 40
HCH = 32
DBG_PREP = [0]
VQ = ['sp']


class Op:
    __slots__ = ("q", "fn", "reads", "writes", "dma", "sem", "deps", "signal", "count", "idx", "prev")


class Prog:
    QUEUES = ("pe", "dve", "act", "pool", "sp")
    RING = 8
    _scnt = [0]

    def __init__(self, nc, strict=True):
        self.nc = nc
        self.ops = []
        self.strict = strict

    def add(self, q, fn, reads=(), writes=(), dma=False):
        o = Op()
        o.q, o.fn, o.reads, o.writes, o.dma = q, fn, tuple(reads), tuple(writes), dma
        o.sem = None
        o.deps, o.signal, o.count, o.idx, o.prev = [], False, 0, len(self.ops), 0
        self.ops.append(o)
        return o

    def dma(self, q, out, in_, reads, writes, **kw):
        return self.add(q, lambda e: e.dma_start(out=out, in_=in_, **kw), reads, writes, dma=True)

    def emit(self):
        nc = self.nc
        ops = self.ops
        last_w = {}
        readers = {}
        for o in ops:
            deps = set()
            for k in o.reads:
                if k in last_w:
                    deps.add(last_w[k])
            for k in o.writes:
                if k in last_w:
                    deps.add(last_w[k])
                for r in readers.get(k, ()):
                    deps.add(r)
            deps.discard(o.idx)
            o.deps = sorted(deps)
            for k in o.writes:
                last_w[k] = o.idx
                readers[k] = []
            for k in o.reads:
                readers.setdefault(k, []).append(o.idx)
        for o in ops:
            if o.dma:
                o.signal = True
            for d in o.deps:
                a = ops[d]
                if a.dma or a.q != o.q or (self.strict and a.q != "pe"):
                    a.signal = True
        counts = {}
        nd = {}
        for o in ops:
            if not o.signal:
                continue
            if o.dma:
                k = nd.get(o.q, 0)
                nd[o.q] = k + 1
                name = ("dq", o.q, k % self.RING)
            else:
                name = ("eng", o.q)
            o.sem = name
            o.prev = counts.get(name, 0)
            counts[name] = o.prev + (16 if o.dma else 1)
            o.count = counts[name]
        with contextlib.ExitStack() as st:
            sems = {}
            for i, name in enumerate(counts):
                Prog._scnt[0] += 1
                sems[name] = st.enter_context(nc.semaphore("s%d" % Prog._scnt[0]))
            block = st.enter_context(nc.Block())
            engs = {"pe": block.tensor, "dve": block.vector, "act": block.scalar,
                    "pool": block.gpsimd, "sp": block.sync}
            for q in self.QUEUES:
                qops = [o for o in ops if o.q == q]
                if not qops and q != "sp":
                    continue

                def body(e, qops=qops, q=q):
                    seen = {}

                    def wait(name, v):
                        if v > 0 and seen.get(name, 0) < v:
                            e.wait_ge(sems[name], v)
                            seen[name] = v
                    for o in qops:
                        need = {}
                        for d in o.deps:
                            a = ops[d]
                            if not a.signal:
                                continue
                            if (not a.dma) and a.q == q and (q == "pe" or not self.strict):
                                continue
                            if need.get(a.sem, 0) < a.count:
                                need[a.sem] = a.count
                        for nm, v in need.items():
                            wait(nm, v)
                        if o.dma:
                            wait(o.sem, o.prev)
                        ins = o.fn(e)
                        if o.signal:
                            ins.then_inc(sems[o.sem], 16 if o.dma else 1)
                    if q == "sp":
                        for name, c in counts.items():
                            if name[0] == "dq":
                                wait(name, c)
                engs[q](body)


class Stage:
    _cnt = [0]

    def __init__(self, nc, name):
        Stage._cnt[0] += 1
        self.nc, self.name = nc, "%s%d" % (name, Stage._cnt[0])
        self.st = contextlib.ExitStack()
        self.P = Prog(nc)
        self.n = 0

    def sb(self, shape, dt=F32, name=None):
        self.n += 1
        return self.st.enter_context(self.nc.sbuf_tensor("%s_%s%d" % (self.name, name or "t", self.n), list(shape), dt))

    def ps(self, shape=(128, 512), dt=F32, name=None):
        self.n += 1
        return self.st.enter_context(self.nc.psum_tensor("%s_%s%d" % (self.name, name or "p", self.n), list(shape), dt))

    def finish(self):
        self.P.emit()
        self.st.close()


def stage_params(nc, vecs, w_ada, lbl, idf_d, par, lbo):
    S = Stage(nc, "par")
    P = S.P
    V = S.sb([128, 10, 128])
    VT = S.sb([128, NVEC, 32])
    idf = S.sb([128, 128])
    sc = S.sb([128, 32])
    modT = S.sb([128, 6, 32])
    mm = S.sb([128, 6, 32])
    outp = S.sb([128, DEPTH * 6 + 1, 32])
    wsl = [S.sb([128, KC, 512]) for _ in range(2)]
    LR = S.sb([128, 128])
    LT = S.sb([128, 2, DEPTH, 16])
    EX = S.sb([128, 2, DEPTH, 16])
    mx = S.sb([128, 2, 16])
    sm = S.sb([128, 2, 16])
    lbs = S.sb([128, 2, 2, DEPTH, 16])
    pst = [S.ps() for _ in range(2)]
    psm = S.ps()

    P.dma("sp", idf[:], idf_d, [], ["idf"])
    P.dma("sp", V[:], vecs.rearrange("(i r) p -> r i p", r=128), [], ["V"])
    P.dma("sp", LR[:], lbl, [], ["LR"])
    for i in range(10):
        b = i % 2
        P.add("pe", lambda e, i=i, b=b: e.transpose(out=pst[b][:, 0:128], in_=V[:, i, :], identity=idf[:]),
              ["V", "idf"], [("pst", b)])
        P.add("dve", lambda e, i=i, b=b: e.tensor_copy(
            out=VT[:, 4 * i:4 * i + 4, :], in_=pst[b][:, 0:128].rearrange("p (v c) -> p v c", c=32)),
            [("pst", b)], ["VT"])
    P.add("act", lambda e: e.activation(out=sc[:], in_=VT[:, 0, :], func=AF.Silu), ["VT"], ["sc"])
    wv = w_ada.rearrange("(c p) n -> p c n", p=128)
    NSL = (6 * D) // 512
    for s in range(NSL):
        b = s % 2
        P.dma("sp", wsl[b][:], wv[:, :, s * 512:(s + 1) * 512], [], [("wsl", b)])
        for g in range(4):
            col = s * 4 + g
            for c in range(KC):
                P.add("pe", lambda e, b=b, g=g, c=c, col=col: e.matmul(
                    psm[:, col:col + 1], lhsT=wsl[b][:, c, g * 128:(g + 1) * 128], rhs=sc[:, c:c + 1],
                    start=(c == 0), stop=(c == KC - 1)), [("wsl", b), "sc"], ["psm"])
    P.add("dve", lambda e: e.tensor_tensor(out=modT[:], in0=psm[:, 0:192].rearrange("p (j c) -> p j c", c=32),
                                           in1=VT[:, 1:7, :], op=ALU.add), ["psm", "VT"], ["modT"])
    for l in range(DEPTH):
        P.add("dve", lambda e, l=l: e.tensor_tensor(out=mm[:], in0=modT[:], in1=VT[:, 7 + 6 * l:13 + 6 * l, :], op=ALU.add),
              ["modT", "VT", "outp"], ["mm"])
        P.add("dve", lambda e, l=l: e.scalar_tensor_tensor(out=outp[:, 6 * l + 0, :], in0=mm[:, 1, :], scalar=1.0,
                                                           in1=VT[:, 31 + l, :], op0=ALU.add, op1=ALU.mult), ["mm", "VT"], ["outp"])
        P.add("dve", lambda e, l=l: e.tensor_copy(out=outp[:, 6 * l + 1, :], in_=mm[:, 0, :]), ["mm"], ["outp"])
        P.add("dve", lambda e, l=l: e.tensor_copy(out=outp[:, 6 * l + 2, :], in_=mm[:, 2, :]), ["mm"], ["outp"])
        P.add("dve", lambda e, l=l: e.scalar_tensor_tensor(out=outp[:, 6 * l + 3, :], in0=mm[:, 4, :], scalar=1.0,
                                                           in1=VT[:, 35 + l, :], op0=ALU.add, op1=ALU.mult), ["mm", "VT"], ["outp"])
        P.add("dve", lambda e, l=l: e.tensor_copy(out=outp[:, 6 * l + 4, :], in_=mm[:, 3, :]), ["mm"], ["outp"])
        P.add("dve", lambda e, l=l: e.tensor_copy(out=outp[:, 6 * l + 5, :], in_=mm[:, 5, :]), ["mm"], ["outp"])
    P.add("dve", lambda e: e.tensor_copy(out=outp[:, 6 * DEPTH, :], in_=VT[:, 39, :]), ["VT"], ["outp"])
    P.dma("sp", par, outp[:], ["outp"], ["par"])
    P.add("pe", lambda e: e.transpose(out=pst[0][:, 0:128], in_=LR[:], identity=idf[:]), ["LR", "idf", ("pst", 0)], [("pst", 0)])
    P.add("dve", lambda e: e.tensor_copy(out=LT[:], in_=pst[0][:, 0:128].rearrange("p (d l h) -> p d l h", d=2, l=DEPTH)),
          [("pst", 0)], ["LT"])
    P.add("dve", lambda e: e.tensor_tensor(out=mx[:], in0=LT[:, :, 0, :], in1=LT[:, :, 1, :], op=ALU.max), ["LT"], ["mx"])
    for l in (2, 3):
        P.add("dve", lambda e, l=l: e.tensor_tensor(out=mx[:], in0=mx[:], in1=LT[:, :, l, :], op=ALU.max), ["LT", "mx"], ["mx"])
    for l in range(DEPTH):
        P.add("dve", lambda e, l=l: e.tensor_tensor(out=EX[:, :, l, :], in0=LT[:, :, l, :], in1=mx[:], op=ALU.subtract),
              ["LT", "mx"], ["EX"])
    P.add("act", lambda e: e.activation(out=EX[:], in_=EX[:], func=AF.Exp), ["EX"], ["EX"])
    P.add("dve", lambda e: e.tensor_tensor(out=sm[:], in0=EX[:, :, 0, :], in1=EX[:, :, 1, :], op=ALU.add), ["EX"], ["sm"])
    for l in (2, 3):
        P.add("dve", lambda e, l=l: e.tensor_tensor(out=sm[:], in0=sm[:], in1=EX[:, :, l, :], op=ALU.add), ["EX", "sm"], ["sm"])
    P.add("dve", lambda e: e.reciprocal(out=sm[:], in_=sm[:]), ["sm"], ["sm"])
    for l in range(DEPTH):
        P.add("dve", lambda e, l=l: e.tensor_tensor(out=EX[:, :, l, :], in0=EX[:, :, l, :], in1=sm[:], op=ALU.mult),
              ["EX", "sm"], ["EX"])
    P.add("dve", lambda e: e.memset(lbs[:, 0, :, 0, :], 0.0), [], ["lbs"])
    P.add("dve", lambda e: e.tensor_copy(out=lbs[:, 0, :, 1, :], in_=EX[:, :, 1, :]), ["EX", "lbs"], ["lbs"])
    for l in (2, 3):
        P.add("dve", lambda e, l=l: e.tensor_tensor(out=lbs[:, 0, :, l, :], in0=lbs[:, 0, :, l - 1, :], in1=EX[:, :, l, :], op=ALU.add),
              ["EX", "lbs"], ["lbs"])
    P.add("dve", lambda e: e.tensor_scalar(out=lbs[:, 1], in0=lbs[:, 0], scalar1=-1.0, scalar2=1.0, op0=ALU.mult, op1=ALU.add),
          ["lbs"], ["lbs"])
    P.dma("sp", lbo, lbs[:], ["lbs"], ["lbo"])
    S.finish()


def stage_pre(nc, x, xT, idf_d, ntok=TL):
    S = Stage(nc, "pre")
    P = S.P
    idf = S.sb([128, 128])
    xin = [S.sb([128, D]) for _ in range(2)]
    xo = [S.sb([128, 4, 128]) for _ in range(4)]
    pst = [S.ps() for _ in range(4)]
    P.dma("sp", idf[:], idf_d, [], ["idf"])
    xTv = xT.rearrange("(c p) t -> p c t", p=128)
    k = 0
    for tb in range(ntok // 128):
        b = tb % 2
        P.dma("sp", xin[b][:], x[tb * 128:(tb + 1) * 128, :], [], [("xin", b)])
        for cg in range(KC // 4):
            r = k % 4
            k += 1
            for j in range(4):
                c = cg * 4 + j
                P.add("pe", lambda e, b=b, c=c, j=j, r=r: e.transpose(out=pst[r][:, j * 128:(j + 1) * 128],
                                                                      in_=xin[b][:, c * 128:(c + 1) * 128], identity=idf[:]),
                      [("xin", b), "idf"], [("pst", r)])
            eng = "act" if (k % 2) else "dve"
            if eng == "act":
                P.add("act", lambda e, r=r: e.copy(out=xo[r][:], in_=pst[r][:].rearrange("p (j t) -> p j t", j=4)),
                      [("pst", r)], [("xo", r)])
            else:
                P.add("dve", lambda e, r=r: e.tensor_copy(out=xo[r][:], in_=pst[r][:].rearrange("p (j t) -> p j t", j=4)),
                      [("pst", r)], [("xo", r)])
            P.dma("pool", xTv[:, cg * 4:cg * 4 + 4, tb * 128:(tb + 1) * 128], xo[r][:], [("xo", r)], [("xT", tb, cg)])
    S.finish()


def stage_norm(nc, xT, par, ia, ib, out, idf_d=None, final=False, ntok=TL, TT=512, out_off=0):
    S = Stage(nc, "nrm")
    P = S.P
    A = S.sb([128, 32])
    B = S.sb([128, 32])
    ones = S.sb([128, 128])
    xt = [S.sb([128, KC, TT]) for _ in range(2)]
    sq = [S.sb([128, TT]) for _ in range(3)]
    rstd = S.sb([128, TT])
    tmp = [S.sb([128, TT]) for _ in range(3)]
    pss = S.ps()
    P.dma("sp", A[:], par[:, ia, :], [], ["A"])
    if not final:
        P.dma("sp", B[:], par[:, ib, :], [], ["B"])
        ho = [S.sb([128, KC, TT], BF16) for _ in range(2)]
        ov = out.rearrange("(c p) t -> p c t", p=128)
    else:
        idf = S.sb([128, 128])
        P.dma("sp", idf[:], idf_d, [], ["idf"])
        yo = [S.sb([128, TT]) for _ in range(3)]
        pst = [S.ps() for _ in range(3)]
        yrow = [S.sb([128, D]) for _ in range(2)]
    P.add("dve", lambda e: e.memset(ones[:], 1.0 / D), [], ["ones"])
    epst = S.sb([128, 1])
    P.add("dve", lambda e: e.memset(epst[:], EPS), [], ["eps"])
    xv = xT.rearrange("(c p) t -> p c t", p=128)
    nt = ntok // TT
    for tt in range(nt):
        b = tt % 2
        P.dma("sp", xt[b][:], xv[:, :, tt * TT:(tt + 1) * TT], [], [("xt", b)])
        for c in range(KC):
            r = c % 3
            P.add("act", lambda e, b=b, c=c, r=r: e.activation(out=sq[r][:], in_=xt[b][:, c, :], func=AF.Square),
                  [("xt", b)], [("sq", r)])
            P.add("pe", lambda e, c=c, r=r: e.matmul(pss[:, 0:TT], lhsT=ones[:], rhs=sq[r][:], start=(c == 0), stop=(c == KC - 1)),
                  [("sq", r), "ones"], ["pss"])
        P.add("act", lambda e: e.activation(out=rstd[:], in_=pss[:, 0:TT], func=AF.Sqrt, bias=epst[:, 0:1], scale=1.0),
              ["pss", "eps"], ["rstd"])
        P.add("dve", lambda e: e.reciprocal(out=rstd[:], in_=rstd[:]), ["rstd"], ["rstd"])
        if not final:
            for c in range(KC):
                r = c % 3
                P.add("dve", lambda e, b=b, c=c, r=r: e.scalar_tensor_tensor(out=tmp[r][:], in0=xt[b][:, c, :], scalar=A[:, c:c + 1],
                                                                             in1=rstd[:], op0=ALU.mult, op1=ALU.mult),
                      [("xt", b), "A", "rstd"], [("tmp", r)])
                P.add("act", lambda e, b=b, c=c, r=r: e.activation(out=ho[b][:, c, :], in_=tmp[r][:], func=AF.Identity,
                                                                   bias=B[:, c:c + 1], scale=1.0),
                      [("tmp", r), "B"], [("ho", b)])
            P.dma("pool", ov[:, :, out_off + tt * TT:out_off + (tt + 1) * TT], ho[b][:], [("ho", b)], [("out", tt)])
        else:
            for c in range(KC):
                r = c % 3
                P.add("dve", lambda e, b=b, c=c, r=r: e.scalar_tensor_tensor(out=yo[r][:], in0=xt[b][:, c, :], scalar=A[:, c:c + 1],
                                                                             in1=rstd[:], op0=ALU.mult, op1=ALU.mult),
                      [("xt", b), "A", "rstd"], [("yo", r)])
                for tb in range(TT // 128):
                    P.add("pe", lambda e, r=r, tb=tb: e.transpose(out=pst[r][:, tb * 128:(tb + 1) * 128],
                                                                  in_=yo[r][:, tb * 128:(tb + 1) * 128], identity=idf[:]),
                          [("yo", r), "idf"], [("pst", r)])
                P.add("act", lambda e, r=r: e.copy(out=tmp[r][:], in_=pst[r][:, 0:TT]), [("pst", r)], [("tmp", r)])
                for tb in range(TT // 128):
                    t0 = tt * TT + tb * 128
                    P.dma("pool", out[t0:t0 + 128, c * 128:(c + 1) * 128], tmp[r][:, tb * 128:(tb + 1) * 128],
                          [("tmp", r)], [("out", tt, c, tb)])
    S.finish()


def stage_inproj(nc, hT_all, w, projT, vtok, ntok=T, TT=512):
    S = Stage(nc, "inp")
    P = S.P
    NCOL = 2048
    W = S.sb([128, KC, NCOL], BF16)
    ht = [S.sb([128, KC, TT], BF16) for _ in range(2)]
    og = [S.sb([128, 4, TT], BF16) for _ in range(2)]
    vo = [S.sb([128, TT // 128, 256], BF16) for _ in range(2)]
    ps = [S.ps() for _ in range(4)]
    vtv = vtok.rearrange("(n p) d -> p n d", p=128)
    wv = w.rearrange("(c p) n -> p c n", p=128)
    for blk in range(8):
        P.dma("pool", W[:, :, blk * 256:(blk + 1) * 256], wv[:, :, blk * 256:(blk + 1) * 256], [], [("W", blk)])
    hv = hT_all.rearrange("(c p) t -> p c t", p=128)
    pv = projT.rearrange("(g p) t -> p g t", p=128)
    qscale = 128.0 ** -0.5
    k = 0
    for tt in range(ntok // TT):
        b = tt % 2
        P.dma("sp", ht[b][:], hv[:, :, tt * TT:(tt + 1) * TT], [], [("ht", b)])
        for cg in range(14):
            r = k % 4
            k += 1
            for c in range(KC):
                P.add("pe", lambda e, b=b, c=c, cg=cg, r=r: e.matmul(ps[r][:, 0:TT], lhsT=W[:, c, cg * 128:(cg + 1) * 128],
                                                                     rhs=ht[b][:, c, :], start=(c == 0), stop=(c == KC - 1)),
                      [("W", cg // 2), ("ht", b)], [("ps", r)])
            ob = (k // 4) % 2 if False else ((tt * 4 + cg // 4) % 2)
            sc = qscale if cg in (0, 1, 10, 11) else 1.0
            if k % 2:
                P.add("act", lambda e, r=r, ob=ob, cg=cg, sc=sc: e.activation(out=og[ob][:, cg % 4, :], in_=ps[r][:, 0:TT],
                                                                             func=AF.Copy, scale=sc),
                      [("ps", r)], [("og", ob)])
            else:
                P.add("dve", lambda e, r=r, ob=ob, cg=cg, sc=sc: e.tensor_scalar(out=og[ob][:, cg % 4, :], in0=ps[r][:, 0:TT],
                                                                                scalar1=sc, scalar2=None, op0=ALU.mult),
                      [("ps", r)], [("og", ob)])
            if cg % 4 == 3 or cg == 13:
                g0 = (cg // 4) * 4
                ng = cg - g0 + 1
                P.dma("pool", pv[:, g0:g0 + ng, tt * TT:(tt + 1) * TT], og[ob][:, 0:ng, :], [("og", ob)], [("proj", tt, g0)])
        for tb in range(TT // 128):
            r = k % 4
            k += 1
            for c in range(KC):
                P.add("pe", lambda e, b=b, c=c, tb=tb, r=r: e.matmul(ps[r][:, 0:256], lhsT=ht[b][:, c, tb * 128:(tb + 1) * 128],
                                                                     rhs=W[:, c, 1792:2048], start=(c == 0), stop=(c == KC - 1)),
                      [("W", 7), ("ht", b)], [("ps", r)])
            P.add("dve", lambda e, r=r, b=b, tb=tb: e.tensor_copy(out=vo[b][:, tb, :], in_=ps[r][:, 0:256]), [("ps", r)], [("vo", b)])
        P.dma("pool", vtv[:, tt * (TT // 128):(tt + 1) * (TT // 128), :], vo[b][:], [("vo", b)], [("vtok", tt)])
    S.finish()


def attn_consts(j):
    m = 2.0 ** (-8.0 * (j + 1) / 8.0)
    p = np.arange(128, dtype=np.float64)[:, None]
    n = np.arange(128, dtype=np.float64)[None, :]
    BL = -m * (128.0 * n - p)
    BR = -m * (p + 128.0 * n + 1.0)
    f = np.arange(256, dtype=np.float64)[None, :]
    Dg = [-m * np.abs(f - p - 128.0 * a) for a in range(2)]
    qb = np.arange(2, dtype=np.float64)[None, :]
    fL = np.exp(-m * (p + 128.0 * qb))
    fR = np.exp(-m * (255.0 - p - 128.0 * qb))
    return np.concatenate([BL, BR, Dg[0], Dg[1], fL, fR], axis=1).astype(np.float32)


def stage_attn(nc, projT, vtok, acst_d, lam_d, gn_d, idb_d, oT_rows, lamc_d, ntok=T, dbg=None):
    S = Stage(nc, "att")
    P = S.P
    NKB = ntok // 128
    NQG = ntok // 256
    kT = S.sb([128, 2, ntok], BF16)
    Vt = S.sb([128, NKB, 258], BF16)
    acst = S.sb([128, 772])
    BLm = S.sb([128, 128])
    BRm = S.sb([128, 128])
    lam4 = S.sb([128, 4])
    gnb = S.sb([128, 256])
    idb = S.sb([128, 128], BF16)
    onesf = S.sb([128, 128])
    qt = [S.sb([128, 2, 256], BF16) for _ in range(3)]
    PT = [S.sb([128, 2, 256], BF16) for _ in range(4)]
    sd = [S.sb([128, 2, 256]) for _ in range(2)]
    Osb = S.sb([128, 4, 257])
    sqc = [S.sb([128, 512], BF16) for _ in range(2)]
    onesb = S.sb([128, 128], BF16)
    lhl = S.sb([128, 4], BF16)
    vin = [S.sb([128, 2, 512], BF16) for _ in range(2)]
    mxs = S.sb([128, 2, 2 * (ntok // 512)])
    sm = S.sb([128, 16])
    res = S.sb([128, 256])
    junk = S.sb([128, 256])
    yb = S.sb([128, 256], BF16)
    oTs = [S.sb([128, 2, 256], BF16) for _ in range(2)]
    Sp = [S.ps() for _ in range(3)]
    Op = [S.ps() for _ in range(4)]
    Tp = S.ps([128, 1024], BF16)

    qv = projT[1280:1536, :].rearrange("(m d) t -> d m t", d=128)
    kv = projT[1536:1792, :].rearrange("(m d) t -> d m t", d=128)
    lamc = S.sb([128, 2])
    P.dma("sp", lamc[:], lamc_d, [], ["lamc"])
    P.dma("sp", acst[:], acst_d, [], ["acst"])
    P.dma("sp", lam4[:], lam_d, [], ["lam4"])
    P.dma("sp", gnb[:], gn_d, [], ["gnb"])
    P.dma("sp", idb[:], idb_d, [], ["idb"])
    P.dma("sp", kT[:], kv, [], ["kT"])
    P.add("dve", lambda e: e.memset(onesf[:], 1.0), [], ["onesf"])
    P.add("dve", lambda e: e.memset(onesb[:], 1.0), [], ["onesf"])
    P.add("dve", lambda e: e.memset(sm[:], 0.0), [], ["sm"])
    P.add("dve", lambda e: e.memset(sm[:, 15:16], EPS), ["sm"], ["sm"])
    P.add("dve", lambda e: e.memset(Vt[:, :, 256:258], 1.0), [], ["Vt1"])
    P.add("dve", lambda e: e.tensor_scalar(out=gnb[:], in0=gnb[:], scalar1=lamc[:, 0:1], scalar2=None, op0=ALU.mult), ["gnb", "lamc"], ["gnb"])
    if DBG_PREP[0] == 1:
        S.finish()
        return
    nch = ntok // 512
    k = 0
    for which, src in ((0, qv), (1, kv)):
        for m in range(2):
            for ch in range(nch):
                b = k % 2
                k += 1
                if which == 0:
                    P.dma("sp", vin[b][:, 0, :], src[:, m, ch * 512:(ch + 1) * 512], [], [("vin", b)])
                    P.add("act", lambda e, b=b: e.activation(out=sqc[b][:], in_=vin[b][:, 0, :], func=AF.Square), [("vin", b)], [("sqc", b)])
                else:
                    P.add("act", lambda e, b=b, m=m, ch=ch: e.activation(out=sqc[b][:], in_=kT[:, m, ch * 512:(ch + 1) * 512], func=AF.Square),
                          ["kT"], [("sqc", b)])
                P.add("pe", lambda e, b=b: e.matmul(Sp[b][:, 0:512], lhsT=onesb[:], rhs=sqc[b][:], start=True, stop=True),
                      [("sqc", b), "onesf"], [("S", b)])
                P.add("dve", lambda e, b=b, which=which, m=m, ch=ch: e.reduce_max(out=mxs[:, which, m * nch + ch:m * nch + ch + 1],
                                                                                 in_=Sp[b][:, 0:512], axis=AX.X),
                      [("S", b)], ["mxs"])
    P.add("dve", lambda e: e.reduce_max(out=sm[:, 0:2], in_=mxs[:], axis=AX.X), ["mxs"], ["sm"])
    P.add("dve", lambda e: e.tensor_tensor(out=sm[:, 2:3], in0=sm[:, 0:1], in1=sm[:, 1:2], op=ALU.mult), ["sm"], ["sm"])
    P.add("act", lambda e: e.activation(out=sm[:, 3:4], in_=sm[:, 2:3], func=AF.Sqrt, scale=1.1), ["sm"], ["sm"])
    P.add("dve", lambda e: e.tensor_scalar(out=sm[:, 4:5], in0=sm[:, 3:4], scalar1=-1.0, scalar2=None, op0=ALU.mult), ["sm"], ["sm"])
    P.add("dve", lambda e: e.tensor_scalar(out=BLm[:], in0=acst[:, 0:128], scalar1=sm[:, 3:4], scalar2=None, op0=ALU.subtract),
          ["acst", "sm"], ["BLm"])
    P.add("dve", lambda e: e.tensor_scalar(out=BRm[:], in0=acst[:, 128:256], scalar1=sm[:, 3:4], scalar2=None, op0=ALU.subtract),
          ["acst", "sm"], ["BRm"])
    P.add("dve", lambda e: e.tensor_tensor(out=sm[:, 5:6], in0=lam4[:, 0:1], in1=lam4[:, 1:2], op=ALU.mult), ["lam4", "sm"], ["sm"])
    P.add("dve", lambda e: e.tensor_tensor(out=sm[:, 6:7], in0=lam4[:, 2:3], in1=lam4[:, 3:4], op=ALU.mult), ["lam4", "sm"], ["sm"])
    P.add("dve", lambda e: e.tensor_copy(out=lhl[:, 0:2], in_=sm[:, 5:7]), ["sm"], ["lhl"])
    P.add("dve", lambda e: e.tensor_tensor(out=sm[:, 9:11], in0=sm[:, 5:7], in1=lhl[:, 0:2], op=ALU.subtract), ["sm", "lhl"], ["sm"])
    P.add("dve", lambda e: e.tensor_copy(out=lhl[:, 2:4], in_=sm[:, 9:11]), ["sm", "lhl"], ["lhl"])
    P.add("pe", lambda e: e.matmul(Sp[2][:, 0:4], lhsT=onesb[:], rhs=lhl[:], start=True, stop=True), ["lhl", "onesf"], [("S", 2)])
    P.add("dve", lambda e: e.tensor_copy(out=sm[:, 9:11], in_=Sp[2][:, 2:4]), [("S", 2), "sm"], ["sm"])
    P.add("dve", lambda e: e.tensor_tensor(out=sm[:, 5:7], in0=Sp[2][:, 0:2], in1=sm[:, 9:11], op=ALU.add), [("S", 2), "sm"], ["sm"])
    P.add("act", lambda e: e.activation(out=sm[:, 9:11], in_=sm[:, 5:7], func=AF.Exp), ["sm"], ["sm"])
    P.add("dve", lambda e: e.tensor_tensor(out=sm[:, 8:9], in0=sm[:, 10:11], in1=sm[:, 9:10], op=ALU.subtract), ["sm"], ["sm"])
    P.add("dve", lambda e: e.tensor_scalar(out=sm[:, 8:9], in0=sm[:, 8:9], scalar1=lamc[:, 1:2], scalar2=None, op0=ALU.add), ["sm", "lamc"], ["sm"])
    if DBG_PREP[0] == 2:
        S.finish()
        return
    vtv = vtok.rearrange("(n p) d -> p n d", p=128)
    VG = max(1, NKB // 4)
    for g4 in range(NKB // VG):
        P.dma("sp", Vt[:, g4 * VG:(g4 + 1) * VG, 0:256], vtv[:, g4 * VG:(g4 + 1) * VG, :], [], [("Vt", g4)])
    steps = []
    for qg in (range(NQG) if dbg is None else dbg):
        sides = []
        if qg > 0:
            sides.append(("L", list(range(0, 2 * qg))))
        sides.append(("C", [2 * qg, 2 * qg + 1]))
        if 2 * qg + 2 < NKB:
            sides.append(("R", list(range(2 * qg + 2, NKB))))
        for si, (side, kbs) in enumerate(sides):
            for ki, kb in enumerate(kbs):
                steps.append((qg, si, side, kb, ki, len(kbs), len(sides)))
    yb4 = [S.sb([128, 256], BF16) for _ in range(4)]
    LA = 2
    deferred = []

    def emit_front(t):
        qg, si, side, kb, ki, nk, ns = steps[t]
        qb_ = qg % 3
        r = t % 3
        r2 = t % 4
        if si == 0 and ki == 0:
            P.dma("sp", qt[qb_][:], qv[:, :, qg * 256:(qg + 1) * 256], [], [("qt", qb_)])
        for m in range(2):
            P.add("pe", lambda e, r=r, m=m, kb=kb, qb_=qb_: e.matmul(Sp[r][:, m * 256:(m + 1) * 256], lhsT=kT[:, m, kb * 128:(kb + 1) * 128],
                                                                   rhs=qt[qb_][:, m, :], start=True, stop=True),
                  ["kT", ("qt", qb_)], [("S", r)])
        if side == "L":
            n = 2 * qg - kb
            P.add("act", lambda e, r=r, r2=r2, n=n: e.activation(out=PT[r2][:].rearrange("p m q -> p (m q)"), in_=Sp[r][:, 0:512], func=AF.Exp,
                                                                 bias=BLm[:, n:n + 1], scale=1.0),
                  [("S", r), "BLm"], [("PT", r2)])
        elif side == "R":
            n = kb - 2 * qg - 2
            P.add("act", lambda e, r=r, r2=r2, n=n: e.activation(out=PT[r2][:].rearrange("p m q -> p (m q)"), in_=Sp[r][:, 0:512], func=AF.Exp,
                                                                 bias=BRm[:, n:n + 1], scale=1.0),
                  [("S", r), "BRm"], [("PT", r2)])
        else:
            a = kb - 2 * qg
            d_ = ki % 2
            for m in range(2):
                P.add("dve", lambda e, r=r, m=m, a=a, d_=d_: e.tensor_tensor(out=sd[d_][:, m, :], in0=Sp[r][:, m * 256:(m + 1) * 256],
                                                                             in1=acst[:, 256 + a * 256:512 + a * 256], op=ALU.add),
                      [("S", r), "acst"], [("sd", d_)])
            P.add("act", lambda e, r2=r2, d_=d_: e.activation(out=PT[r2][:].rearrange("p m q -> p (m q)"), in_=sd[d_][:].rearrange("p m q -> p (m q)"),
                                                              func=AF.Exp, bias=sm[:, 4:5], scale=1.0),
                  [("sd", d_), "sm"], [("PT", r2)])

    def emit_back(t):
        qg, si, side, kb, ki, nk, ns = steps[t]
        r2 = t % 4
        for qb in range(2):
            for m in range(2):
                i = qb * 2 + m
                P.add("pe", lambda e, i=i, r2=r2, m=m, qb=qb, kb=kb, ki=ki, nk=nk: e.matmul(
                    Op[i][:, 0:257], lhsT=PT[r2][:, m, qb * 128:(qb + 1) * 128], rhs=Vt[:, kb, 0:257],
                    start=(ki == 0), stop=(ki == nk - 1)),
                    [("PT", r2), ("Vt", kb // VG), "Vt1"], [("O", i)])
        if ki != nk - 1:
            return
        for i in range(4):
            qb = i // 2
            if si == 0:
                if side == "L":
                    P.add("dve", lambda e, i=i, qb=qb: e.tensor_scalar(out=Osb[:, i, :], in0=Op[i][:, 0:257], scalar1=acst[:, 768 + qb:769 + qb],
                                                                       scalar2=None, op0=ALU.mult), [("O", i), "acst"], ["Osb"])
                else:
                    P.add("dve", lambda e, i=i: e.tensor_copy(out=Osb[:, i, :], in_=Op[i][:, 0:257]), [("O", i)], ["Osb"])
            elif side == "C":
                P.add("dve", lambda e, i=i: e.tensor_tensor(out=Osb[:, i, :], in0=Osb[:, i, :], in1=Op[i][:, 0:257], op=ALU.add),
                      [("O", i), "Osb"], ["Osb"])
            else:
                P.add("dve", lambda e, i=i, qb=qb: e.scalar_tensor_tensor(out=Osb[:, i, :], in0=Op[i][:, 0:257], scalar=acst[:, 770 + qb:771 + qb],
                                                                          in1=Osb[:, i, :], op0=ALU.mult, op1=ALU.add),
                      [("O", i), "Osb", "acst"], ["Osb"])
        if si != ns - 1:
            return
        ob = qg % 2
        for qb in range(2):
            y_ = yb4[ob * 2 + qb]
            yk = ("yb", ob * 2 + qb)
            P.add("dve", lambda e, qb=qb: e.reciprocal(out=sm[:, 11:12], in_=Osb[:, 2 * qb, 256:257]), ["Osb", "sm"], ["sm"])
            P.add("dve", lambda e, qb=qb: e.reciprocal(out=sm[:, 12:13], in_=Osb[:, 2 * qb + 1, 256:257]), ["Osb", "sm"], ["sm"])
            P.add("dve", lambda e: e.tensor_tensor(out=sm[:, 12:13], in0=sm[:, 12:13], in1=sm[:, 8:9], op=ALU.mult), ["sm"], ["sm"])
            P.add("dve", lambda e, qb=qb: e.tensor_scalar(out=res[:], in0=Osb[:, 2 * qb, 0:256], scalar1=sm[:, 11:12], scalar2=None, op0=ALU.mult),
                  ["Osb", "sm"], ["res"])
            P.add("dve", lambda e, qb=qb: e.scalar_tensor_tensor(out=res[:], in0=Osb[:, 2 * qb + 1, 0:256], scalar=sm[:, 12:13], in1=res[:],
                                                                 op0=ALU.mult, op1=ALU.add), ["Osb", "sm", "res"], ["res"])
            P.add("dve", lambda e: e.memset(sm[:, 13:14], 0.0), ["sm"], ["sm"])
            P.add("act", lambda e: e.activation(out=junk[:], in_=res[:], func=AF.Square, accum_out=sm[:, 13:14]), ["res", "sm"], ["junk", "sm"])
            P.add("act", lambda e: e.activation(out=sm[:, 14:15], in_=sm[:, 13:14], func=AF.Sqrt, bias=sm[:, 15:16], scale=1.0 / 256.0),
                  ["sm"], ["sm"])
            P.add("dve", lambda e: e.reciprocal(out=sm[:, 14:15], in_=sm[:, 14:15]), ["sm"], ["sm"])
            P.add("dve", lambda e, y_=y_: e.scalar_tensor_tensor(out=y_[:], in0=res[:], scalar=sm[:, 14:15], in1=gnb[:], op0=ALU.mult, op1=ALU.mult),
                  ["res", "sm", "gnb"], [yk])

        def tail(qg=qg, ob=ob):
            for qb in range(2):
                y_ = yb4[ob * 2 + qb]
                yk = ("yb", ob * 2 + qb)
                tslot = 512 * 0 + qb * 256
                for dc in range(2):
                    P.add("pe", lambda e, dc=dc, y_=y_, tslot=tslot: e.transpose(out=Tp[:, tslot + dc * 128:tslot + (dc + 1) * 128],
                                                                                 in_=y_[:, dc * 128:(dc + 1) * 128], identity=idb[:]),
                          [yk, "idb"], ["Tpbank"])
                P.add("act", lambda e, ob=ob, qb=qb, tslot=tslot: e.copy(out=oTs[ob][:, :, qb * 128:(qb + 1) * 128],
                                                                         in_=Tp[:, tslot:tslot + 256].rearrange("p (c q) -> p c q", c=2)),
                      ["Tpbank"], [("oTs", ob)])
            P.dma("pool", oT_rows.rearrange("(c p) t -> p c t", p=128)[:, :, qg * 256:(qg + 1) * 256], oTs[ob][:], [("oTs", ob)], [("oT", qg)])
        deferred.append((t + 10, tail))

    ns_ = len(steps)
    for t in range(ns_ + LA):
        if t < ns_:
            emit_front(t)
        if t >= LA:
            emit_back(t - LA)
        while deferred and deferred[0][0] <= t:
            deferred.pop(0)[1]()
    while deferred:
        deferred.pop(0)[1]()
    S.finish()


def hgrn_consts(SEG=512):
    cm = np.ones((128, SEG), np.float32)
    cm[:, 0::HCH] = 0.0
    s = np.arange(HCH)[:, None]
    t = np.arange(HCH)[None, :]
    mk = np.zeros((128, 2 * HCH), np.float32)
    mk[0:HCH, 0:HCH] = (s <= t)
    mk[0:HCH, HCH:2 * HCH] = (s >= t)
    return np.concatenate([cm, mk], axis=1)


def stage_hgrn(nc, projT, lbo_d, l, hcst_d, gnh_d, idb_d, o_scr, oT_rows, ntok=T, SEG=512):
    S = Stage(nc, "hg")
    P = S.P
    nseg = ntok // SEG
    nch = SEG // HCH
    hcst = S.sb([128, SEG + 2 * HCH])
    idb = S.sb([128, 128], BF16)
    lbt = S.sb([128, 2, 2, DEPTH, 16])
    noml = S.sb([128, 2, DEPTH, 16])
    gnh = S.sb([128, 1])
    onesn = S.sb([128, 128])
    epst = S.sb([128, 1])
    S32 = [S.sb([128, 128]) for _ in range(2)]
    Sbf = [S.sb([128, 128], BF16) for _ in range(2)]
    qin = [S.sb([128, 2, SEG], BF16) for _ in range(2)]
    zin = [S.sb([128, 2, SEG], BF16) for _ in range(2)]
    vin = [S.sb([128, 2, SEG], BF16) for _ in range(2)]
    gin = [S.sb([128, 2, SEG], BF16) for _ in range(2)]
    ofw = [S.sb([128, 2, SEG]) for _ in range(2)]
    sig = [S.sb([128, SEG]) for _ in range(2)]
    g = [S.sb([128, SEG]) for _ in range(2)]
    kc = [S.sb([128, SEG]) for _ in range(2)]
    bb = [S.sb([128, SEG]) for _ in range(2)]
    arg = [S.sb([128, SEG]) for _ in range(2)]
    cq = [S.sb([128, SEG]) for _ in range(2)]
    ck = [S.sb([128, SEG]) for _ in range(2)]
    ex = [S.sb([128, SEG]) for _ in range(2)]
    Qt = [S.sb([128, SEG], BF16) for _ in range(2)]
    Kt = [S.sb([128, SEG], BF16) for _ in range(2)]
    Kh = [S.sb([128, SEG], BF16) for _ in range(2)]
    ebl = [S.sb([128, nch]) for _ in range(2)]
    Am = [S.sb([HCH, HCH], BF16) for _ in range(4)]
    VK = [S.sb([HCH, 256], BF16) for _ in range(4)]
    osb = [S.sb([128, SEG]) for _ in range(2)]
    ysb = [S.sb([128, SEG]) for _ in range(2)]
    yb = [S.sb([128, SEG], BF16) for _ in range(2)]
    Ap2 = [S.ps() for _ in range(2)]
    Dp2 = [S.ps() for _ in range(2)]
    Tp2 = [S.ps([128, 1024], BF16) for _ in range(2)]
    Np = Dp2[0]
    Oa1 = [S.ps() for _ in range(2)]

    P.dma("sp", hcst[:], hcst_d, [], ["hcst"])
    P.dma("sp", idb[:], idb_d, [], ["idb"])
    P.dma("sp", lbt[:], lbo_d, [], ["lbt"])
    P.dma("sp", gnh[:], gnh_d, [], ["gnh"])
    P.add("dve", lambda e: e.tensor_scalar(out=noml[:], in0=lbt[:, 1], scalar1=-1.0, scalar2=None, op0=ALU.mult), ["lbt"], ["noml"])
    P.add("dve", lambda e: e.memset(onesn[:], 1.0 / 128.0), [], ["onesn"])
    P.add("dve", lambda e: e.memset(epst[:], EPS), [], ["eps"])
    cm = hcst[:, 0:SEG]
    pq = projT[0:256, :].rearrange("(a k) t -> k a t", k=128)
    pv = projT[768:1024, :].rearrange("(a k) t -> k a t", k=128)
    pg = projT[1024:1280, :].rearrange("(a k) t -> k a t", k=128)
    osv = o_scr.rearrange("(a k) t -> k a t", k=128)
    oTv = oT_rows.rearrange("(a k) t -> k a t", k=128)
    kk = 0
    for dirn in range(2):
        pz = projT[256 + 256 * dirn:512 + 256 * dirn, :].rearrange("(a k) t -> k a t", k=128)
        mask = hcst[0:HCH, SEG + dirn * HCH:SEG + (dirn + 1) * HCH]
        for a in range(2):
            P.add("dve", lambda e, a=a: e.memset(S32[a][:], 0.0), [("S32", a)], [("S32", a)])
            P.add("dve", lambda e, a=a: e.memset(Sbf[a][:], 0.0), [("Sbf", a)], [("Sbf", a)])
        segs = list(range(nseg)) if dirn == 0 else list(range(nseg - 1, -1, -1))
        for si, seg in enumerate(segs):
            sb_ = si % 2
            ts = slice(seg * SEG, (seg + 1) * SEG)
            P.dma("sp", qin[sb_][:], pq[:, :, ts], [], [("qin", sb_)])
            P.dma("sp", zin[sb_][:], pz[:, :, ts], [], [("zin", sb_)])
            P.dma("sp", vin[sb_][:], pv[:, :, ts], [], [("vin", sb_)])
            if dirn == 1:
                P.dma("sp", gin[sb_][:], pg[:, :, ts], [], [("gin", sb_)])
                P.dma("sp", ofw[sb_][:], osv[:, :, ts], [("oscr", seg)], [("ofw", sb_)])
            for a in range(2):
                lb_ap = lbt[:, 0, dirn, l, a:a + 1]
                oml_ap = lbt[:, 1, dirn, l, a:a + 1]
                noml_ap = noml[:, dirn, l, a:a + 1]
                P.add("act", lambda e, a=a, sb_=sb_: e.activation(out=sig[a][:], in_=zin[sb_][:, a, :], func=AF.Sigmoid), [("zin", sb_)], [("sig", a)])
                P.add("act", lambda e, a=a, lb_ap=lb_ap, oml_ap=oml_ap: e.activation(out=g[a][:], in_=sig[a][:], func=AF.Ln, bias=lb_ap, scale=oml_ap),
                      [("sig", a), "lbt"], [("g", a)])
                P.add("dve", lambda e, a=a, noml_ap=noml_ap, oml_ap=oml_ap: e.tensor_scalar(out=kc[a][:], in0=sig[a][:], scalar1=noml_ap, scalar2=oml_ap,
                                                                                          op0=ALU.mult, op1=ALU.add), [("sig", a), "noml", "lbt"], [("kc", a)])
                P.add("dve", lambda e, a=a: e.tensor_tensor_scan(out=bb[a][:], data0=cm, data1=g[a][:], initial=0.0, op0=ALU.mult, op1=ALU.add),
                      [("g", a), "hcst"], [("bb", a)])
                b3 = bb[a][:].rearrange("p (n c) -> p n c", c=HCH)
                P.add("dve", lambda e, a=a, b3=b3: e.tensor_tensor(out=arg[a][:].rearrange("p (n c) -> p n c", c=HCH),
                                                                   in0=b3[:, :, HCH - 1:HCH].to_broadcast([128, nch, HCH]), in1=b3, op=ALU.subtract),
                      [("bb", a)], [("arg", a)])
                P.add("act", lambda e, a=a, b3=b3: e.activation(out=ebl[a][:], in_=b3[:, :, HCH - 1], func=AF.Exp), [("bb", a)], [("ebl", a)])
                if dirn == 0:
                    cq_ap, ck_ap = bb[a], arg[a]
                    kq, kk_ = ("bb", a), ("arg", a)
                else:
                    P.add("dve", lambda e, a=a: e.tensor_tensor(out=cq[a][:], in0=arg[a][:], in1=g[a][:], op=ALU.add), [("arg", a), ("g", a)], [("cq", a)])
                    P.add("dve", lambda e, a=a: e.tensor_tensor(out=ck[a][:], in0=bb[a][:], in1=g[a][:], op=ALU.subtract), [("bb", a), ("g", a)], [("ck", a)])
                    cq_ap, ck_ap = cq[a], ck[a]
                    kq, kk_ = ("cq", a), ("ck", a)
                P.add("act", lambda e, a=a, cq_ap=cq_ap: e.activation(out=ex[0][:], in_=cq_ap[:], func=AF.Exp), [kq], [("ex", 0)])
                P.add("dve", lambda e, a=a, sb_=sb_: e.tensor_tensor(out=Qt[a][:], in0=qin[sb_][:, a, :], in1=ex[0][:], op=ALU.mult),
                      [("ex", 0), ("qin", sb_)], [("Qt", a)])
                P.add("act", lambda e, a=a, cq_ap=cq_ap: e.activation(out=ex[1][:], in_=cq_ap[:], func=AF.Exp, scale=-1.0), [kq], [("ex", 1)])
                P.add("dve", lambda e, a=a: e.tensor_tensor(out=Kt[a][:], in0=kc[a][:], in1=ex[1][:], op=ALU.mult), [("ex", 1), ("kc", a)], [("Kt", a)])
                P.add("act", lambda e, a=a, ck_ap=ck_ap: e.activation(out=ex[0][:], in_=ck_ap[:], func=AF.Exp), [kk_, ("ex", 0)], [("ex", 0)])
                P.add("dve", lambda e, a=a: e.tensor_tensor(out=Kh[a][:], in0=kc[a][:], in1=ex[0][:], op=ALU.mult), [("ex", 0), ("kc", a)], [("Kh", a)])
            chunks = list(range(nch)) if dirn == 0 else list(range(nch - 1, -1, -1))
            for ci, n in enumerate(chunks):
                cs = slice(n * HCH, (n + 1) * HCH)
                for a in range(2):
                    r = kk % 4
                    rb = kk % 2
                    kk += 1
                    Ap, Dp, Tp = Ap2[rb], Dp2[rb], Tp2[rb]
                    P.add("pe", lambda e, a=a, Ap=Ap, cs=cs: e.matmul(Ap[0:HCH, 0:HCH], lhsT=Kt[a][:, cs], rhs=Qt[a][:, cs], start=True, stop=True),
                          [("Kt", a), ("Qt", a)], [("Ap", rb)])
                    P.add("dve", lambda e, r=r, Ap=Ap, mask=mask: e.tensor_tensor(out=Am[r][:], in0=Ap[0:HCH, 0:HCH], in1=mask, op=ALU.mult),
                          [("Ap", rb), "hcst"], [("Am", r)])
                    P.add("pe", lambda e, a=a, Tp=Tp, cs=cs, sb_=sb_: e.transpose(out=Tp[0:HCH, 0:128], in_=vin[sb_][:, a, cs], identity=idb[:]),
                          [("vin", sb_), "idb"], [("Tp", rb)])
                    P.add("pe", lambda e, a=a, Tp=Tp, cs=cs: e.transpose(out=Tp[0:HCH, 128:256], in_=Kh[a][:, cs], identity=idb[:]),
                          [("Kh", a), "idb"], [("Tp", rb)])
                    P.add("act", lambda e, r=r, Tp=Tp: e.copy(out=VK[r][:], in_=Tp[0:HCH, 0:256]), [("Tp", rb)], [("VK", r)])
                    P.add("pe", lambda e, a=a, cs=cs: e.matmul(Oa1[a][:, cs], lhsT=Sbf[a][:], rhs=Qt[a][:, cs], start=True, stop=False),
                          [("Sbf", a), ("Qt", a)], [("Oa", a)])
                    P.add("pe", lambda e, a=a, r=r, cs=cs: e.matmul(Oa1[a][:, cs], lhsT=VK[r][:, 0:128], rhs=Am[r][:], start=False, stop=True),
                          [("VK", r), ("Am", r)], [("Oa", a)])
                    P.add("pe", lambda e, r=r, Dp=Dp: e.matmul(Dp[:, 0:128], lhsT=VK[r][:, 128:256], rhs=VK[r][:, 0:128], start=True, stop=True),
                          [("VK", r)], [("Dp", rb)])
                    P.add("dve", lambda e, a=a, Dp=Dp, n=n: e.scalar_tensor_tensor(out=S32[a][:], in0=S32[a][:], scalar=ebl[a][:, n:n + 1],
                                                                                  in1=Dp[:, 0:128], op0=ALU.mult, op1=ALU.add),
                          [("S32", a), ("ebl", a), ("Dp", rb)], [("S32", a)])
                    P.add("act", lambda e, a=a: e.copy(out=Sbf[a][:], in_=S32[a][:]), [("S32", a)], [("Sbf", a)])
            for a in range(2):
                if dirn == 0:
                    P.add("dve", lambda e, a=a, sb_=sb_: e.tensor_copy(out=osb[a][:], in_=Oa1[a][:, 0:SEG]), [("Oa", a)], [("osb", a)])
                    P.dma("pool", osv[:, a, ts], osb[a][:], [("osb", a)], [("oscr", seg)])
                else:
                    P.add("dve", lambda e, a=a, sb_=sb_: e.tensor_tensor(out=osb[a][:], in0=Oa1[a][:, 0:SEG], in1=ofw[sb_][:, a, :], op=ALU.add),
                          [("Oa", a), ("ofw", sb_)], [("osb", a)])
                    P.add("act", lambda e, a=a: e.activation(out=ysb[a][:], in_=osb[a][:], func=AF.Square), [("osb", a)], [("ysb", a)])
                    P.add("pe", lambda e, a=a: e.matmul(Np[:, 0:SEG], lhsT=onesn[:], rhs=ysb[a][:], start=True, stop=True), [("ysb", a), "onesn"], [("Dp", 0)])
                    P.add("act", lambda e, a=a: e.activation(out=ysb[a][:], in_=Np[:, 0:SEG], func=AF.Sqrt, bias=epst[:, 0:1], scale=1.0),
                          [("Dp", 0), "eps"], [("ysb", a)])
                    P.add("dve", lambda e, a=a: e.reciprocal(out=ysb[a][:], in_=ysb[a][:]), [("ysb", a)], [("ysb", a)])
                    P.add("dve", lambda e, a=a: e.tensor_tensor(out=osb[a][:], in0=osb[a][:], in1=ysb[a][:], op=ALU.mult), [("osb", a), ("ysb", a)], [("osb", a)])
                    P.add("act", lambda e, a=a, sb_=sb_: e.activation(out=ysb[a][:], in_=gin[sb_][:, a, :], func=AF.Silu), [("gin", sb_), ("ysb", a)], [("ysb", a)])
                    P.add("dve", lambda e, a=a: e.scalar_tensor_tensor(out=yb[a][:], in0=osb[a][:], scalar=gnh[:, 0:1], in1=ysb[a][:], op0=ALU.mult, op1=ALU.mult),
                          [("osb", a), ("ysb", a), "gnh"], [("yb", a)])
                    P.dma("pool", oTv[:, a, ts], yb[a][:], [("yb", a)], [("oT", seg, a)])
    S.finish()


def stage_outproj(nc, oT_tok, w_out, xT_in, xT_out, par, ig, ntok=TL, TT=512):
    S = Stage(nc, "op")
    P = S.P
    G = S.sb([128, 32])
    oTr = S.sb([128, KC, ntok], BF16)
    Wg = [S.sb([128, KC, 128], BF16) for _ in range(2)]
    xg = [S.sb([128, ntok]) for _ in range(2)]
    ps = [S.ps() for _ in range(4)]
    P.dma("sp", G[:], par[:, ig, :], [], ["G"])
    ov = oT_tok.rearrange("(c p) t -> p c t", p=128)
    for q4 in range(4):
        P.dma("sp", oTr[:, q4 * 8:(q4 + 1) * 8, :], ov[:, q4 * 8:(q4 + 1) * 8, :], [], [("oTr", q4)])
    wv = w_out.rearrange("(c p) n -> p c n", p=128)
    k = 0
    for cg in range(D // 128):
        b = cg % 2
        P.dma("pool", Wg[b][:], wv[:, :, cg * 128:(cg + 1) * 128], [], [("Wg", b)])
        P.dma("sp", xg[b][:], xT_in[cg * 128:(cg + 1) * 128, :], [], [("xg", b)])
        for tt in range(ntok // TT):
            r = k % 4
            k += 1
            for c in range(KC):
                P.add("pe", lambda e, b=b, c=c, r=r, tt=tt: e.matmul(ps[r][:, 0:TT], lhsT=Wg[b][:, c, :], rhs=oTr[:, c, tt * TT:(tt + 1) * TT],
                                                                     start=(c == 0), stop=(c == KC - 1)),
                      [("Wg", b), ("oTr", c // 8)], [("ps", r)])
            P.add("dve", lambda e, b=b, r=r, tt=tt, cg=cg: e.scalar_tensor_tensor(out=xg[b][:, tt * TT:(tt + 1) * TT], in0=ps[r][:, 0:TT], scalar=G[:, cg:cg + 1],
                                                                                  in1=xg[b][:, tt * TT:(tt + 1) * TT], op0=ALU.mult, op1=ALU.add),
                  [("ps", r), "G", ("xg", b)], [("xg", b)])
        P.dma("sp", xT_out[cg * 128:(cg + 1) * 128, :], xg[b][:], [("xg", b)], [("xTo", cg)])
    S.finish()


def stage_ffn_up(nc, h2T_halo, w_up, cwp_d, mT, ntok=TL, TT=512):
    S = Stage(nc, "up")
    P = S.P
    NH = ntok + 2
    h2 = S.sb([128, KC, NH], BF16)
    Wg = [S.sb([128, KC, 256], BF16) for _ in range(2)]
    u = S.sb([128, 2, NH])
    cwp = S.sb([128, 4, 128])
    HT = 1024
    tA = [S.sb([128, HT]) for _ in range(2)]
    sg = S.sb([128, HT])
    mo = [S.sb([128, ntok], BF16) for _ in range(2)]
    ps = [S.ps() for _ in range(4)]
    P.dma("sp", cwp[:], cwp_d, [], ["cwp"])
    hv = h2T_halo.rearrange("(c p) t -> p c t", p=128)
    for q4 in range(4):
        P.dma("sp", h2[:, q4 * 8:(q4 + 1) * 8, :], hv[:, q4 * 8:(q4 + 1) * 8, :], [], [("h2", q4)])
    wv = w_up.rearrange("(c p) n -> p c n", p=128)
    mv = mT.rearrange("(g p) t -> p g t", p=128)
    k = 0
    for jg in range(DFF // 128):
        b = jg % 2
        P.dma("pool", Wg[b][:, :, 0:128], wv[:, :, jg * 128:(jg + 1) * 128], [], [("Wg", b, 0)])
        P.dma("pool", Wg[b][:, :, 128:256], wv[:, :, DFF + jg * 128:DFF + (jg + 1) * 128], [], [("Wg", b, 1)])
        for gv in range(2):
            for ct in range(ntok // TT + 1):
                r = k % 4
                k += 1
                if ct < ntok // TT:
                    n = TT
                    rsl = slice(1 + ct * TT, 1 + (ct + 1) * TT)
                else:
                    n = 2
                    rsl = slice(0, NH, NH - 1)
                for c in range(KC):
                    P.add("pe", lambda e, b=b, c=c, r=r, gv=gv, n=n, rsl=rsl: e.matmul(ps[r][:, 0:n], lhsT=Wg[b][:, c, gv * 128:(gv + 1) * 128],
                                                                                       rhs=h2[:, c, rsl], start=(c == 0), stop=(c == KC - 1)),
                          [("Wg", b, gv), ("h2", c // 8)], [("ps", r)])
                if k % 2:
                    P.add("act", lambda e, r=r, gv=gv, n=n, rsl=rsl: e.copy(out=u[:, gv, rsl], in_=ps[r][:, 0:n]), [("ps", r)], [("u", gv)])
                else:
                    P.add("dve", lambda e, r=r, gv=gv, n=n, rsl=rsl: e.tensor_copy(out=u[:, gv, rsl], in_=ps[r][:, 0:n]), [("ps", r)], [("u", gv)])
        for th in range(ntok // HT):
            o = th * HT
            for gv in range(2):
                gi = gv * 64 + jg
                P.add("act", lambda e, gv=gv, gi=gi, o=o: e.activation(out=tA[gv][:], in_=u[:, gv, 1 + o:1 + o + HT], func=AF.Identity,
                                                                       bias=cwp[:, 3, gi:gi + 1], scale=cwp[:, 1, gi:gi + 1]),
                      [("u", gv), "cwp"], [("tA", gv)])
                P.add("dve", lambda e, gv=gv, gi=gi, o=o: e.scalar_tensor_tensor(out=tA[gv][:], in0=u[:, gv, o:o + HT], scalar=cwp[:, 0, gi:gi + 1],
                                                                                 in1=tA[gv][:], op0=ALU.mult, op1=ALU.add),
                      [("u", gv), "cwp", ("tA", gv)], [("tA", gv)])
                P.add("dve", lambda e, gv=gv, gi=gi, o=o: e.scalar_tensor_tensor(out=tA[gv][:], in0=u[:, gv, 2 + o:2 + o + HT], scalar=cwp[:, 2, gi:gi + 1],
                                                                                  in1=tA[gv][:], op0=ALU.mult, op1=ALU.add),
                      [("u", gv), "cwp", ("tA", gv)], [("tA", gv)])
            P.add("act", lambda e: e.activation(out=sg[:], in_=tA[0][:], func=AF.Silu), [("tA", 0)], ["sg"])
            P.add("dve", lambda e, b=b, o=o: e.tensor_tensor(out=mo[b][:, o:o + HT], in0=sg[:], in1=tA[1][:], op=ALU.mult),
                  ["sg", ("tA", 1)], [("mo", b)])
        P.dma("sp", mv[:, jg, :], mo[b][:], [("mo", b)], [("mT", jg)])
    S.finish()


def stage_ffn_down(nc, mT, w_down, xT_in, xT_out, par, ig, ntok=TL, TT=512):
    S = Stage(nc, "dn")
    P = S.P
    NK = DFF // 128
    G = S.sb([128, 32])
    mt = [S.sb([128, NK, TT], BF16) for _ in range(2)]
    Wg = [S.sb([128, NK, 128], BF16) for _ in range(2)]
    xg = [S.sb([128, TT]) for _ in range(3)]
    ps = [S.ps() for _ in range(4)]
    P.dma("sp", G[:], par[:, ig, :], [], ["G"])
    mv = mT.rearrange("(c p) t -> p c t", p=128)
    wv = w_down.rearrange("(c p) n -> p c n", p=128)
    k = 0
    for tt in range(ntok // TT):
        tb = tt % 2
        ts = slice(tt * TT, (tt + 1) * TT)
        for q4 in range(4):
            P.dma("sp", mt[tb][:, q4 * 16:(q4 + 1) * 16, :], mv[:, q4 * 16:(q4 + 1) * 16, ts], [("mT", g) for g in range(q4 * 16, (q4 + 1) * 16)],
                  [("mt", tb, q4)])
        for cg in range(D // 128):
            b = k % 2
            r = k % 4
            x3 = k % 3
            k += 1
            P.dma("pool", Wg[b][:], wv[:, :, cg * 128:(cg + 1) * 128], [], [("Wg", b)])
            P.dma("sp", xg[x3][:], xT_in[cg * 128:(cg + 1) * 128, ts], [], [("xg", x3)])
            for c in range(NK):
                P.add("pe", lambda e, b=b, c=c, r=r, tb=tb: e.matmul(ps[r][:, 0:TT], lhsT=Wg[b][:, c, :], rhs=mt[tb][:, c, :],
                                                                     start=(c == 0), stop=(c == NK - 1)),
                      [("Wg", b), ("mt", tb, c // 16)], [("ps", r)])
            P.add("dve", lambda e, r=r, x3=x3, cg=cg: e.scalar_tensor_tensor(out=xg[x3][:], in0=ps[r][:, 0:TT], scalar=G[:, cg:cg + 1],
                                                                             in1=xg[x3][:], op0=ALU.mult, op1=ALU.add),
                  [("ps", r), "G", ("xg", x3)], [("xg", x3)])
            P.dma("sp", xT_out[cg * 128:(cg + 1) * 128, ts], xg[x3][:], [("xg", x3)], [("xTo", cg, tt)])
    S.finish()


_PROGS = {}


def _dt(nc, n, s, d=F32, k="ExternalInput"):
    return nc.dram_tensor(n, list(s), d, kind=k).ap()


def prog_L0():
    if "L0" in _PROGS:
        return _PROGS["L0"]
    nc = bass.Bass("TRN2", target_bir_lowering=False)
    x = _dt(nc, "x", [TL, D]); vecs = _dt(nc, "vecs", [1280, 128]); w_ada = _dt(nc, "w_ada", [D, 6 * D])
    lbl = _dt(nc, "lbl", [128, 128]); idf = _dt(nc, "idf", [128, 128])
    par = _dt(nc, "par", [128, DEPTH * 6 + 1, 32], F32, "ExternalOutput")
    lbo = _dt(nc, "lbo", [128, 2, 2, DEPTH, 16], F32, "ExternalOutput")
    xT = _dt(nc, "xT", [D, TL], F32, "ExternalOutput")
    hT = _dt(nc, "hT", [D, TL], BF16, "ExternalOutput")
    stage_params(nc, vecs, w_ada, lbl, idf, par, lbo)
    stage_pre(nc, x, xT, idf)
    stage_norm(nc, xT, par, 0, 1, hT)
    _PROGS["L0"] = nc
    return nc


def prog_LB():
    if "LB" in _PROGS:
        return _PROGS["LB"]
    nc = bass.Bass("TRN2", target_bir_lowering=False)
    hT = _dt(nc, "hT", [D, T], BF16); w = _dt(nc, "w", [D, 2048]); acst = _dt(nc, "acst", [128, 772]); lam = _dt(nc, "lam", [128, 4])
    lamc = _dt(nc, "lamc", [128, 2]); gn = _dt(nc, "gn", [128, 256]); idb = _dt(nc, "idb", [128, 128], BF16)
    lbo = _dt(nc, "lbo", [128, 2, 2, DEPTH, 16]); hc = _dt(nc, "hc", [128, 512 + 64]); gnh = _dt(nc, "gnh", [128, 1])
    projT = _dt(nc, "projT", [2048, T], BF16, "Internal")
    oscr = _dt(nc, "oscr", [256, T], F32, "Internal")
    vtok = _dt(nc, "vtok", [T, 256], BF16, "Internal")
    oT = _dt(nc, "oT", [512, T], BF16, "ExternalOutput")
    stage_inproj(nc, hT, w, projT, vtok)
    stage_attn(nc, projT, vtok, acst, lam, gn, idb, oT[256:512, :], lamc)
    stage_hgrn(nc, projT, lbo, 0, hc, gnh, idb, oscr, oT[0:256, :])
    _PROGS["LB"] = nc
    return nc


def prog_LC1():
    if "LC1" in _PROGS:
        return _PROGS["LC1"]
    nc = bass.Bass("TRN2", target_bir_lowering=False)
    oT = _dt(nc, "oT", [D, TL], BF16); w_out = _dt(nc, "w_out", [D, D]); xT = _dt(nc, "xT", [D, TL]); par = _dt(nc, "par", [128, 8, 32])
    xTo = _dt(nc, "xTo", [D, TL], F32, "ExternalOutput")
    h2T = _dt(nc, "h2T", [D, TL], BF16, "ExternalOutput")
    stage_outproj(nc, oT, w_out, xT, xTo, par, 2)
    stage_norm(nc, xTo, par, 3, 4, h2T)
    _PROGS["LC1"] = nc
    return nc


def prog_LC2(final):
    key = "LC2f" if final else "LC2"
    if key in _PROGS:
        return _PROGS[key]
    nc = bass.Bass("TRN2", target_bir_lowering=False)
    h2 = _dt(nc, "h2", [D, TL + 2], BF16); w_up = _dt(nc, "w_up", [D, 2 * DFF]); cwp = _dt(nc, "cwp", [128, 4, 128])
    w_down = _dt(nc, "w_down", [DFF, D]); xT = _dt(nc, "xT", [D, TL]); par = _dt(nc, "par", [128, 8, 32])
    mT = _dt(nc, "mT", [DFF, TL], BF16, "Internal")
    xTo = _dt(nc, "xTo", [D, TL], F32, "ExternalOutput")
    stage_ffn_up(nc, h2, w_up, cwp, mT)
    stage_ffn_down(nc, mT, w_down, xT, xTo, par, 5)
    if final:
        idf = _dt(nc, "idf", [128, 128])
        y = _dt(nc, "y", [TL, D], F32, "ExternalOutput")
        stage_norm(nc, xTo, par, 6, None, y, idf_d=idf, final=True)
    else:
        hT = _dt(nc, "hT", [D, TL], BF16, "ExternalOutput")
        stage_norm(nc, xTo, par, 6, 7, hT)
    _PROGS[key] = nc
    return nc


def _run(nc, maps):
    return run_bass_kernel_spmd(nc, maps, core_ids=list(range(NCORE))).results


def kernel(x, c, w_ada, b_ada, ada_table, norm1_g, w_in, hg_lb_logits, hg_norm_g, da_lambda,
           da_norm_g, w_out, norm2_g, w_up, conv_w, conv_b, w_down, final_g):
    f32 = np.float32
    A = lambda a: np.ascontiguousarray(np.asarray(a))
    x = A(x); w_ada = A(w_ada); w_in = np.asarray(w_in); w_out = np.asarray(w_out); w_up = np.asarray(w_up); w_down = np.asarray(w_down)
    x2 = x.reshape(T, D)
    vecs = np.concatenate([np.asarray(c).reshape(1, D), np.asarray(b_ada).reshape(6, D), np.asarray(ada_table).reshape(24, D),
                           np.asarray(norm1_g), np.asarray(norm2_g), np.asarray(final_g).reshape(1, D)], 0).astype(f32).reshape(1280, 128)
    idf = np.eye(128, dtype=f32)
    idb = np.eye(128).astype(ml_dtypes.bfloat16)
    lbl4 = np.asarray(hg_lb_logits).reshape(2, DEPTH, 16, 128)
    hc = hgrn_consts()
    maps = []
    for j in range(NCORE):
        order = [2 * j, 2 * j + 1] + [h for h in range(16) if h not in (2 * j, 2 * j + 1)]
        maps.append({"x": A(x2[j * TL:(j + 1) * TL]), "vecs": vecs, "w_ada": w_ada,
                     "lbl": A(lbl4[:, :, order, :].reshape(128, 128)), "idf": idf})
    r0 = _run(prog_L0(), maps)
    par = r0[0]["par"]
    lbo = [r0[j]["lbo"] for j in range(NCORE)]
    xT = [r0[j]["xT"] for j in range(NCORE)]
    hT = [r0[j]["hT"] for j in range(NCORE)]
    del r0
    y = None
    for l in range(DEPTH):
        lam_init = 0.8 - 0.6 * math.exp(-0.3 * l)
        hT_all = np.concatenate(hT, axis=1)
        lamc = np.tile(np.array([[1.0 - lam_init, -lam_init]], f32), (128, 1))
        maps = []
        for j in range(NCORE):
            cols = np.concatenate([np.arange(g * 2048 + j * 256, g * 2048 + (j + 1) * 256) for g in range(5)] +
                                  [np.arange(10240 + g * 2048 + j * 256, 10240 + g * 2048 + (j + 1) * 256) for g in range(3)])
            lbo_l = np.repeat(lbo[j][:, :, :, l:l + 1, :], DEPTH, axis=3)
            maps.append({"hT": hT_all, "w": A(w_in[l][:, cols]), "acst": attn_consts(j), "lam": A(np.asarray(da_lambda)[l].T.astype(f32)),
                         "lamc": lamc, "gn": A(np.broadcast_to(np.asarray(da_norm_g)[l], (128, 256)).astype(f32)), "idb": idb,
                         "lbo": A(lbo_l), "hc": hc, "gnh": A(np.asarray(hg_norm_g)[l].reshape(128, 1).astype(f32))})
        rb = _run(prog_LB(), maps)
        oT_all = np.empty((D, T), dtype=ml_dtypes.bfloat16)
        for j in range(NCORE):
            oT_all[j * 256:(j + 1) * 256] = rb[j]["oT"][0:256]
            oT_all[2048 + j * 256:2048 + (j + 1) * 256] = rb[j]["oT"][256:512]
        del rb, hT_all
        nxt = par[:, 6 * (l + 1):6 * (l + 1) + 2, :] if l + 1 < DEPTH else np.repeat(par[:, 24:25, :], 2, axis=1)
        par_l = A(np.concatenate([par[:, 6 * l:6 * l + 6, :], nxt], axis=1))
        maps = [{"oT": A(oT_all[:, j * TL:(j + 1) * TL]), "w_out": A(w_out[l]), "xT": xT[j], "par": par_l} for j in range(NCORE)]
        rc = _run(prog_LC1(), maps)
        xT = [rc[j]["xTo"] for j in range(NCORE)]
        h2 = [rc[j]["h2T"] for j in range(NCORE)]
        del rc, oT_all
        zcol = np.zeros((D, 1), dtype=ml_dtypes.bfloat16)
        cwp = A(np.concatenate([np.asarray(conv_w)[l], np.asarray(conv_b)[l][None]], 0).astype(f32).reshape(4, 128, 128).transpose(2, 0, 1))
        maps = []
        for j in range(NCORE):
            left = h2[j - 1][:, -1:] if j > 0 else zcol
            right = h2[j + 1][:, :1] if j + 1 < NCORE else zcol
            m = {"h2": A(np.concatenate([left, h2[j], right], axis=1)), "w_up": A(w_up[l]), "cwp": cwp, "w_down": A(w_down[l]),
                 "xT": xT[j], "par": par_l}
            if l == DEPTH - 1:
                m["idf"] = idf
            maps.append(m)
        rd = _run(prog_LC2(l == DEPTH - 1), maps)
        xT = [rd[j]["xTo"] for j in range(NCORE)]
        if l == DEPTH - 1:
            y = np.concatenate([rd[j]["y"] for j in range(NCORE)], axis=0)
        else:
            hT = [rd[j]["hT"] for j in range(NCORE)]
        del rd
    return y.reshape(1, T, D).astype(f32)
```

```python
import contextlib
import math
import numpy as np
import ml_dtypes
import concourse.bass as bass
import concourse.mybir as mybir
from concourse.bass_utils import run_bass_kernel_spmd

F32 = mybir.dt.float32
BF16 = mybir.dt.bfloat16
AF = mybir.ActivationFunctionType
ALU = mybir.AluOpType
AX = mybir.AxisListType

D = 4096
KC = 32
T = 16384
NCORE = 8
TL = T // NCORE
DEPTH = 4
DFF = 8192
EPS = 1e-6
NVEC = 40
HCH = 32
DBG_PREP = [0]
VQ = ['sp']


class Op:
    __slots__ = ("q", "fn", "reads", "writes", "dma", "sem", "deps", "signal", "count", "idx", "prev")


class Prog:
    QUEUES = ("pe", "dve", "act", "pool", "sp")
    RING = 8
    _scnt = [0]

    def __init__(self, nc, strict=True):
        self.nc = nc
        self.ops = []
        self.strict = strict

    def add(self, q, fn, reads=(), writes=(), dma=False):
        o = Op()
        o.q, o.fn, o.reads, o.writes, o.dma = q, fn, tuple(reads), tuple(writes), dma
        o.sem = None
        o.deps, o.signal, o.count, o.idx, o.prev = [], False, 0, len(self.ops), 0
        self.ops.append(o)
        return o

    def dma(self, q, out, in_, reads, writes, **kw):
        return self.add(q, lambda e: e.dma_start(out=out, in_=in_, **kw), reads, writes, dma=True)

    def emit(self):
        nc = self.nc
        ops = self.ops
        last_w = {}
        readers = {}
        for o in ops:
            deps = set()
            for k in o.reads:
                if k in last_w:
                    deps.add(last_w[k])
            for k in o.writes:
                if k in last_w:
                    deps.add(last_w[k])
                for r in readers.get(k, ()):
                    deps.add(r)
            deps.discard(o.idx)
            o.deps = sorted(deps)
            for k in o.writes:
                last_w[k] = o.idx
                readers[k] = []
            for k in o.reads:
                readers.setdefault(k, []).append(o.idx)
        for o in ops:
            if o.dma:
                o.signal = True
            for d in o.deps:
                a = ops[d]
                if a.dma or a.q != o.q or (self.strict and a.q != "pe"):
                    a.signal = True
        counts = {}
        nd = {}
        for o in ops:
            if not o.signal:
                continue
            if o.dma:
                k = nd.get(o.q, 0)
                nd[o.q] = k + 1
                name = ("dq", o.q, k % self.RING)
            else:
                name = ("eng", o.q)
            o.sem = name
            o.prev = counts.get(name, 0)
            counts[name] = o.prev + (16 if o.dma else 1)
            o.count = counts[name]
        with contextlib.ExitStack() as st:
            sems = {}
            for i, name in enumerate(counts):
                Prog._scnt[0] += 1
                sems[name] = st.enter_context(nc.semaphore("s%d" % Prog._scnt[0]))
            block = st.enter_context(nc.Block())
            engs = {"pe": block.tensor, "dve": block.vector, "act": block.scalar,
                    "pool": block.gpsimd, "sp": block.sync}
            for q in self.QUEUES:
                qops = [o for o in ops if o.q == q]
                if not qops and q != "sp":
                    continue

                def body(e, qops=qops, q=q):
                    seen = {}

                    def wait(name, v):
                        if v > 0 and seen.get(name, 0) < v:
                            e.wait_ge(sems[name], v)
                            seen[name] = v
                    for o in qops:
                        need = {}
                        for d in o.deps:
                            a = ops[d]
                            if not a.signal:
                                continue
                            if (not a.dma) and a.q == q and (q == "pe" or not self.strict):
                                continue
                            if need.get(a.sem, 0) < a.count:
                                need[a.sem] = a.count
                        for nm, v in need.items():
                            wait(nm, v)
                        if o.dma:
                            wait(o.sem, o.prev)
                        ins = o.fn(e)
                        if o.signal:
                            ins.then_inc(sems[o.sem], 16 if o.dma else 1)
                    if q == "sp":
                        for name, c in counts.items():
                            if name[0] == "dq":
                                wait(name, c)
                engs[q](body)


class Stage:
    _cnt = [0]

    def __init__(self, nc, name):
        Stage._cnt[0] += 1
        self.nc, self.name = nc, "%s%d" % (name, Stage._cnt[0])
        self.st = contextlib.ExitStack()
        self.P = Prog(nc)
        self.n = 0

    def sb(self, shape, dt=F32, name=None):
        self.n += 1
        return self.st.enter_context(self.nc.sbuf_tensor("%s_%s%d" % (self.name, name or "t", self.n), list(shape), dt))

    def ps(self, shape=(128, 512), dt=F32, name=None):
        self.n += 1
        return self.st.enter_context(self.nc.psum_tensor("%s_%s%d" % (self.name, name or "p", self.n), list(shape), dt))

    def finish(self):
        self.P.emit()
        self.st.close()


def stage_params(nc, vecs, w_ada, lbl, idf_d, par, lbo):
    S = Stage(nc, "par")
    P = S.P
    V = S.sb([128, 10, 128])
    VT = S.sb([128, NVEC, 32])
    idf = S.sb([128, 128])
    sc = S.sb([128, 32])
    modT = S.sb([128, 6, 32])
    mm = S.sb([128, 6, 32])
    outp = S.sb([128, DEPTH * 6 + 1, 32])
    wsl = [S.sb([128, KC, 512]) for _ in range(2)]
    LR = S.sb([128, 128])
    LT = S.sb([128, 2, DEPTH, 16])
    EX = S.sb([128, 2, DEPTH, 16])
    mx = S.sb([128, 2, 16])
    sm = S.sb([128, 2, 16])
    lbs = S.sb([128, 2, 2, DEPTH, 16])
    pst = [S.ps() for _ in range(2)]
    psm = S.ps()

    P.dma("sp", idf[:], idf_d, [], ["idf"])
    P.dma("sp", V[:], vecs.rearrange("(i r) p -> r i p", r=128), [], ["V"])
    P.dma("sp", LR[:], lbl, [], ["LR"])
    for i in range(10):
        b = i % 2
        P.add("pe", lambda e, i=i, b=b: e.transpose(out=pst[b][:, 0:128], in_=V[:, i, :], identity=idf[:]),
              ["V", "idf"], [("pst", b)])
        P.add("dve", lambda e, i=i, b=b: e.tensor_copy(
            out=VT[:, 4 * i:4 * i + 4, :], in_=pst[b][:, 0:128].rearrange("p (v c) -> p v c", c=32)),
            [("pst", b)], ["VT"])
    P.add("act", lambda e: e.activation(out=sc[:], in_=VT[:, 0, :], func=AF.Silu), ["VT"], ["sc"])
    wv = w_ada.rearrange("(c p) n -> p c n", p=128)
    NSL = (6 * D) // 512
    for s in range(NSL):
        b = s % 2
        P.dma("sp", wsl[b][:], wv[:, :, s * 512:(s + 1) * 512], [], [("wsl", b)])
        for g in range(4):
            col = s * 4 + g
            for c in range(KC):
                P.add("pe", lambda e, b=b, g=g, c=c, col=col: e.matmul(
                    psm[:, col:col + 1], lhsT=wsl[b][:, c, g * 128:(g + 1) * 128], rhs=sc[:, c:c + 1],
                    start=(c == 0), stop=(c == KC - 1)), [("wsl", b), "sc"], ["psm"])
    P.add("dve", lambda e: e.tensor_tensor(out=modT[:], in0=psm[:, 0:192].rearrange("p (j c) -> p j c", c=32),
                                           in1=VT[:, 1:7, :], op=ALU.add), ["psm", "VT"], ["modT"])
    for l in range(DEPTH):
        P.add("dve", lambda e, l=l: e.tensor_tensor(out=mm[:], in0=modT[:], in1=VT[:, 7 + 6 * l:13 + 6 * l, :], op=ALU.add),
              ["modT", "VT", "outp"], ["mm"])
        P.add("dve", lambda e, l=l: e.scalar_tensor_tensor(out=outp[:, 6 * l + 0, :], in0=mm[:, 1, :], scalar=1.0,
                                                           in1=VT[:, 31 + l, :], op0=ALU.add, op1=ALU.mult), ["mm", "VT"], ["outp"])
        P.add("dve", lambda e, l=l: e.tensor_copy(out=outp[:, 6 * l + 1, :], in_=mm[:, 0, :]), ["mm"], ["outp"])
        P.add("dve", lambda e, l=l: e.tensor_copy(out=outp[:, 6 * l + 2, :], in_=mm[:, 2, :]), ["mm"], ["outp"])
        P.add("dve", lambda e, l=l: e.scalar_tensor_tensor(out=outp[:, 6 * l + 3, :], in0=mm[:, 4, :], scalar=1.0,
                                                           in1=VT[:, 35 + l, :], op0=ALU.add, op1=ALU.mult), ["mm", "VT"], ["outp"])
        P.add("dve", lambda e, l=l: e.tensor_copy(out=outp[:, 6 * l + 4, :], in_=mm[:, 3, :]), ["mm"], ["outp"])
        P.add("dve", lambda e, l=l: e.tensor_copy(out=outp[:, 6 * l + 5, :], in_=mm[:, 5, :]), ["mm"], ["outp"])
    P.add("dve", lambda e: e.tensor_copy(out=outp[:, 6 * DEPTH, :], in_=VT[:, 39, :]), ["VT"], ["outp"])
    P.dma("sp", par, outp[:], ["outp"], ["par"])
    P.add("pe", lambda e: e.transpose(out=pst[0][:, 0:128], in_=LR[:], identity=idf[:]), ["LR", "idf", ("pst", 0)], [("pst", 0)])
    P.add("dve", lambda e: e.tensor_copy(out=LT[:], in_=pst[0][:, 0:128].rearrange("p (d l h) -> p d l h", d=2, l=DEPTH)),
          [("pst", 0)], ["LT"])
    P.add("dve", lambda e: e.tensor_tensor(out=mx[:], in0=LT[:, :, 0, :], in1=LT[:, :, 1, :], op=ALU.max), ["LT"], ["mx"])
    for l in (2, 3):
        P.add("dve", lambda e, l=l: e.tensor_tensor(out=mx[:], in0=mx[:], in1=LT[:, :, l, :], op=ALU.max), ["LT", "mx"], ["mx"])
    for l in range(DEPTH):
        P.add("dve", lambda e, l=l: e.tensor_tensor(out=EX[:, :, l, :], in0=LT[:, :, l, :], in1=mx[:], op=ALU.subtract),
              ["LT", "mx"], ["EX"])
    P.add("act", lambda e: e.activation(out=EX[:], in_=EX[:], func=AF.Exp), ["EX"], ["EX"])
    P.add("dve", lambda e: e.tensor_tensor(out=sm[:], in0=EX[:, :, 0, :], in1=EX[:, :, 1, :], op=ALU.add), ["EX"], ["sm"])
    for l in (2, 3):
        P.add("dve", lambda e, l=l: e.tensor_tensor(out=sm[:], in0=sm[:], in1=EX[:, :, l, :], op=ALU.add), ["EX", "sm"], ["sm"])
    P.add("dve", lambda e: e.reciprocal(out=sm[:], in_=sm[:]), ["sm"], ["sm"])
    for l in range(DEPTH):
        P.add("dve", lambda e, l=l: e.tensor_tensor(out=EX[:, :, l, :], in0=EX[:, :, l, :], in1=sm[:], op=ALU.mult),
              ["EX", "sm"], ["EX"])
    P.add("dve", lambda e: e.memset(lbs[:, 0, :, 0, :], 0.0), [], ["lbs"])
    P.add("dve", lambda e: e.tensor_copy(out=lbs[:, 0, :, 1, :], in_=EX[:, :, 1, :]), ["EX", "lbs"], ["lbs"])
    for l in (2, 3):
        P.add("dve", lambda e, l=l: e.tensor_tensor(out=lbs[:, 0, :, l, :], in0=lbs[:, 0, :, l - 1, :], in1=EX[:, :, l, :], op=ALU.add),
              ["EX", "lbs"], ["lbs"])
    P.add("dve", lambda e: e.tensor_scalar(out=lbs[:, 1], in0=lbs[:, 0], scalar1=-1.0, scalar2=1.0, op0=ALU.mult, op1=ALU.add),
          ["lbs"], ["lbs"])
    P.dma("sp", lbo, lbs[:], ["lbs"], ["lbo"])
    S.finish()


def stage_pre(nc, x, xT, idf_d, ntok=TL):
    S = Stage(nc, "pre")
    P = S.P
    idf = S.sb([128, 128])
    xin = [S.sb([128, D]) for _ in range(2)]
    xo = [S.sb([128, 4, 128]) for _ in range(4)]
    pst = [S.ps() for _ in range(4)]
    P.dma("sp", idf[:], idf_d, [], ["idf"])
    xTv = xT.rearrange("(c p) t -> p c t", p=128)
    k = 0
    for tb in range(ntok // 128):
        b = tb % 2
        P.dma("sp", xin[b][:], x[tb * 128:(tb + 1) * 128, :], [], [("xin", b)])
        for cg in range(KC // 4):
            r = k % 4
            k += 1
            for j in range(4):
                c = cg * 4 + j
                P.add("pe", lambda e, b=b, c=c, j=j, r=r: e.transpose(out=pst[r][:, j * 128:(j + 1) * 128],
                                                                      in_=xin[b][:, c * 128:(c + 1) * 128], identity=idf[:]),
                      [("xin", b), "idf"], [("pst", r)])
            eng = "act" if (k % 2) else "dve"
            if eng == "act":
                P.add("act", lambda e, r=r: e.copy(out=xo[r][:], in_=pst[r][:].rearrange("p (j t) -> p j t", j=4)),
                      [("pst", r)], [("xo", r)])
            else:
                P.add("dve", lambda e, r=r: e.tensor_copy(out=xo[r][:], in_=pst[r][:].rearrange("p (j t) -> p j t", j=4)),
                      [("pst", r)], [("xo", r)])
            P.dma("pool", xTv[:, cg * 4:cg * 4 + 4, tb * 128:(tb + 1) * 128], xo[r][:], [("xo", r)], [("xT", tb, cg)])
    S.finish()


def stage_norm(nc, xT, par, ia, ib, out, idf_d=None, final=False, ntok=TL, TT=512, out_off=0):
    S = Stage(nc, "nrm")
    P = S.P
    A = S.sb([128, 32])
    B = S.sb([128, 32])
    ones = S.sb([128, 128])
    xt = [S.sb([128, KC, TT]) for _ in range(2)]
    sq = [S.sb([128, TT]) for _ in range(3)]
    rstd = S.sb([128, TT])
    tmp = [S.sb([128, TT]) for _ in range(3)]
    pss = S.ps()
    P.dma("sp", A[:], par[:, ia, :], [], ["A"])
    if not final:
        P.dma("sp", B[:], par[:, ib, :], [], ["B"])
        ho = [S.sb([128, KC, TT], BF16) for _ in range(2)]
        ov = out.rearrange("(c p) t -> p c t", p=128)
    else:
        idf = S.sb([128, 128])
        P.dma("sp", idf[:], idf_d, [], ["idf"])
        yo = [S.sb([128, TT]) for _ in range(3)]
        pst = [S.ps() for _ in range(3)]
        yrow = [S.sb([128, D]) for _ in range(2)]
    P.add("dve", lambda e: e.memset(ones[:], 1.0 / D), [], ["ones"])
    epst = S.sb([128, 1])
    P.add("dve", lambda e: e.memset(epst[:], EPS), [], ["eps"])
    xv = xT.rearrange("(c p) t -> p c t", p=128)
    nt = ntok // TT
    for tt in range(nt):
        b = tt % 2
        P.dma("sp", xt[b][:], xv[:, :, tt * TT:(tt + 1) * TT], [], [("xt", b)])
        for c in range(KC):
            r = c % 3
            P.add("act", lambda e, b=b, c=c, r=r: e.activation(out=sq[r][:], in_=xt[b][:, c, :], func=AF.Square),
                  [("xt", b)], [("sq", r)])
            P.add("pe", lambda e, c=c, r=r: e.matmul(pss[:, 0:TT], lhsT=ones[:], rhs=sq[r][:], start=(c == 0), stop=(c == KC - 1)),
                  [("sq", r), "ones"], ["pss"])
        P.add("act", lambda e: e.activation(out=rstd[:], in_=pss[:, 0:TT], func=AF.Sqrt, bias=epst[:, 0:1], scale=1.0),
              ["pss", "eps"], ["rstd"])
        P.add("dve", lambda e: e.reciprocal(out=rstd[:], in_=rstd[:]), ["rstd"], ["rstd"])
        if not final:
            for c in range(KC):
                r = c % 3
                P.add("dve", lambda e, b=b, c=c, r=r: e.scalar_tensor_tensor(out=tmp[r][:], in0=xt[b][:, c, :], scalar=A[:, c:c + 1],
                                                                             in1=rstd[:], op0=ALU.mult, op1=ALU.mult),
                      [("xt", b), "A", "rstd"], [("tmp", r)])
                P.add("act", lambda e, b=b, c=c, r=r: e.activation(out=ho[b][:, c, :], in_=tmp[r][:], func=AF.Identity,
                                                                   bias=B[:, c:c + 1], scale=1.0),
                      [("tmp", r), "B"], [("ho", b)])
            P.dma("pool", ov[:, :, out_off + tt * TT:out_off + (tt + 1) * TT], ho[b][:], [("ho", b)], [("out", tt)])
        else:
            for c in range(KC):
                r = c % 3
                P.add("dve", lambda e, b=b, c=c, r=r: e.scalar_tensor_tensor(out=yo[r][:], in0=xt[b][:, c, :], scalar=A[:, c:c + 1],
                                                                             in1=rstd[:], op0=ALU.mult, op1=ALU.mult),
                      [("xt", b), "A", "rstd"], [("yo", r)])
                for tb in range(TT // 128):
                    P.add("pe", lambda e, r=r, tb=tb: e.transpose(out=pst[r][:, tb * 128:(tb + 1) * 128],
                                                                  in_=yo[r][:, tb * 128:(tb + 1) * 128], identity=idf[:]),
                          [("yo", r), "idf"], [("pst", r)])
                P.add("act", lambda e, r=r: e.copy(out=tmp[r][:], in_=pst[r][:, 0:TT]), [("pst", r)], [("tmp", r)])
                for tb in range(TT // 128):
                    t0 = tt * TT + tb * 128
                    P.dma("pool", out[t0:t0 + 128, c * 128:(c + 1) * 128], tmp[r][:, tb * 128:(tb + 1) * 128],
                          [("tmp", r)], [("out", tt, c, tb)])
    S.finish()


def stage_inproj(nc, hT_all, w, projT, vtok, ntok=T, TT=512):
    S = Stage(nc, "inp")
    P = S.P
    NCOL = 2048
    W = S.sb([128, KC, NCOL], BF16)
    ht = [S.sb([128, KC, TT], BF16) for _ in range(2)]
    og = [S.sb([128, 4, TT], BF16) for _ in range(2)]
    vo = [S.sb([128, TT // 128, 256], BF16) for _ in range(2)]
    ps = [S.ps() for _ in range(4)]
    vtv = vtok.rearrange("(n p) d -> p n d", p=128)
    wv = w.rearrange("(c p) n -> p c n", p=128)
    for blk in range(8):
        P.dma("pool", W[:, :, blk * 256:(blk + 1) * 256], wv[:, :, blk * 256:(blk + 1) * 256], [], [("W", blk)])
    hv = hT_all.rearrange("(c p) t -> p c t", p=128)
    pv = projT.rearrange("(g p) t -> p g t", p=128)
    qscale = 128.0 ** -0.5
    k = 0
    for tt in range(ntok // TT):
        b = tt % 2
        P.dma("sp", ht[b][:], hv[:, :, tt * TT:(tt + 1) * TT], [], [("ht", b)])
        for cg in range(14):
            r = k % 4
            k += 1
            for c in range(KC):
                P.add("pe", lambda e, b=b, c=c, cg=cg, r=r: e.matmul(ps[r][:, 0:TT], lhsT=W[:, c, cg * 128:(cg + 1) * 128],
                                                                     rhs=ht[b][:, c, :], start=(c == 0), stop=(c == KC - 1)),
                      [("W", cg // 2), ("ht", b)], [("ps", r)])
            ob = (k // 4) % 2 if False else ((tt * 4 + cg // 4) % 2)
            sc = qscale if cg in (0, 1, 10, 11) else 1.0
            if k % 2:
                P.add("act", lambda e, r=r, ob=ob, cg=cg, sc=sc: e.activation(out=og[ob][:, cg % 4, :], in_=ps[r][:, 0:TT],
                                                                             func=AF.Copy, scale=sc),
                      [("ps", r)], [("og", ob)])
            else:
                P.add("dve", lambda e, r=r, ob=ob, cg=cg, sc=sc: e.tensor_scalar(out=og[ob][:, cg % 4, :], in0=ps[r][:, 0:TT],
                                                                                scalar1=sc, scalar2=None, op0=ALU.mult),
                      [("ps", r)], [("og", ob)])
            if cg % 4 == 3 or cg == 13:
                g0 = (cg // 4) * 4
                ng = cg - g0 + 1
                P.dma("pool", pv[:, g0:g0 + ng, tt * TT:(tt + 1) * TT], og[ob][:, 0:ng, :], [("og", ob)], [("proj", tt, g0)])
        for tb in range(TT // 128):
            r = k % 4
            k += 1
            for c in range(KC):
                P.add("pe", lambda e, b=b, c=c, tb=tb, r=r: e.matmul(ps[r][:, 0:256], lhsT=ht[b][:, c, tb * 128:(tb + 1) * 128],
                                                                     rhs=W[:, c, 1792:2048], start=(c == 0), stop=(c == KC - 1)),
                      [("W", 7), ("ht", b)], [("ps", r)])
            P.add("dve", lambda e, r=r, b=b, tb=tb: e.tensor_copy(out=vo[b][:, tb, :], in_=ps[r][:, 0:256]), [("ps", r)], [("vo", b)])
        P.dma("pool", vtv[:, tt * (TT // 128):(tt + 1) * (TT // 128), :], vo[b][:], [("vo", b)], [("vtok", tt)])
    S.finish()


def attn_consts(j):
    m = 2.0 ** (-8.0 * (j + 1) / 8.0)
    p = np.arange(128, dtype=np.float64)[:, None]
    n = np.arange(128, dtype=np.float64)[None, :]
    BL = -m * (128.0 * n - p)
    BR = -m * (p + 128.0 * n + 1.0)
    f = np.arange(256, dtype=np.float64)[None, :]
    Dg = [-m * np.abs(f - p - 128.0 * a) for a in range(2)]
    qb = np.arange(2, dtype=np.float64)[None, :]
    fL = np.exp(-m * (p + 128.0 * qb))
    fR = np.exp(-m * (255.0 - p - 128.0 * qb))
    return np.concatenate([BL, BR, Dg[0], Dg[1], fL, fR], axis=1).astype(np.float32)


def stage_attn(nc, projT, vtok, acst_d, lam_d, gn_d, idb_d, oT_rows, lamc_d, ntok=T, dbg=None):
    S = Stage(nc, "att")
    P = S.P
    NKB = ntok // 128
    NQG = ntok // 256
    kT = S.sb([128, 2, ntok], BF16)
    Vt = S.sb([128, NKB, 258], BF16)
    acst = S.sb([128, 772])
    BLm = S.sb([128, 128])
    BRm = S.sb([128, 128])
    lam4 = S.sb([128, 4])
    gnb = S.sb([128, 256])
    idb = S.sb([128, 128], BF16)
    onesf = S.sb([128, 128])
    qt = [S.sb([128, 2, 256], BF16) for _ in range(3)]
    PT = [S.sb([128, 2, 256], BF16) for _ in range(4)]
    sd = [S.sb([128, 2, 256]) for _ in range(2)]
    Osb = S.sb([128, 4, 257])
    sqc = [S.sb([128, 512], BF16) for _ in range(2)]
    onesb = S.sb([128, 128], BF16)
    lhl = S.sb([128, 4], BF16)
    vin = [S.sb([128, 2, 512], BF16) for _ in range(2)]
    mxs = S.sb([128, 2, 2 * (ntok // 512)])
    sm = S.sb([128, 16])
    res = S.sb([128, 256])
    junk = S.sb([128, 256])
    yb = S.sb([128, 256], BF16)
    oTs = [S.sb([128, 2, 256], BF16) for _ in range(2)]
    Sp = [S.ps() for _ in range(3)]
    Op = [S.ps() for _ in range(4)]
    Tp = S.ps([128, 1024], BF16)

    qv = projT[1280:1536, :].rearrange("(m d) t -> d m t", d=128)
    kv = projT[1536:1792, :].rearrange("(m d) t -> d m t", d=128)
    lamc = S.sb([128, 2])
    P.dma("sp", lamc[:], lamc_d, [], ["lamc"])
    P.dma("sp", acst[:], acst_d, [], ["acst"])
    P.dma("sp", lam4[:], lam_d, [], ["lam4"])
    P.dma("sp", gnb[:], gn_d, [], ["gnb"])
    P.dma("sp", idb[:], idb_d, [], ["idb"])
    P.dma("sp", kT[:], kv, [], ["kT"])
    P.add("dve", lambda e: e.memset(onesf[:], 1.0), [], ["onesf"])
    P.add("dve", lambda e: e.memset(onesb[:], 1.0), [], ["onesf"])
    P.add("dve", lambda e: e.memset(sm[:], 0.0), [], ["sm"])
    P.add("dve", lambda e: e.memset(sm[:, 15:16], EPS), ["sm"], ["sm"])
    P.add("dve", lambda e: e.memset(Vt[:, :, 256:258], 1.0), [], ["Vt1"])
    P.add("dve", lambda e: e.tensor_scalar(out=gnb[:], in0=gnb[:], scalar1=lamc[:, 0:1], scalar2=None, op0=ALU.mult), ["gnb", "lamc"], ["gnb"])
    if DBG_PREP[0] == 1:
        S.finish()
        return
    nch = ntok // 512
    k = 0
    for which, src in ((0, qv), (1, kv)):
        for m in range(2):
            for ch in range(nch):
                b = k % 2
                k += 1
                if which == 0:
                    P.dma("sp", vin[b][:, 0, :], src[:, m, ch * 512:(ch + 1) * 512], [], [("vin", b)])
                    P.add("act", lambda e, b=b: e.activation(out=sqc[b][:], in_=vin[b][:, 0, :], func=AF.Square), [("vin", b)], [("sqc", b)])
                else:
                    P.add("act", lambda e, b=b, m=m, ch=ch: e.activation(out=sqc[b][:], in_=kT[:, m, ch * 512:(ch + 1) * 512], func=AF.Square),
                          ["kT"], [("sqc", b)])
                P.add("pe", lambda e, b=b: e.matmul(Sp[b][:, 0:512], lhsT=onesb[:], rhs=sqc[b][:], start=True, stop=True),
                      [("sqc", b), "onesf"], [("S", b)])
                P.add("dve", lambda e, b=b, which=which, m=m, ch=ch: e.reduce_max(out=mxs[:, which, m * nch + ch:m * nch + ch + 1],
                                                                                 in_=Sp[b][:, 0:512], axis=AX.X),
                      [("S", b)], ["mxs"])
    P.add("dve", lambda e: e.reduce_max(out=sm[:, 0:2], in_=mxs[:], axis=AX.X), ["mxs"], ["sm"])
    P.add("dve", lambda e: e.tensor_tensor(out=sm[:, 2:3], in0=sm[:, 0:1], in1=sm[:, 1:2], op=ALU.mult), ["sm"], ["sm"])
    P.add("act", lambda e: e.activation(out=sm[:, 3:4], in_=sm[:, 2:3], func=AF.Sqrt, scale=1.1), ["sm"], ["sm"])
    P.add("dve", lambda e: e.tensor_scalar(out=sm[:, 4:5], in0=sm[:, 3:4], scalar1=-1.0, scalar2=None, op0=ALU.mult), ["sm"], ["sm"])
    P.add("dve", lambda e: e.tensor_scalar(out=BLm[:], in0=acst[:, 0:128], scalar1=sm[:, 3:4], scalar2=None, op0=ALU.subtract),
          ["acst", "sm"], ["BLm"])
    P.add("dve", lambda e: e.tensor_scalar(out=BRm[:], in0=acst[:, 128:256], scalar1=sm[:, 3:4], scalar2=None, op0=ALU.subtract),
          ["acst", "sm"], ["BRm"])
    P.add("dve", lambda e: e.tensor_tensor(out=sm[:, 5:6], in0=lam4[:, 0:1], in1=lam4[:, 1:2], op=ALU.mult), ["lam4", "sm"], ["sm"])
    P.add("dve", lambda e: e.tensor_tensor(out=sm[:, 6:7], in0=lam4[:, 2:3], in1=lam4[:, 3:4], op=ALU.mult), ["lam4", "sm"], ["sm"])
    P.add("dve", lambda e: e.tensor_copy(out=lhl[:, 0:2], in_=sm[:, 5:7]), ["sm"], ["lhl"])
    P.add("dve", lambda e: e.tensor_tensor(out=sm[:, 9:11], in0=sm[:, 5:7], in1=lhl[:, 0:2], op=ALU.subtract), ["sm", "lhl"], ["sm"])
    P.add("dve", lambda e: e.tensor_copy(out=lhl[:, 2:4], in_=sm[:, 9:11]), ["sm", "lhl"], ["lhl"])
    P.add("pe", lambda e: e.matmul(Sp[2][:, 0:4], lhsT=onesb[:], rhs=lhl[:], start=True, stop=True), ["lhl", "onesf"], [("S", 2)])
    P.add("dve", lambda e: e.tensor_copy(out=sm[:, 9:11], in_=Sp[2][:, 2:4]), [("S", 2), "sm"], ["sm"])
    P.add("dve", lambda e: e.tensor_tensor(out=sm[:, 5:7], in0=Sp[2][:, 0:2], in1=sm[:, 9:11], op=ALU.add), [("S", 2), "sm"], ["sm"])
    P.add("act", lambda e: e.activation(out=sm[:, 9:11], in_=sm[:, 5:7], func=AF.Exp), ["sm"], ["sm"])
    P.add("dve", lambda e: e.tensor_tensor(out=sm[:, 8:9], in0=sm[:, 10:11], in1=sm[:, 9:10], op=ALU.subtract), ["sm"], ["sm"])
    P.add("dve", lambda e: e.tensor_scalar(out=sm[:, 8:9], in0=sm[:, 8:9], scalar1=lamc[:, 1:2], scalar2=None, op0=ALU.add), ["sm", "lamc"], ["sm"])
    if DBG_PREP[0] == 2:
        S.finish()
        return
    vtv = vtok.rearrange("(n p) d -> p n d", p=128)
    VG = max(1, NKB // 4)
    for g4 in range(NKB // VG):
        P.dma("sp", Vt[:, g4 * VG:(g4 + 1) * VG, 0:256], vtv[:, g4 * VG:(g4 + 1) * VG, :], [], [("Vt", g4)])
    steps = []
    for qg in (range(NQG) if dbg is None else dbg):
        sides = []
        if qg > 0:
            sides.append(("L", list(range(0, 2 * qg))))
        sides.append(("C", [2 * qg, 2 * qg + 1]))
        if 2 * qg + 2 < NKB:
            sides.append(("R", list(range(2 * qg + 2, NKB))))
        for si, (side, kbs) in enumerate(sides):
            for ki, kb in enumerate(kbs):
                steps.append((qg, si, side, kb, ki, len(kbs), len(sides)))
    yb4 = [S.sb([128, 256], BF16) for _ in range(4)]
    LA = 2
    deferred = []

    def emit_front(t):
        qg, si, side, kb, ki, nk, ns = steps[t]
        qb_ = qg % 3
        r = t % 3
        r2 = t % 4
        if si == 0 and ki == 0:
            P.dma("sp", qt[qb_][:], qv[:, :, qg * 256:(qg + 1) * 256], [], [("qt", qb_)])
        for m in range(2):
            P.add("pe", lambda e, r=r, m=m, kb=kb, qb_=qb_: e.matmul(Sp[r][:, m * 256:(m + 1) * 256], lhsT=kT[:, m, kb * 128:(kb + 1) * 128],
                                                                   rhs=qt[qb_][:, m, :], start=True, stop=True),
                  ["kT", ("qt", qb_)], [("S", r)])
        if side == "L":
            n = 2 * qg - kb
            P.add("act", lambda e, r=r, r2=r2, n=n: e.activation(out=PT[r2][:].rearrange("p m q -> p (m q)"), in_=Sp[r][:, 0:512], func=AF.Exp,
                                                                 bias=BLm[:, n:n + 1], scale=1.0),
                  [("S", r), "BLm"], [("PT", r2)])
        elif side == "R":
            n = kb - 2 * qg - 2
            P.add("act", lambda e, r=r, r2=r2, n=n: e.activation(out=PT[r2][:].rearrange("p m q -> p (m q)"), in_=Sp[r][:, 0:512], func=AF.Exp,
                                                                 bias=BRm[:, n:n + 1], scale=1.0),
                  [("S", r), "BRm"], [("PT", r2)])
        else:
            a = kb - 2 * qg
            d_ = ki % 2
            for m in range(2):
                P.add("dve", lambda e, r=r, m=m, a=a, d_=d_: e.tensor_tensor(out=sd[d_][:, m, :], in0=Sp[r][:, m * 256:(m + 1) * 256],
                                                                             in1=acst[:, 256 + a * 256:512 + a * 256], op=ALU.add),
                      [("S", r), "acst"], [("sd", d_)])
            P.add("act", lambda e, r2=r2, d_=d_: e.activation(out=PT[r2][:].rearrange("p m q -> p (m q)"), in_=sd[d_][:].rearrange("p m q -> p (m q)"),
                                                              func=AF.Exp, bias=sm[:, 4:5], scale=1.0),
                  [("sd", d_), "sm"], [("PT", r2)])

    def emit_back(t):
        qg, si, side, kb, ki, nk, ns = steps[t]
        r2 = t % 4
        for qb in range(2):
            for m in range(2):
                i = qb * 2 + m
                P.add("pe", lambda e, i=i, r2=r2, m=m, qb=qb, kb=kb, ki=ki, nk=nk: e.matmul(
                    Op[i][:, 0:257], lhsT=PT[r2][:, m, qb * 128:(qb + 1) * 128], rhs=Vt[:, kb, 0:257],
                    start=(ki == 0), stop=(ki == nk - 1)),
                    [("PT", r2), ("Vt", kb // VG), "Vt1"], [("O", i)])
        if ki != nk - 1:
            return
        for i in range(4):
            qb = i // 2
            if si == 0:
                if side == "L":
                    P.add("dve", lambda e, i=i, qb=qb: e.tensor_scalar(out=Osb[:, i, :], in0=Op[i][:, 0:257], scalar1=acst[:, 768 + qb:769 + qb],
                                                                       scalar2=None, op0=ALU.mult), [("O", i), "acst"], ["Osb"])
                else:
                    P.add("dve", lambda e, i=i: e.tensor_copy(out=Osb[:, i, :], in_=Op[i][:, 0:257]), [("O", i)], ["Osb"])
            elif side == "C":
                P.add("dve", lambda e, i=i: e.tensor_tensor(out=Osb[:, i, :], in0=Osb[:, i, :], in1=Op[i][:, 0:257], op=ALU.add),
                      [("O", i), "Osb"], ["Osb"])
            else:
                P.add("dve", lambda e, i=i, qb=qb: e.scalar_tensor_tensor(out=Osb[:, i, :], in0=Op[i][:, 0:257], scalar=acst[:, 770 + qb:771 + qb],
                                                                          in1=Osb[:, i, :], op0=ALU.mult, op1=ALU.add),
                      [("O", i), "Osb", "acst"], ["Osb"])
        if si != ns - 1:
            return
        ob = qg % 2
        for qb in range(2):
            y_ = yb4[ob * 2 + qb]
            yk = ("yb", ob * 2 + qb)
            P.add("dve", lambda e, qb=qb: e.reciprocal(out=sm[:, 11:12], in_=Osb[:, 2 * qb, 256:257]), ["Osb", "sm"], ["sm"])
            P.add("dve", lambda e, qb=qb: e.reciprocal(out=sm[:, 12:13], in_=Osb[:, 2 * qb + 1, 256:257]), ["Osb", "sm"], ["sm"])
            P.add("dve", lambda e: e.tensor_tensor(out=sm[:, 12:13], in0=sm[:, 12:13], in1=sm[:, 8:9], op=ALU.mult), ["sm"], ["sm"])
            P.add("dve", lambda e, qb=qb: e.tensor_scalar(out=res[:], in0=Osb[:, 2 * qb, 0:256], scalar1=sm[:, 11:12], scalar2=None, op0=ALU.mult),
                  ["Osb", "sm"], ["res"])
            P.add("dve", lambda e, qb=qb: e.scalar_tensor_tensor(out=res[:], in0=Osb[:, 2 * qb + 1, 0:256], scalar=sm[:, 12:13], in1=res[:],
                                                                 op0=ALU.mult, op1=ALU.add), ["Osb", "sm", "res"], ["res"])
            P.add("dve", lambda e: e.memset(sm[:, 13:14], 0.0), ["sm"], ["sm"])
            P.add("act", lambda e: e.activation(out=junk[:], in_=res[:], func=AF.Square, accum_out=sm[:, 13:14]), ["res", "sm"], ["junk", "sm"])
            P.add("act", lambda e: e.activation(out=sm[:, 14:15], in_=sm[:, 13:14], func=AF.Sqrt, bias=sm[:, 15:16], scale=1.0 / 256.0),
                  ["sm"], ["sm"])
            P.add("dve", lambda e: e.reciprocal(out=sm[:, 14:15], in_=sm[:, 14:15]), ["sm"], ["sm"])
            P.add("dve", lambda e, y_=y_: e.scalar_tensor_tensor(out=y_[:], in0=res[:], scalar=sm[:, 14:15], in1=gnb[:], op0=ALU.mult, op1=ALU.mult),
                  ["res", "sm", "gnb"], [yk])

        def tail(qg=qg, ob=ob):
            for qb in range(2):
                y_ = yb4[ob * 2 + qb]
                yk = ("yb", ob * 2 + qb)
                tslot = 512 * 0 + qb * 256
                for dc in range(2):
                    P.add("pe", lambda e, dc=dc, y_=y_, tslot=tslot: e.transpose(out=Tp[:, tslot + dc * 128:tslot + (dc + 1) * 128],
                                                                                 in_=y_[:, dc * 128:(dc + 1) * 128], identity=idb[:]),
                          [yk, "idb"], ["Tpbank"])
                P.add("act", lambda e, ob=ob, qb=qb, tslot=tslot: e.copy(out=oTs[ob][:, :, qb * 128:(qb + 1) * 128],
                                                                         in_=Tp[:, tslot:tslot + 256].rearrange("p (c q) -> p c q", c=2)),
                      ["Tpbank"], [("oTs", ob)])
            P.dma("pool", oT_rows.rearrange("(c p) t -> p c t", p=128)[:, :, qg * 256:(qg + 1) * 256], oTs[ob][:], [("oTs", ob)], [("oT", qg)])
        deferred.append((t + 10, tail))

    ns_ = len(steps)
    for t in range(ns_ + LA):
        if t < ns_:
            emit_front(t)
        if t >= LA:
            emit_back(t - LA)
        while deferred and deferred[0][0] <= t:
            deferred.pop(0)[1]()
    while deferred:
        deferred.pop(0)[1]()
    S.finish()


def hgrn_consts(SEG=512):
    cm = np.ones((128, SEG), np.float32)
    cm[:, 0::HCH] = 0.0
    s = np.arange(HCH)[:, None]
    t = np.arange(HCH)[None, :]
    mk = np.zeros((128, 2 * HCH), np.float32)
    mk[0:HCH, 0:HCH] = (s <= t)
    mk[0:HCH, HCH:2 * HCH] = (s >= t)
    return np.concatenate([cm, mk], axis=1)


def stage_hgrn(nc, projT, lbo_d, l, hcst_d, gnh_d, idb_d, o_scr, oT_rows, ntok=T, SEG=512):
    S = Stage(nc, "hg")
    P = S.P
    nseg = ntok // SEG
    nch = SEG // HCH
    hcst = S.sb([128, SEG + 2 * HCH])
    idb = S.sb([128, 128], BF16)
    lbt = S.sb([128, 2, 2, DEPTH, 16])
    noml = S.sb([128, 2, DEPTH, 16])
    gnh = S.sb([128, 1])
    onesn = S.sb([128, 128])
    epst = S.sb([128, 1])
    S32 = [S.sb([128, 128]) for _ in range(2)]
    Sbf = [S.sb([128, 128], BF16) for _ in range(2)]
    qin = [S.sb([128, 2, SEG], BF16) for _ in range(2)]
    zin = [S.sb([128, 2, SEG], BF16) for _ in range(2)]
    vin = [S.sb([128, 2, SEG], BF16) for _ in range(2)]
    gin = [S.sb([128, 2, SEG], BF16) for _ in range(2)]
    ofw = [S.sb([128, 2, SEG]) for _ in range(2)]
    sig = [S.sb([128, SEG]) for _ in range(2)]
    g = [S.sb([128, SEG]) for _ in range(2)]
    kc = [S.sb([128, SEG]) for _ in range(2)]
    bb = [S.sb([128, SEG]) for _ in range(2)]
    arg = [S.sb([128, SEG]) for _ in range(2)]
    cq = [S.sb([128, SEG]) for _ in range(2)]
    ck = [S.sb([128, SEG]) for _ in range(2)]
    ex = [S.sb([128, SEG]) for _ in range(2)]
    Qt = [S.sb([128, SEG], BF16) for _ in range(2)]
    Kt = [S.sb([128, SEG], BF16) for _ in range(2)]
    Kh = [S.sb([128, SEG], BF16) for _ in range(2)]
    ebl = [S.sb([128, nch]) for _ in range(2)]
    Am = [S.sb([HCH, HCH], BF16) for _ in range(4)]
    VK = [S.sb([HCH, 256], BF16) for _ in range(4)]
    osb = [S.sb([128, SEG]) for _ in range(2)]
    ysb = [S.sb([128, SEG]) for _ in range(2)]
    yb = [S.sb([128, SEG], BF16) for _ in range(2)]
    Ap2 = [S.ps() for _ in range(2)]
    Dp2 = [S.ps() for _ in range(2)]
    Tp2 = [S.ps([128, 1024], BF16) for _ in range(2)]
    Np = Dp2[0]
    Oa1 = [S.ps() for _ in range(2)]

    P.dma("sp", hcst[:], hcst_d, [], ["hcst"])
    P.dma("sp", idb[:], idb_d, [], ["idb"])
    P.dma("sp", lbt[:], lbo_d, [], ["lbt"])
    P.dma("sp", gnh[:], gnh_d, [], ["gnh"])
    P.add("dve", lambda e: e.tensor_scalar(out=noml[:], in0=lbt[:, 1], scalar1=-1.0, scalar2=None, op0=ALU.mult), ["lbt"], ["noml"])
    P.add("dve", lambda e: e.memset(onesn[:], 1.0 / 128.0), [], ["onesn"])
    P.add("dve", lambda e: e.memset(epst[:], EPS), [], ["eps"])
    cm = hcst[:, 0:SEG]
    pq = projT[0:256, :].rearrange("(a k) t -> k a t", k=128)
    pv = projT[768:1024, :].rearrange("(a k) t -> k a t", k=128)
    pg = projT[1024:1280, :].rearrange("(a k) t -> k a t", k=128)
    osv = o_scr.rearrange("(a k) t -> k a t", k=128)
    oTv = oT_rows.rearrange("(a k) t -> k a t", k=128)
    kk = 0
    for dirn in range(2):
        pz = projT[256 + 256 * dirn:512 + 256 * dirn, :].rearrange("(a k) t -> k a t", k=128)
        mask = hcst[0:HCH, SEG + dirn * HCH:SEG + (dirn + 1) * HCH]
        for a in range(2):
            P.add("dve", lambda e, a=a: e.memset(S32[a][:], 0.0), [("S32", a)], [("S32", a)])
            P.add("dve", lambda e, a=a: e.memset(Sbf[a][:], 0.0), [("Sbf", a)], [("Sbf", a)])
        segs = list(range(nseg)) if dirn == 0 else list(range(nseg - 1, -1, -1))
        for si, seg in enumerate(segs):
            sb_ = si % 2
            ts = slice(seg * SEG, (seg + 1) * SEG)
            P.dma("sp", qin[sb_][:], pq[:, :, ts], [], [("qin", sb_)])
            P.dma("sp", zin[sb_][:], pz[:, :, ts], [], [("zin", sb_)])
            P.dma("sp", vin[sb_][:], pv[:, :, ts], [], [("vin", sb_)])
            if dirn == 1:
                P.dma("sp", gin[sb_][:], pg[:, :, ts], [], [("gin", sb_)])
                P.dma("sp", ofw[sb_][:], osv[:, :, ts], [("oscr", seg)], [("ofw", sb_)])
            for a in range(2):
                lb_ap = lbt[:, 0, dirn, l, a:a + 1]
                oml_ap = lbt[:, 1, dirn, l, a:a + 1]
                noml_ap = noml[:, dirn, l, a:a + 1]
                P.add("act", lambda e, a=a, sb_=sb_: e.activation(out=sig[a][:], in_=zin[sb_][:, a, :], func=AF.Sigmoid), [("zin", sb_)], [("sig", a)])
                P.add("act", lambda e, a=a, lb_ap=lb_ap, oml_ap=oml_ap: e.activation(out=g[a][:], in_=sig[a][:], func=AF.Ln, bias=lb_ap, scale=oml_ap),
                      [("sig", a), "lbt"], [("g", a)])
                P.add("dve", lambda e, a=a, noml_ap=noml_ap, oml_ap=oml_ap: e.tensor_scalar(out=kc[a][:], in0=sig[a][:], scalar1=noml_ap, scalar2=oml_ap,
                                                                                          op0=ALU.mult, op1=ALU.add), [("sig", a), "noml", "lbt"], [("kc", a)])
                P.add("dve", lambda e, a=a: e.tensor_tensor_scan(out=bb[a][:], data0=cm, data1=g[a][:], initial=0.0, op0=ALU.mult, op1=ALU.add),
                      [("g", a), "hcst"], [("bb", a)])
                b3 = bb[a][:].rearrange("p (n c) -> p n c", c=HCH)
                P.add("dve", lambda e, a=a, b3=b3: e.tensor_tensor(out=arg[a][:].rearrange("p (n c) -> p n c", c=HCH),
                                                                   in0=b3[:, :, HCH - 1:HCH].to_broadcast([128, nch, HCH]), in1=b3, op=ALU.subtract),
                      [("bb", a)], [("arg", a)])
                P.add("act", lambda e, a=a, b3=b3: e.activation(out=ebl[a][:], in_=b3[:, :, HCH - 1], func=AF.Exp), [("bb", a)], [("ebl", a)])
                if dirn == 0:
                    cq_ap, ck_ap = bb[a], arg[a]
                    kq, kk_ = ("bb", a), ("arg", a)
                else:
                    P.add("dve", lambda e, a=a: e.tensor_tensor(out=cq[a][:], in0=arg[a][:], in1=g[a][:], op=ALU.add), [("arg", a), ("g", a)], [("cq", a)])
                    P.add("dve", lambda e, a=a: e.tensor_tensor(out=ck[a][:], in0=bb[a][:], in1=g[a][:], op=ALU.subtract), [("bb", a), ("g", a)], [("ck", a)])
                    cq_ap, ck_ap = cq[a], ck[a]
                    kq, kk_ = ("cq", a), ("ck", a)
                P.add("act", lambda e, a=a, cq_ap=cq_ap: e.activation(out=ex[0][:], in_=cq_ap[:], func=AF.Exp), [kq], [("ex", 0)])
                P.add("dve", lambda e, a=a, sb_=sb_: e.tensor_tensor(out=Qt[a][:], in0=qin[sb_][:, a, :], in1=ex[0][:], op=ALU.mult),
                      [("ex", 0), ("qin", sb_)], [("Qt", a)])
                P.add("act", lambda e, a=a, cq_ap=cq_ap: e.activation(out=ex[1][:], in_=cq_ap[:], func=AF.Exp, scale=-1.0), [kq], [("ex", 1)])
                P.add("dve", lambda e, a=a: e.tensor_tensor(out=Kt[a][:], in0=kc[a][:], in1=ex[1][:], op=ALU.mult), [("ex", 1), ("kc", a)], [("Kt", a)])
                P.add("act", lambda e, a=a, ck_ap=ck_ap: e.activation(out=ex[0][:], in_=ck_ap[:], func=AF.Exp), [kk_, ("ex", 0)], [("ex", 0)])
                P.add("dve", lambda e, a=a: e.tensor_tensor(out=Kh[a][:], in0=kc[a][:], in1=ex[0][:], op=ALU.mult), [("ex", 0), ("kc", a)], [("Kh", a)])
            chunks = list(range(nch)) if dirn == 0 else list(range(nch - 1, -1, -1))
            hsteps = [(n, a) for n in chunks for a in range(2)]

            def h_front(t, kk0=kk, hsteps=hsteps, sb_=sb_, mask=mask):
                n, a = hsteps[t]
                cs = slice(n * HCH, (n + 1) * HCH)
                r = (kk0 + t) % 4
                rb = (kk0 + t) % 2
                Ap, Tp = Ap2[rb], Tp2[rb]
                P.add("pe", lambda e, a=a, Ap=Ap, cs=cs: e.matmul(Ap[0:HCH, 0:HCH], lhsT=Kt[a][:, cs], rhs=Qt[a][:, cs], start=True, stop=True),
                      [("Kt", a), ("Qt", a)], [("Ap", rb)])
                P.add("dve", lambda e, r=r, Ap=Ap, mask=mask: e.tensor_tensor(out=Am[r][:], in0=Ap[0:HCH, 0:HCH], in1=mask, op=ALU.mult),
                      [("Ap", rb), "hcst"], [("Am", r)])
                P.add("pe", lambda e, a=a, Tp=Tp, cs=cs, sb_=sb_: e.transpose(out=Tp[0:HCH, 0:128], in_=vin[sb_][:, a, cs], identity=idb[:]),
                      [("vin", sb_), "idb"], [("Tp", rb)])
                P.add("pe", lambda e, a=a, Tp=Tp, cs=cs: e.transpose(out=Tp[0:HCH, 128:256], in_=Kh[a][:, cs], identity=idb[:]),
                      [("Kh", a), "idb"], [("Tp", rb)])
                P.add("act", lambda e, r=r, Tp=Tp: e.copy(out=VK[r][:], in_=Tp[0:HCH, 0:256]), [("Tp", rb)], [("VK", r)])

            def h_back(t, kk0=kk, hsteps=hsteps):
                n, a = hsteps[t]
                cs = slice(n * HCH, (n + 1) * HCH)
                r = (kk0 + t) % 4
                rb = (kk0 + t) % 2
                Dp = Dp2[rb]
                P.add("pe", lambda e, a=a, cs=cs: e.matmul(Oa1[a][:, cs], lhsT=Sbf[a][:], rhs=Qt[a][:, cs], start=True, stop=False),
                      [("Sbf", a), ("Qt", a)], [("Oa", a)])
                P.add("pe", lambda e, a=a, r=r, cs=cs: e.matmul(Oa1[a][:, cs], lhsT=VK[r][:, 0:128], rhs=Am[r][:], start=False, stop=True),
                      [("VK", r), ("Am", r)], [("Oa", a)])
                P.add("pe", lambda e, r=r, Dp=Dp: e.matmul(Dp[:, 0:128], lhsT=VK[r][:, 128:256], rhs=VK[r][:, 0:128], start=True, stop=True),
                      [("VK", r)], [("Dp", rb)])
                P.add("dve", lambda e, a=a, Dp=Dp, n=n: e.scalar_tensor_tensor(out=S32[a][:], in0=S32[a][:], scalar=ebl[a][:, n:n + 1],
                                                                              in1=Dp[:, 0:128], op0=ALU.mult, op1=ALU.add),
                      [("S32", a), ("ebl", a), ("Dp", rb)], [("S32", a)])
                P.add("act", lambda e, a=a: e.copy(out=Sbf[a][:], in_=S32[a][:]), [("S32", a)], [("Sbf", a)])

            HLA = 2
            for t in range(len(hsteps) + HLA):
                if t < len(hsteps):
                    h_front(t)
                if t >= HLA:
                    h_back(t - HLA)
            kk += len(hsteps)
            for a in range(2):
                if dirn == 0:
                    P.add("dve", lambda e, a=a, sb_=sb_: e.tensor_copy(out=osb[a][:], in_=Oa1[a][:, 0:SEG]), [("Oa", a)], [("osb", a)])
                    P.dma("pool", osv[:, a, ts], osb[a][:], [("osb", a)], [("oscr", seg)])
                else:
                    P.add("dve", lambda e, a=a, sb_=sb_: e.tensor_tensor(out=osb[a][:], in0=Oa1[a][:, 0:SEG], in1=ofw[sb_][:, a, :], op=ALU.add),
                          [("Oa", a), ("ofw", sb_)], [("osb", a)])
                    P.add("act", lambda e, a=a: e.activation(out=ysb[a][:], in_=osb[a][:], func=AF.Square), [("osb", a)], [("ysb", a)])
                    P.add("pe", lambda e, a=a: e.matmul(Np[:, 0:SEG], lhsT=onesn[:], rhs=ysb[a][:], start=True, stop=True), [("ysb", a), "onesn"], [("Dp", 0)])
                    P.add("act", lambda e, a=a: e.activation(out=ysb[a][:], in_=Np[:, 0:SEG], func=AF.Sqrt, bias=epst[:, 0:1], scale=1.0),
                          [("Dp", 0), "eps"], [("ysb", a)])
                    P.add("dve", lambda e, a=a: e.reciprocal(out=ysb[a][:], in_=ysb[a][:]), [("ysb", a)], [("ysb", a)])
                    P.add("dve", lambda e, a=a: e.tensor_tensor(out=osb[a][:], in0=osb[a][:], in1=ysb[a][:], op=ALU.mult), [("osb", a), ("ysb", a)], [("osb", a)])
                    P.add("act", lambda e, a=a, sb_=sb_: e.activation(out=ysb[a][:], in_=gin[sb_][:, a, :], func=AF.Silu), [("gin", sb_), ("ysb", a)], [("ysb", a)])
                    P.add("dve", lambda e, a=a: e.scalar_tensor_tensor(out=yb[a][:], in0=osb[a][:], scalar=gnh[:, 0:1], in1=ysb[a][:], op0=ALU.mult, op1=ALU.mult),
                          [("osb", a), ("ysb", a), "gnh"], [("yb", a)])
                    P.dma("pool", oTv[:, a, ts], yb[a][:], [("yb", a)], [("oT", seg, a)])
    S.finish()


def stage_outproj(nc, oT_tok, w_out, xT_in, xT_out, par, ig, ntok=TL, TT=512):
    S = Stage(nc, "op")
    P = S.P
    G = S.sb([128, 32])
    oTr = S.sb([128, KC, ntok], BF16)
    Wg = [S.sb([128, KC, 128], BF16) for _ in range(2)]
    xg = [S.sb([128, ntok]) for _ in range(2)]
    ps = [S.ps() for _ in range(4)]
    P.dma("sp", G[:], par[:, ig, :], [], ["G"])
    ov = oT_tok.rearrange("(c p) t -> p c t", p=128)
    for q4 in range(4):
        P.dma("sp", oTr[:, q4 * 8:(q4 + 1) * 8, :], ov[:, q4 * 8:(q4 + 1) * 8, :], [], [("oTr", q4)])
    wv = w_out.rearrange("(c p) n -> p c n", p=128)
    k = 0
    for cg in range(D // 128):
        b = cg % 2
        P.dma("pool", Wg[b][:], wv[:, :, cg * 128:(cg + 1) * 128], [], [("Wg", b)])
        P.dma("sp", xg[b][:], xT_in[cg * 128:(cg + 1) * 128, :], [], [("xg", b)])
        for tt in range(ntok // TT):
            r = k % 4
            k += 1
            for c in range(KC):
                P.add("pe", lambda e, b=b, c=c, r=r, tt=tt: e.matmul(ps[r][:, 0:TT], lhsT=Wg[b][:, c, :], rhs=oTr[:, c, tt * TT:(tt + 1) * TT],
                                                                     start=(c == 0), stop=(c == KC - 1)),
                      [("Wg", b), ("oTr", c // 8)], [("ps", r)])
            P.add("dve", lambda e, b=b, r=r, tt=tt, cg=cg: e.scalar_tensor_tensor(out=xg[b][:, tt * TT:(tt + 1) * TT], in0=ps[r][:, 0:TT], scalar=G[:, cg:cg + 1],
                                                                                  in1=xg[b][:, tt * TT:(tt + 1) * TT], op0=ALU.mult, op1=ALU.add),
                  [("ps", r), "G", ("xg", b)], [("xg", b)])
        P.dma("sp", xT_out[cg * 128:(cg + 1) * 128, :], xg[b][:], [("xg", b)], [("xTo", cg)])
    S.finish()


def stage_ffn_up(nc, h2T_halo, w_up, cwp_d, mT, ntok=TL, TT=512):
    S = Stage(nc, "up")
    P = S.P
    NH = ntok + 2
    h2 = S.sb([128, KC, NH], BF16)
    Wg = [S.sb([128, KC, 256], BF16) for _ in range(2)]
    u2 = [S.sb([128, 2, NH]) for _ in range(2)]
    cwp = S.sb([128, 4, 128])
    HT = 512
    tA = [S.sb([128, HT]) for _ in range(2)]
    sg = S.sb([128, HT])
    mo = [S.sb([128, ntok], BF16)]
    ps = [S.ps() for _ in range(4)]
    P.dma("sp", cwp[:], cwp_d, [], ["cwp"])
    hv = h2T_halo.rearrange("(c p) t -> p c t", p=128)
    for q4 in range(4):
        P.dma("sp", h2[:, q4 * 8:(q4 + 1) * 8, :], hv[:, q4 * 8:(q4 + 1) * 8, :], [], [("h2", q4)])
    wv = w_up.rearrange("(c p) n -> p c n", p=128)
    mv = mT.rearrange("(g p) t -> p g t", p=128)
    k = 0
    for jg in range(DFF // 128):
        b = jg % 2
        u = u2[b]
        P.dma("pool", Wg[b][:, :, 0:128], wv[:, :, jg * 128:(jg + 1) * 128], [], [("Wg", b, 0)])
        P.dma("pool", Wg[b][:, :, 128:256], wv[:, :, DFF + jg * 128:DFF + (jg + 1) * 128], [], [("Wg", b, 1)])
        for gv in range(2):
            for ct in range(ntok // TT + 1):
                r = k % 4
                k += 1
                if ct < ntok // TT:
                    n = TT
                    rsl = slice(1 + ct * TT, 1 + (ct + 1) * TT)
                else:
                    n = 2
                    rsl = slice(0, NH, NH - 1)
                for c in range(KC):
                    P.add("pe", lambda e, b=b, c=c, r=r, gv=gv, n=n, rsl=rsl: e.matmul(ps[r][:, 0:n], lhsT=Wg[b][:, c, gv * 128:(gv + 1) * 128],
                                                                                       rhs=h2[:, c, rsl], start=(c == 0), stop=(c == KC - 1)),
                          [("Wg", b, gv), ("h2", c // 8)], [("ps", r)])
                if k % 2:
                    P.add("act", lambda e, r=r, gv=gv, n=n, rsl=rsl, u=u: e.copy(out=u[:, gv, rsl], in_=ps[r][:, 0:n]), [("ps", r)], [("u", b, gv)])
                else:
                    P.add("dve", lambda e, r=r, gv=gv, n=n, rsl=rsl, u=u: e.tensor_copy(out=u[:, gv, rsl], in_=ps[r][:, 0:n]), [("ps", r)], [("u", b, gv)])
        for th in range(ntok // HT):
            o = th * HT
            for gv in range(2):
                gi = gv * 64 + jg
                P.add("act", lambda e, gv=gv, gi=gi, o=o, u=u: e.activation(out=tA[gv][:], in_=u[:, gv, 1 + o:1 + o + HT], func=AF.Identity,
                                                                       bias=cwp[:, 3, gi:gi + 1], scale=cwp[:, 1, gi:gi + 1]),
                      [("u", b, gv), "cwp"], [("tA", gv)])
                P.add("dve", lambda e, gv=gv, gi=gi, o=o, u=u: e.scalar_tensor_tensor(out=tA[gv][:], in0=u[:, gv, o:o + HT], scalar=cwp[:, 0, gi:gi + 1],
                                                                                 in1=tA[gv][:], op0=ALU.mult, op1=ALU.add),
                      [("u", b, gv), "cwp", ("tA", gv)], [("tA", gv)])
                P.add("dve", lambda e, gv=gv, gi=gi, o=o, u=u: e.scalar_tensor_tensor(out=tA[gv][:], in0=u[:, gv, 2 + o:2 + o + HT], scalar=cwp[:, 2, gi:gi + 1],
                                                                                  in1=tA[gv][:], op0=ALU.mult, op1=ALU.add),
                      [("u", b, gv), "cwp", ("tA", gv)], [("tA", gv)])
            P.add("act", lambda e: e.activation(out=sg[:], in_=tA[0][:], func=AF.Silu), [("tA", 0)], ["sg"])
            P.add("dve", lambda e, o=o: e.tensor_tensor(out=mo[0][:, o:o + HT], in0=sg[:], in1=tA[1][:], op=ALU.mult),
                  ["sg", ("tA", 1)], [("mo", 0)])
        P.dma("sp", mv[:, jg, :], mo[0][:], [("mo", 0)], [("mT", jg)])
    S.finish()


def stage_ffn_down(nc, mT, w_down, xT_in, xT_out, par, ig, ntok=TL, TT=512):
    S = Stage(nc, "dn")
    P = S.P
    NK = DFF // 128
    TB = min(1024, ntok)
    G = S.sb([128, 32])
    mt = S.sb([128, NK, TB], BF16)
    Wg = [S.sb([128, NK, 128], BF16) for _ in range(2)]
    xg = [S.sb([128, TB]) for _ in range(3)]
    ps = [S.ps() for _ in range(4)]
    P.dma("sp", G[:], par[:, ig, :], [], ["G"])
    mv = mT.rearrange("(c p) t -> p c t", p=128)
    wv = w_down.rearrange("(c p) n -> p c n", p=128)
    k = 0
    kc_ = 0
    for th in range(ntok // TB):
        ts = slice(th * TB, (th + 1) * TB)
        for q4 in range(8):
            P.dma("sp", mt[:, q4 * 8:(q4 + 1) * 8, :], mv[:, q4 * 8:(q4 + 1) * 8, ts], [("mT", g) for g in range(q4 * 8, (q4 + 1) * 8)],
                  [("mt", q4)])
        for cg in range(D // 128):
            b = kc_ % 2
            x3 = kc_ % 3
            kc_ += 1
            P.dma("pool", Wg[b][:], wv[:, :, cg * 128:(cg + 1) * 128], [], [("Wg", b)])
            P.dma("sp", xg[x3][:], xT_in[cg * 128:(cg + 1) * 128, ts], [], [("xg", x3)])
            for ct in range(TB // TT):
                r = k % 4
                k += 1
                cs = slice(ct * TT, (ct + 1) * TT)
                for c in range(NK):
                    P.add("pe", lambda e, b=b, c=c, r=r, cs=cs: e.matmul(ps[r][:, 0:TT], lhsT=Wg[b][:, c, :], rhs=mt[:, c, cs],
                                                                         start=(c == 0), stop=(c == NK - 1)),
                          [("Wg", b), ("mt", c // 8)], [("ps", r)])
                P.add("dve", lambda e, r=r, x3=x3, cg=cg, cs=cs: e.scalar_tensor_tensor(out=xg[x3][:, cs], in0=ps[r][:, 0:TT], scalar=G[:, cg:cg + 1],
                                                                                        in1=xg[x3][:, cs], op0=ALU.mult, op1=ALU.add),
                      [("ps", r), "G", ("xg", x3)], [("xg", x3)])
            P.dma("sp", xT_out[cg * 128:(cg + 1) * 128, ts], xg[x3][:], [("xg", x3)], [("xTo", cg, th)])
    S.finish()


_PROGS = {}


def _dt(nc, n, s, d=F32, k="ExternalInput"):
    return nc.dram_tensor(n, list(s), d, kind=k).ap()


def prog_L0():
    if "L0" in _PROGS:
        return _PROGS["L0"]
    nc = bass.Bass("TRN2", target_bir_lowering=False)
    x = _dt(nc, "x", [TL, D]); vecs = _dt(nc, "vecs", [1280, 128]); w_ada = _dt(nc, "w_ada", [D, 6 * D])
    lbl = _dt(nc, "lbl", [128, 128]); idf = _dt(nc, "idf", [128, 128])
    par = _dt(nc, "par", [128, DEPTH * 6 + 1, 32], F32, "ExternalOutput")
    lbo = _dt(nc, "lbo", [128, 2, 2, DEPTH, 16], F32, "ExternalOutput")
    xT = _dt(nc, "xT", [D, TL], F32, "ExternalOutput")
    hT = _dt(nc, "hT", [D, TL], BF16, "ExternalOutput")
    stage_params(nc, vecs, w_ada, lbl, idf, par, lbo)
    stage_pre(nc, x, xT, idf)
    stage_norm(nc, xT, par, 0, 1, hT)
    _PROGS["L0"] = nc
    return nc


def prog_LB():
    if "LB" in _PROGS:
        return _PROGS["LB"]
    nc = bass.Bass("TRN2", target_bir_lowering=False)
    hT = _dt(nc, "hT", [D, T], BF16); w = _dt(nc, "w", [D, 2048]); acst = _dt(nc, "acst", [128, 772]); lam = _dt(nc, "lam", [128, 4])
    lamc = _dt(nc, "lamc", [128, 2]); gn = _dt(nc, "gn", [128, 256]); idb = _dt(nc, "idb", [128, 128], BF16)
    lbo = _dt(nc, "lbo", [128, 2, 2, DEPTH, 16]); hc = _dt(nc, "hc", [128, 512 + 64]); gnh = _dt(nc, "gnh", [128, 1])
    projT = _dt(nc, "projT", [2048, T], BF16, "Internal")
    oscr = _dt(nc, "oscr", [256, T], F32, "Internal")
    vtok = _dt(nc, "vtok", [T, 256], BF16, "Internal")
    oT = _dt(nc, "oT", [512, T], BF16, "ExternalOutput")
    stage_inproj(nc, hT, w, projT, vtok)
    stage_attn(nc, projT, vtok, acst, lam, gn, idb, oT[256:512, :], lamc)
    stage_hgrn(nc, projT, lbo, 0, hc, gnh, idb, oscr, oT[0:256, :])
    _PROGS["LB"] = nc
    return nc


def prog_LC1():
    if "LC1" in _PROGS:
        return _PROGS["LC1"]
    nc = bass.Bass("TRN2", target_bir_lowering=False)
    oT = _dt(nc, "oT", [D, TL], BF16); w_out = _dt(nc, "w_out", [D, D]); xT = _dt(nc, "xT", [D, TL]); par = _dt(nc, "par", [128, 8, 32])
    xTo = _dt(nc, "xTo", [D, TL], F32, "ExternalOutput")
    h2T = _dt(nc, "h2T", [D, TL], BF16, "ExternalOutput")
    stage_outproj(nc, oT, w_out, xT, xTo, par, 2)
    stage_norm(nc, xTo, par, 3, 4, h2T)
    _PROGS["LC1"] = nc
    return nc


def prog_LC2(final):
    key = "LC2f" if final else "LC2"
    if key in _PROGS:
        return _PROGS[key]
    nc = bass.Bass("TRN2", target_bir_lowering=False)
    h2 = _dt(nc, "h2", [D, TL + 2], BF16); w_up = _dt(nc, "w_up", [D, 2 * DFF]); cwp = _dt(nc, "cwp", [128, 4, 128])
    w_down = _dt(nc, "w_down", [DFF, D]); xT = _dt(nc, "xT", [D, TL]); par = _dt(nc, "par", [128, 8, 32])
    mT = _dt(nc, "mT", [DFF, TL], BF16, "Internal")
    xTo = _dt(nc, "xTo", [D, TL], F32, "ExternalOutput")
    stage_ffn_up(nc, h2, w_up, cwp, mT)
    stage_ffn_down(nc, mT, w_down, xT, xTo, par, 5)
    if final:
        idf = _dt(nc, "idf", [128, 128])
        y = _dt(nc, "y", [TL, D], F32, "ExternalOutput")
        stage_norm(nc, xTo, par, 6, None, y, idf_d=idf, final=True)
    else:
        hT = _dt(nc, "hT", [D, TL], BF16, "ExternalOutput")
        stage_norm(nc, xTo, par, 6, 7, hT)
    _PROGS[key] = nc
    return nc


def _run(nc, maps):
    return run_bass_kernel_spmd(nc, maps, core_ids=list(range(NCORE))).results


def kernel(x, c, w_ada, b_ada, ada_table, norm1_g, w_in, hg_lb_logits, hg_norm_g, da_lambda,
           da_norm_g, w_out, norm2_g, w_up, conv_w, conv_b, w_down, final_g):
    f32 = np.float32
    A = lambda a: np.ascontiguousarray(np.asarray(a))
    x = A(x); w_ada = A(w_ada); w_in = np.asarray(w_in); w_out = np.asarray(w_out); w_up = np.asarray(w_up); w_down = np.asarray(w_down)
    x2 = x.reshape(T, D)
    vecs = np.concatenate([np.asarray(c).reshape(1, D), np.asarray(b_ada).reshape(6, D), np.asarray(ada_table).reshape(24, D),
                           np.asarray(norm1_g), np.asarray(norm2_g), np.asarray(final_g).reshape(1, D)], 0).astype(f32).reshape(1280, 128)
    idf = np.eye(128, dtype=f32)
    idb = np.eye(128).astype(ml_dtypes.bfloat16)
    lbl4 = np.asarray(hg_lb_logits).reshape(2, DEPTH, 16, 128)
    hc = hgrn_consts()
    maps = []
    for j in range(NCORE):
        order = [2 * j, 2 * j + 1] + [h for h in range(16) if h not in (2 * j, 2 * j + 1)]
        maps.append({"x": A(x2[j * TL:(j + 1) * TL]), "vecs": vecs, "w_ada": w_ada,
                     "lbl": A(lbl4[:, :, order, :].reshape(128, 128)), "idf": idf})
    r0 = _run(prog_L0(), maps)
    par = r0[0]["par"]
    lbo = [r0[j]["lbo"] for j in range(NCORE)]
    xT = [r0[j]["xT"] for j in range(NCORE)]
    hT = [r0[j]["hT"] for j in range(NCORE)]
    del r0
    y = None
    for l in range(DEPTH):
        lam_init = 0.8 - 0.6 * math.exp(-0.3 * l)
        hT_all = np.concatenate(hT, axis=1)
        lamc = np.tile(np.array([[1.0 - lam_init, -lam_init]], f32), (128, 1))
        maps = []
        for j in range(NCORE):
            cols = np.concatenate([np.arange(g * 2048 + j * 256, g * 2048 + (j + 1) * 256) for g in range(5)] +
                                  [np.arange(10240 + g * 2048 + j * 256, 10240 + g * 2048 + (j + 1) * 256) for g in range(3)])
            lbo_l = np.repeat(lbo[j][:, :, :, l:l + 1, :], DEPTH, axis=3)
            maps.append({"hT": hT_all, "w": A(w_in[l][:, cols]), "acst": attn_consts(j), "lam": A(np.asarray(da_lambda)[l].T.astype(f32)),
                         "lamc": lamc, "gn": A(np.broadcast_to(np.asarray(da_norm_g)[l], (128, 256)).astype(f32)), "idb": idb,
                         "lbo": A(lbo_l), "hc": hc, "gnh": A(np.asarray(hg_norm_g)[l].reshape(128, 1).astype(f32))})
        rb = _run(prog_LB(), maps)
        oT_all = np.empty((D, T), dtype=ml_dtypes.bfloat16)
        for j in range(NCORE):
            oT_all[j * 256:(j + 1) * 256] = rb[j]["oT"][0:256]
            oT_all[2048 + j * 256:2048 + (j + 1) * 256] = rb[j]["oT"][256:512]
        del rb, hT_all
        nxt = par[:, 6 * (l + 1):6 * (l + 1) + 2, :] if l + 1 < DEPTH else np.repeat(par[:, 24:25, :], 2, axis=1)
        par_l = A(np.concatenate([par[:, 6 * l:6 * l + 6, :], nxt], axis=1))
        maps = [{"oT": A(oT_all[:, j * TL:(j + 1) * TL]), "w_out": A(w_out[l]), "xT": xT[j], "par": par_l} for j in range(NCORE)]
        rc = _run(prog_LC1(), maps)
        xT = [rc[j]["xTo"] for j in range(NCORE)]
        h2 = [rc[j]["h2T"] for j in range(NCORE)]
        del rc, oT_all
        zcol = np.zeros((D, 1), dtype=ml_dtypes.bfloat16)
        cwp = A(np.concatenate([np.asarray(conv_w)[l], np.asarray(conv_b)[l][None]], 0).astype(f32).reshape(4, 128, 128).transpose(2, 0, 1))
        maps = []
        for j in range(NCORE):
            left = h2[j - 1][:, -1:] if j > 0 else zcol
            right = h2[j + 1][:, :1] if j + 1 < NCORE else zcol
            m = {"h2": A(np.concatenate([left, h2[j], right], axis=1)), "w_up": A(w_up[l]), "cwp": cwp, "w_down": A(w_down[l]),
                 "xT": xT[j], "par": par_l}
            if l == DEPTH - 1:
                m["idf"] = idf
            maps.append(m)
        rd = _run(prog_LC2(l == DEPTH - 1), maps)
        xT = [rd[j]["xTo"] for j in range(NCORE)]
        if l == DEPTH - 1:
            y = np.concatenate([rd[j]["y"] for j in range(NCORE)], axis=0)
        else:
            hT = [rd[j]["hT"] for j in range(NCORE)]
        del rd
    return y.reshape(1, T, D).astype(f32)
```

```python
import contextlib
import math
import numpy as np
import ml_dtypes
import concourse.bass as bass
import concourse.mybir as mybir
from concourse.bass_utils import run_bass_kernel_spmd

F32 = mybir.dt.float32
BF16 = mybir.dt.bfloat16
AF = mybir.ActivationFunctionType
ALU = mybir.AluOpType
AX = mybir.AxisListType

D = 4096
KC = 32
T = 16384
NCORE = 8
TL = T // NCORE
DEPTH = 4
DFF = 8192
EPS = 1e-6
NVEC = 40
HCH = 32
DBG_PREP = [0]
VQ = ['sp']


class Op:
    __slots__ = ("q", "fn", "reads", "writes", "dma", "sem", "deps", "signal", "count", "idx", "prev")


class Prog:
    QUEUES = ("pe", "dve", "act", "pool", "sp")
    RING = 8
    _scnt = [0]

    def __init__(self, nc, strict=True):
        self.nc = nc
        self.ops = []
        self.strict = strict

    def add(self, q, fn, reads=(), writes=(), dma=False):
        o = Op()
        o.q, o.fn, o.reads, o.writes, o.dma = q, fn, tuple(reads), tuple(writes), dma
        o.sem = None
        o.deps, o.signal, o.count, o.idx, o.prev = [], False, 0, len(self.ops), 0
        self.ops.append(o)
        return o

    def dma(self, q, out, in_, reads, writes, **kw):
        return self.add(q, lambda e: e.dma_start(out=out, in_=in_, **kw), reads, writes, dma=True)

    def emit(self):
        nc = self.nc
        ops = self.ops
        last_w = {}
        readers = {}
        for o in ops:
            deps = set()
            for k in o.reads:
                if k in last_w:
                    deps.add(last_w[k])
            for k in o.writes:
                if k in last_w:
                    deps.add(last_w[k])
                for r in readers.get(k, ()):
                    deps.add(r)
            deps.discard(o.idx)
            o.deps = sorted(deps)
            for k in o.writes:
                last_w[k] = o.idx
                readers[k] = []
            for k in o.reads:
                readers.setdefault(k, []).append(o.idx)
        for o in ops:
            if o.dma:
                o.signal = True
            for d in o.deps:
                a = ops[d]
                if a.dma or a.q != o.q or (self.strict and a.q != "pe"):
                    a.signal = True
        counts = {}
        nd = {}
        for o in ops:
            if not o.signal:
                continue
            if o.dma:
                k = nd.get(o.q, 0)
                nd[o.q] = k + 1
                name = ("dq", o.q, k % self.RING)
            else:
                name = ("eng", o.q)
            o.sem = name
            o.prev = counts.get(name, 0)
            counts[name] = o.prev + (16 if o.dma else 1)
            o.count = counts[name]
        with contextlib.ExitStack() as st:
            sems = {}
            for i, name in enumerate(counts):
                Prog._scnt[0] += 1
                sems[name] = st.enter_context(nc.semaphore("s%d" % Prog._scnt[0]))
            block = st.enter_context(nc.Block())
            engs = {"pe": block.tensor, "dve": block.vector, "act": block.scalar,
                    "pool": block.gpsimd, "sp": block.sync}
            for q in self.QUEUES:
                qops = [o for o in ops if o.q == q]
                if not qops and q != "sp":
                    continue

                def body(e, qops=qops, q=q):
                    seen = {}

                    def wait(name, v):
                        if v > 0 and seen.get(name, 0) < v:
                            e.wait_ge(sems[name], v)
                            seen[name] = v
                    for o in qops:
                        need = {}
                        for d in o.deps:
                            a = ops[d]
                            if not a.signal:
                                continue
                            if (not a.dma) and a.q == q and (q == "pe" or not self.strict):
                                continue
                            if need.get(a.sem, 0) < a.count:
                                need[a.sem] = a.count
                        for nm, v in need.items():
                            wait(nm, v)
                        if o.dma:
                            wait(o.sem, o.prev)
                        ins = o.fn(e)
                        if o.signal:
                            ins.then_inc(sems[o.sem], 16 if o.dma else 1)
                    if q == "sp":
                        for name, c in counts.items():
                            if name[0] == "dq":
                                wait(name, c)
                engs[q](body)


class Stage:
    _cnt = [0]

    def __init__(self, nc, name):
        Stage._cnt[0] += 1
        self.nc, self.name = nc, "%s%d" % (name, Stage._cnt[0])
        self.st = contextlib.ExitStack()
        self.P = Prog(nc)
        self.n = 0

    def sb(self, shape, dt=F32, name=None):
        self.n += 1
        return self.st.enter_context(self.nc.sbuf_tensor("%s_%s%d" % (self.name, name or "t", self.n), list(shape), dt))

    def ps(self, shape=(128, 512), dt=F32, name=None):
        self.n += 1
        return self.st.enter_context(self.nc.psum_tensor("%s_%s%d" % (self.name, name or "p", self.n), list(shape), dt))

    def finish(self):
        self.P.emit()
        self.st.close()


def stage_params(nc, vecs, w_ada, lbl, idf_d, par, lbo):
    S = Stage(nc, "par")
    P = S.P
    V = S.sb([128, 10, 128])
    VT = S.sb([128, NVEC, 32])
    idf = S.sb([128, 128])
    sc = S.sb([128, 32])
    modT = S.sb([128, 6, 32])
    mm = S.sb([128, 6, 32])
    outp = S.sb([128, DEPTH * 6 + 1, 32])
    wsl = [S.sb([128, KC, 512]) for _ in range(2)]
    LR = S.sb([128, 128])
    LT = S.sb([128, 2, DEPTH, 16])
    EX = S.sb([128, 2, DEPTH, 16])
    mx = S.sb([128, 2, 16])
    sm = S.sb([128, 2, 16])
    lbs = S.sb([128, 2, 2, DEPTH, 16])
    pst = [S.ps() for _ in range(2)]
    psm = S.ps()

    P.dma("sp", idf[:], idf_d, [], ["idf"])
    P.dma("sp", V[:], vecs.rearrange("(i r) p -> r i p", r=128), [], ["V"])
    P.dma("sp", LR[:], lbl, [], ["LR"])
    for i in range(10):
        b = i % 2
        P.add("pe", lambda e, i=i, b=b: e.transpose(out=pst[b][:, 0:128], in_=V[:, i, :], identity=idf[:]),
              ["V", "idf"], [("pst", b)])
        P.add("dve", lambda e, i=i, b=b: e.tensor_copy(
            out=VT[:, 4 * i:4 * i + 4, :], in_=pst[b][:, 0:128].rearrange("p (v c) -> p v c", c=32)),
            [("pst", b)], ["VT"])
    P.add("act", lambda e: e.activation(out=sc[:], in_=VT[:, 0, :], func=AF.Silu), ["VT"], ["sc"])
    wv = w_ada.rearrange("(c p) n -> p c n", p=128)
    NSL = (6 * D) // 512
    for s in range(NSL):
        b = s % 2
        P.dma("sp", wsl[b][:], wv[:, :, s * 512:(s + 1) * 512], [], [("wsl", b)])
        for g in range(4):
            col = s * 4 + g
            for c in range(KC):
                P.add("pe", lambda e, b=b, g=g, c=c, col=col: e.matmul(
                    psm[:, col:col + 1], lhsT=wsl[b][:, c, g * 128:(g + 1) * 128], rhs=sc[:, c:c + 1],
                    start=(c == 0), stop=(c == KC - 1)), [("wsl", b), "sc"], ["psm"])
    P.add("dve", lambda e: e.tensor_tensor(out=modT[:], in0=psm[:, 0:192].rearrange("p (j c) -> p j c", c=32),
                                           in1=VT[:, 1:7, :], op=ALU.add), ["psm", "VT"], ["modT"])
    for l in range(DEPTH):
        P.add("dve", lambda e, l=l: e.tensor_tensor(out=mm[:], in0=modT[:], in1=VT[:, 7 + 6 * l:13 + 6 * l, :], op=ALU.add),
              ["modT", "VT", "outp"], ["mm"])
        P.add("dve", lambda e, l=l: e.scalar_tensor_tensor(out=outp[:, 6 * l + 0, :], in0=mm[:, 1, :], scalar=1.0,
                                                           in1=VT[:, 31 + l, :], op0=ALU.add, op1=ALU.mult), ["mm", "VT"], ["outp"])
        P.add("dve", lambda e, l=l: e.tensor_copy(out=outp[:, 6 * l + 1, :], in_=mm[:, 0, :]), ["mm"], ["outp"])
        P.add("dve", lambda e, l=l: e.tensor_copy(out=outp[:, 6 * l + 2, :], in_=mm[:, 2, :]), ["mm"], ["outp"])
        P.add("dve", lambda e, l=l: e.scalar_tensor_tensor(out=outp[:, 6 * l + 3, :], in0=mm[:, 4, :], scalar=1.0,
                                                           in1=VT[:, 35 + l, :], op0=ALU.add, op1=ALU.mult), ["mm", "VT"], ["outp"])
        P.add("dve", lambda e, l=l: e.tensor_copy(out=outp[:, 6 * l + 4, :], in_=mm[:, 3, :]), ["mm"], ["outp"])
        P.add("dve", lambda e, l=l: e.tensor_copy(out=outp[:, 6 * l + 5, :], in_=mm[:, 5, :]), ["mm"], ["outp"])
    P.add("dve", lambda e: e.tensor_copy(out=outp[:, 6 * DEPTH, :], in_=VT[:, 39, :]), ["VT"], ["outp"])
    P.dma("sp", par, outp[:], ["outp"], ["par"])
    P.add("pe", lambda e: e.transpose(out=pst[0][:, 0:128], in_=LR[:], identity=idf[:]), ["LR", "idf", ("pst", 0)], [("pst", 0)])
    P.add("dve", lambda e: e.tensor_copy(out=LT[:], in_=pst[0][:, 0:128].rearrange("p (d l h) -> p d l h", d=2, l=DEPTH)),
          [("pst", 0)], ["LT"])
    P.add("dve", lambda e: e.tensor_tensor(out=mx[:], in0=LT[:, :, 0, :], in1=LT[:, :, 1, :], op=ALU.max), ["LT"], ["mx"])
    for l in (2, 3):
        P.add("dve", lambda e, l=l: e.tensor_tensor(out=mx[:], in0=mx[:], in1=LT[:, :, l, :], op=ALU.max), ["LT", "mx"], ["mx"])
    for l in range(DEPTH):
        P.add("dve", lambda e, l=l: e.tensor_tensor(out=EX[:, :, l, :], in0=LT[:, :, l, :], in1=mx[:], op=ALU.subtract),
              ["LT", "mx"], ["EX"])
    P.add("act", lambda e: e.activation(out=EX[:], in_=EX[:], func=AF.Exp), ["EX"], ["EX"])
    P.add("dve", lambda e: e.tensor_tensor(out=sm[:], in0=EX[:, :, 0, :], in1=EX[:, :, 1, :], op=ALU.add), ["EX"], ["sm"])
    for l in (2, 3):
        P.add("dve", lambda e, l=l: e.tensor_tensor(out=sm[:], in0=sm[:], in1=EX[:, :, l, :], op=ALU.add), ["EX", "sm"], ["sm"])
    P.add("dve", lambda e: e.reciprocal(out=sm[:], in_=sm[:]), ["sm"], ["sm"])
    for l in range(DEPTH):
        P.add("dve", lambda e, l=l: e.tensor_tensor(out=EX[:, :, l, :], in0=EX[:, :, l, :], in1=sm[:], op=ALU.mult),
              ["EX", "sm"], ["EX"])
    P.add("dve", lambda e: e.memset(lbs[:, 0, :, 0, :], 0.0), [], ["lbs"])
    P.add("dve", lambda e: e.tensor_copy(out=lbs[:, 0, :, 1, :], in_=EX[:, :, 1, :]), ["EX", "lbs"], ["lbs"])
    for l in (2, 3):
        P.add("dve", lambda e, l=l: e.tensor_tensor(out=lbs[:, 0, :, l, :], in0=lbs[:, 0, :, l - 1, :], in1=EX[:, :, l, :], op=ALU.add),
              ["EX", "lbs"], ["lbs"])
    P.add("dve", lambda e: e.tensor_scalar(out=lbs[:, 1], in0=lbs[:, 0], scalar1=-1.0, scalar2=1.0, op0=ALU.mult, op1=ALU.add),
          ["lbs"], ["lbs"])
    P.dma("sp", lbo, lbs[:], ["lbs"], ["lbo"])
    S.finish()


def stage_pre(nc, x, xT, idf_d, ntok=TL):
    S = Stage(nc, "pre")
    P = S.P
    idf = S.sb([128, 128])
    xin = [S.sb([128, D]) for _ in range(2)]
    xo = [S.sb([128, 4, 128]) for _ in range(4)]
    pst = [S.ps() for _ in range(4)]
    P.dma("sp", idf[:], idf_d, [], ["idf"])
    xTv = xT.rearrange("(c p) t -> p c t", p=128)
    k = 0
    for tb in range(ntok // 128):
        b = tb % 2
        P.dma("sp", xin[b][:], x[tb * 128:(tb + 1) * 128, :], [], [("xin", b)])
        for cg in range(KC // 4):
            r = k % 4
            k += 1
            for j in range(4):
                c = cg * 4 + j
                P.add("pe", lambda e, b=b, c=c, j=j, r=r: e.transpose(out=pst[r][:, j * 128:(j + 1) * 128],
                                                                      in_=xin[b][:, c * 128:(c + 1) * 128], identity=idf[:]),
                      [("xin", b), "idf"], [("pst", r)])
            eng = "act" if (k % 2) else "dve"
            if eng == "act":
                P.add("act", lambda e, r=r: e.copy(out=xo[r][:], in_=pst[r][:].rearrange("p (j t) -> p j t", j=4)),
                      [("pst", r)], [("xo", r)])
            else:
                P.add("dve", lambda e, r=r: e.tensor_copy(out=xo[r][:], in_=pst[r][:].rearrange("p (j t) -> p j t", j=4)),
                      [("pst", r)], [("xo", r)])
            P.dma("pool", xTv[:, cg * 4:cg * 4 + 4, tb * 128:(tb + 1) * 128], xo[r][:], [("xo", r)], [("xT", tb, cg)])
    S.finish()


def stage_norm(nc, xT, par, ia, ib, out, idf_d=None, final=False, ntok=TL, TT=512, out_off=0):
    S = Stage(nc, "nrm")
    P = S.P
    A = S.sb([128, 32])
    B = S.sb([128, 32])
    ones = S.sb([128, 128])
    xt = [S.sb([128, KC, TT]) for _ in range(2)]
    sq = [S.sb([128, TT]) for _ in range(3)]
    rstd = S.sb([128, TT])
    tmp = [S.sb([128, TT]) for _ in range(3)]
    pss = S.ps()
    P.dma("sp", A[:], par[:, ia, :], [], ["A"])
    if not final:
        P.dma("sp", B[:], par[:, ib, :], [], ["B"])
        ho = [S.sb([128, KC, TT], BF16) for _ in range(2)]
        ov = out.rearrange("(c p) t -> p c t", p=128)
    else:
        idf = S.sb([128, 128])
        P.dma("sp", idf[:], idf_d, [], ["idf"])
        yo = [S.sb([128, TT]) for _ in range(3)]
        pst = [S.ps() for _ in range(3)]
        yrow = [S.sb([128, D]) for _ in range(2)]
    P.add("dve", lambda e: e.memset(ones[:], 1.0 / D), [], ["ones"])
    epst = S.sb([128, 1])
    P.add("dve", lambda e: e.memset(epst[:], EPS), [], ["eps"])
    xv = xT.rearrange("(c p) t -> p c t", p=128)
    nt = ntok // TT
    for tt in range(nt):
        b = tt % 2
        P.dma("sp", xt[b][:], xv[:, :, tt * TT:(tt + 1) * TT], [], [("xt", b)])
        for c in range(KC):
            r = c % 3
            P.add("act", lambda e, b=b, c=c, r=r: e.activation(out=sq[r][:], in_=xt[b][:, c, :], func=AF.Square),
                  [("xt", b)], [("sq", r)])
            P.add("pe", lambda e, c=c, r=r: e.matmul(pss[:, 0:TT], lhsT=ones[:], rhs=sq[r][:], start=(c == 0), stop=(c == KC - 1)),
                  [("sq", r), "ones"], ["pss"])
        P.add("act", lambda e: e.activation(out=rstd[:], in_=pss[:, 0:TT], func=AF.Sqrt, bias=epst[:, 0:1], scale=1.0),
              ["pss", "eps"], ["rstd"])
        P.add("dve", lambda e: e.reciprocal(out=rstd[:], in_=rstd[:]), ["rstd"], ["rstd"])
        if not final:
            for c in range(KC):
                r = c % 3
                P.add("dve", lambda e, b=b, c=c, r=r: e.scalar_tensor_tensor(out=tmp[r][:], in0=xt[b][:, c, :], scalar=A[:, c:c + 1],
                                                                             in1=rstd[:], op0=ALU.mult, op1=ALU.mult),
                      [("xt", b), "A", "rstd"], [("tmp", r)])
                P.add("act", lambda e, b=b, c=c, r=r: e.activation(out=ho[b][:, c, :], in_=tmp[r][:], func=AF.Identity,
                                                                   bias=B[:, c:c + 1], scale=1.0),
                      [("tmp", r), "B"], [("ho", b)])
            P.dma("pool", ov[:, :, out_off + tt * TT:out_off + (tt + 1) * TT], ho[b][:], [("ho", b)], [("out", tt)])
        else:
            for c in range(KC):
                r = c % 3
                P.add("dve", lambda e, b=b, c=c, r=r: e.scalar_tensor_tensor(out=yo[r][:], in0=xt[b][:, c, :], scalar=A[:, c:c + 1],
                                                                             in1=rstd[:], op0=ALU.mult, op1=ALU.mult),
                      [("xt", b), "A", "rstd"], [("yo", r)])
                for tb in range(TT // 128):
                    P.add("pe", lambda e, r=r, tb=tb: e.transpose(out=pst[r][:, tb * 128:(tb + 1) * 128],
                                                                  in_=yo[r][:, tb * 128:(tb + 1) * 128], identity=idf[:]),
                          [("yo", r), "idf"], [("pst", r)])
                P.add("act", lambda e, r=r: e.copy(out=tmp[r][:], in_=pst[r][:, 0:TT]), [("pst", r)], [("tmp", r)])
                for tb in range(TT // 128):
                    t0 = tt * TT + tb * 128
                    P.dma("pool", out[t0:t0 + 128, c * 128:(c + 1) * 128], tmp[r][:, tb * 128:(tb + 1) * 128],
                          [("tmp", r)], [("out", tt, c, tb)])
    S.finish()


def stage_inproj(nc, hT_all, w, projT, vtok, ntok=T, TT=512):
    S = Stage(nc, "inp")
    P = S.P
    NCOL = 2048
    W = S.sb([128, KC, NCOL], BF16)
    ht = [S.sb([128, KC, TT], BF16) for _ in range(2)]
    og = [S.sb([128, 4, TT], BF16) for _ in range(2)]
    vo = [S.sb([128, TT // 128, 256], BF16) for _ in range(2)]
    ps = [S.ps() for _ in range(4)]
    vtv = vtok.rearrange("(n p) d -> p n d", p=128)
    wv = w.rearrange("(c p) n -> p c n", p=128)
    for blk in range(8):
        P.dma("pool", W[:, :, blk * 256:(blk + 1) * 256], wv[:, :, blk * 256:(blk + 1) * 256], [], [("W", blk)])
    hv = hT_all.rearrange("(c p) t -> p c t", p=128)
    pv = projT.rearrange("(g p) t -> p g t", p=128)
    qscale = 128.0 ** -0.5
    k = 0
    for tt in range(ntok // TT):
        b = tt % 2
        P.dma("sp", ht[b][:], hv[:, :, tt * TT:(tt + 1) * TT], [], [("ht", b)])
        for cg in range(14):
            r = k % 4
            k += 1
            for c in range(KC):
                P.add("pe", lambda e, b=b, c=c, cg=cg, r=r: e.matmul(ps[r][:, 0:TT], lhsT=W[:, c, cg * 128:(cg + 1) * 128],
                                                                     rhs=ht[b][:, c, :], start=(c == 0), stop=(c == KC - 1)),
                      [("W", cg // 2), ("ht", b)], [("ps", r)])
            ob = (k // 4) % 2 if False else ((tt * 4 + cg // 4) % 2)
            sc = qscale if cg in (0, 1, 10, 11) else 1.0
            if k % 2:
                P.add("act", lambda e, r=r, ob=ob, cg=cg, sc=sc: e.activation(out=og[ob][:, cg % 4, :], in_=ps[r][:, 0:TT],
                                                                             func=AF.Copy, scale=sc),
                      [("ps", r)], [("og", ob)])
            else:
                P.add("dve", lambda e, r=r, ob=ob, cg=cg, sc=sc: e.tensor_scalar(out=og[ob][:, cg % 4, :], in0=ps[r][:, 0:TT],
                                                                                scalar1=sc, scalar2=None, op0=ALU.mult),
                      [("ps", r)], [("og", ob)])
            if cg % 4 == 3 or cg == 13:
                g0 = (cg // 4) * 4
                ng = cg - g0 + 1
                P.dma("pool", pv[:, g0:g0 + ng, tt * TT:(tt + 1) * TT], og[ob][:, 0:ng, :], [("og", ob)], [("proj", tt, g0)])
        for tb in range(TT // 128):
            r = k % 4
            k += 1
            for c in range(KC):
                P.add("pe", lambda e, b=b, c=c, tb=tb, r=r: e.matmul(ps[r][:, 0:256], lhsT=ht[b][:, c, tb * 128:(tb + 1) * 128],
                                                                     rhs=W[:, c, 1792:2048], start=(c == 0), stop=(c == KC - 1)),
                      [("W", 7), ("ht", b)], [("ps", r)])
            P.add("dve", lambda e, r=r, b=b, tb=tb: e.tensor_copy(out=vo[b][:, tb, :], in_=ps[r][:, 0:256]), [("ps", r)], [("vo", b)])
        P.dma("pool", vtv[:, tt * (TT // 128):(tt + 1) * (TT // 128), :], vo[b][:], [("vo", b)], [("vtok", tt)])
    S.finish()


def attn_consts(j):
    m = 2.0 ** (-8.0 * (j + 1) / 8.0)
    p = np.arange(128, dtype=np.float64)[:, None]
    n = np.arange(128, dtype=np.float64)[None, :]
    BL = -m * (128.0 * n - p)
    BR = -m * (p + 128.0 * n + 1.0)
    f = np.arange(256, dtype=np.float64)[None, :]
    Dg = [-m * np.abs(f - p - 128.0 * a) for a in range(2)]
    qb = np.arange(2, dtype=np.float64)[None, :]
    fL = np.exp(-m * (p + 128.0 * qb))
    fR = np.exp(-m * (255.0 - p - 128.0 * qb))
    return np.concatenate([BL, BR, Dg[0], Dg[1], fL, fR], axis=1).astype(np.float32)


def stage_attn(nc, projT, vtok, acst_d, lam_d, gn_d, idb_d, oT_rows, lamc_d, ntok=T, dbg=None):
    S = Stage(nc, "att")
    P = S.P
    NKB = ntok // 128
    NQG = ntok // 256
    kT = S.sb([128, 2, ntok], BF16)
    Vt = S.sb([128, NKB, 258], BF16)
    acst = S.sb([128, 772])
    BLm = S.sb([128, 128])
    BRm = S.sb([128, 128])
    lam4 = S.sb([128, 4])
    gnb = S.sb([128, 256])
    idb = S.sb([128, 128], BF16)
    onesf = S.sb([128, 128])
    qt = [S.sb([128, 2, 256], BF16) for _ in range(3)]
    PT = [S.sb([128, 2, 256], BF16) for _ in range(4)]
    sd = [S.sb([128, 2, 256]) for _ in range(2)]
    Osb = S.sb([128, 4, 257])
    sqc = [S.sb([128, 512], BF16) for _ in range(2)]
    onesb = S.sb([128, 128], BF16)
    lhl = S.sb([128, 4], BF16)
    vin = [S.sb([128, 2, 512], BF16) for _ in range(2)]
    mxs = S.sb([128, 2, 2 * (ntok // 512)])
    sm = S.sb([128, 16])
    res = S.sb([128, 256])
    junk = S.sb([128, 256])
    yb = S.sb([128, 256], BF16)
    oTs = [S.sb([128, 2, 256], BF16) for _ in range(2)]
    Sp = [S.ps() for _ in range(3)]
    Op = [S.ps() for _ in range(4)]
    Tp = S.ps([128, 1024], BF16)

    qv = projT[1280:1536, :].rearrange("(m d) t -> d m t", d=128)
    kv = projT[1536:1792, :].rearrange("(m d) t -> d m t", d=128)
    lamc = S.sb([128, 2])
    P.dma("sp", lamc[:], lamc_d, [], ["lamc"])
    P.dma("sp", acst[:], acst_d, [], ["acst"])
    P.dma("sp", lam4[:], lam_d, [], ["lam4"])
    P.dma("sp", gnb[:], gn_d, [], ["gnb"])
    P.dma("sp", idb[:], idb_d, [], ["idb"])
    P.dma("sp", kT[:], kv, [], ["kT"])
    P.add("dve", lambda e: e.memset(onesf[:], 1.0), [], ["onesf"])
    P.add("dve", lambda e: e.memset(onesb[:], 1.0), [], ["onesf"])
    P.add("dve", lambda e: e.memset(sm[:], 0.0), [], ["sm"])
    P.add("dve", lambda e: e.memset(sm[:, 15:16], EPS), ["sm"], ["sm"])
    P.add("dve", lambda e: e.memset(Vt[:, :, 256:258], 1.0), [], ["Vt1"])
    P.add("dve", lambda e: e.tensor_scalar(out=gnb[:], in0=gnb[:], scalar1=lamc[:, 0:1], scalar2=None, op0=ALU.mult), ["gnb", "lamc"], ["gnb"])
    if DBG_PREP[0] == 1:
        S.finish()
        return
    nch = ntok // 512
    k = 0
    for which, src in ((0, qv), (1, kv)):
        for m in range(2):
            for ch in range(nch):
                b = k % 2
                k += 1
                if which == 0:
                    P.dma("sp", vin[b][:, 0, :], src[:, m, ch * 512:(ch + 1) * 512], [], [("vin", b)])
                    P.add("act", lambda e, b=b: e.activation(out=sqc[b][:], in_=vin[b][:, 0, :], func=AF.Square), [("vin", b)], [("sqc", b)])
                else:
                    P.add("act", lambda e, b=b, m=m, ch=ch: e.activation(out=sqc[b][:], in_=kT[:, m, ch * 512:(ch + 1) * 512], func=AF.Square),
                          ["kT"], [("sqc", b)])
                P.add("pe", lambda e, b=b: e.matmul(Sp[b][:, 0:512], lhsT=onesb[:], rhs=sqc[b][:], start=True, stop=True),
                      [("sqc", b), "onesf"], [("S", b)])
                P.add("dve", lambda e, b=b, which=which, m=m, ch=ch: e.reduce_max(out=mxs[:, which, m * nch + ch:m * nch + ch + 1],
                                                                                 in_=Sp[b][:, 0:512], axis=AX.X),
                      [("S", b)], ["mxs"])
    P.add("dve", lambda e: e.reduce_max(out=sm[:, 0:2], in_=mxs[:], axis=AX.X), ["mxs"], ["sm"])
    P.add("dve", lambda e: e.tensor_tensor(out=sm[:, 2:3], in0=sm[:, 0:1], in1=sm[:, 1:2], op=ALU.mult), ["sm"], ["sm"])
    P.add("act", lambda e: e.activation(out=sm[:, 3:4], in_=sm[:, 2:3], func=AF.Sqrt, scale=1.1), ["sm"], ["sm"])
    P.add("dve", lambda e: e.tensor_scalar(out=sm[:, 4:5], in0=sm[:, 3:4], scalar1=-1.0, scalar2=None, op0=ALU.mult), ["sm"], ["sm"])
    P.add("dve", lambda e: e.tensor_scalar(out=BLm[:], in0=acst[:, 0:128], scalar1=sm[:, 3:4], scalar2=None, op0=ALU.subtract),
          ["acst", "sm"], ["BLm"])
    P.add("dve", lambda e: e.tensor_scalar(out=BRm[:], in0=acst[:, 128:256], scalar1=sm[:, 3:4], scalar2=None, op0=ALU.subtract),
          ["acst", "sm"], ["BRm"])
    P.add("dve", lambda e: e.tensor_tensor(out=sm[:, 5:6], in0=lam4[:, 0:1], in1=lam4[:, 1:2], op=ALU.mult), ["lam4", "sm"], ["sm"])
    P.add("dve", lambda e: e.tensor_tensor(out=sm[:, 6:7], in0=lam4[:, 2:3], in1=lam4[:, 3:4], op=ALU.mult), ["lam4", "sm"], ["sm"])
    P.add("dve", lambda e: e.tensor_copy(out=lhl[:, 0:2], in_=sm[:, 5:7]), ["sm"], ["lhl"])
    P.add("dve", lambda e: e.tensor_tensor(out=sm[:, 9:11], in0=sm[:, 5:7], in1=lhl[:, 0:2], op=ALU.subtract), ["sm", "lhl"], ["sm"])
    P.add("dve", lambda e: e.tensor_copy(out=lhl[:, 2:4], in_=sm[:, 9:11]), ["sm", "lhl"], ["lhl"])
    P.add("pe", lambda e: e.matmul(Sp[2][:, 0:4], lhsT=onesb[:], rhs=lhl[:], start=True, stop=True), ["lhl", "onesf"], [("S", 2)])
    P.add("dve", lambda e: e.tensor_copy(out=sm[:, 9:11], in_=Sp[2][:, 2:4]), [("S", 2), "sm"], ["sm"])
    P.add("dve", lambda e: e.tensor_tensor(out=sm[:, 5:7], in0=Sp[2][:, 0:2], in1=sm[:, 9:11], op=ALU.add), [("S", 2), "sm"], ["sm"])
    P.add("act", lambda e: e.activation(out=sm[:, 9:11], in_=sm[:, 5:7], func=AF.Exp), ["sm"], ["sm"])
    P.add("dve", lambda e: e.tensor_tensor(out=sm[:, 8:9], in0=sm[:, 10:11], in1=sm[:, 9:10], op=ALU.subtract), ["sm"], ["sm"])
    P.add("dve", lambda e: e.tensor_scalar(out=sm[:, 8:9], in0=sm[:, 8:9], scalar1=lamc[:, 1:2], scalar2=None, op0=ALU.add), ["sm", "lamc"], ["sm"])
    if DBG_PREP[0] == 2:
        S.finish()
        return
    vtv = vtok.rearrange("(n p) d -> p n d", p=128)
    VG = max(1, NKB // 4)
    for g4 in range(NKB // VG):
        P.dma("sp", Vt[:, g4 * VG:(g4 + 1) * VG, 0:256], vtv[:, g4 * VG:(g4 + 1) * VG, :], [], [("Vt", g4)])
    steps = []
    for qg in (range(NQG) if dbg is None else dbg):
        sides = []
        if qg > 0:
            sides.append(("L", list(range(0, 2 * qg))))
        sides.append(("C", [2 * qg, 2 * qg + 1]))
        if 2 * qg + 2 < NKB:
            sides.append(("R", list(range(2 * qg + 2, NKB))))
        for si, (side, kbs) in enumerate(sides):
            for ki, kb in enumerate(kbs):
                steps.append((qg, si, side, kb, ki, len(kbs), len(sides)))
    yb4 = [S.sb([128, 256], BF16) for _ in range(4)]
    LA = 2
    deferred = []

    def emit_front(t):
        qg, si, side, kb, ki, nk, ns = steps[t]
        qb_ = qg % 3
        r = t % 3
        r2 = t % 4
        if si == 0 and ki == 0:
            P.dma("sp", qt[qb_][:], qv[:, :, qg * 256:(qg + 1) * 256], [], [("qt", qb_)])
        for m in range(2):
            P.add("pe", lambda e, r=r, m=m, kb=kb, qb_=qb_: e.matmul(Sp[r][:, m * 256:(m + 1) * 256], lhsT=kT[:, m, kb * 128:(kb + 1) * 128],
                                                                   rhs=qt[qb_][:, m, :], start=True, stop=True),
                  ["kT", ("qt", qb_)], [("S", r)])
        if side == "L":
            n = 2 * qg - kb
            P.add("act", lambda e, r=r, r2=r2, n=n: e.activation(out=PT[r2][:].rearrange("p m q -> p (m q)"), in_=Sp[r][:, 0:512], func=AF.Exp,
                                                                 bias=BLm[:, n:n + 1], scale=1.0),
                  [("S", r), "BLm"], [("PT", r2)])
        elif side == "R":
            n = kb - 2 * qg - 2
            P.add("act", lambda e, r=r, r2=r2, n=n: e.activation(out=PT[r2][:].rearrange("p m q -> p (m q)"), in_=Sp[r][:, 0:512], func=AF.Exp,
                                                                 bias=BRm[:, n:n + 1], scale=1.0),
                  [("S", r), "BRm"], [("PT", r2)])
        else:
            a = kb - 2 * qg
            d_ = ki % 2
            for m in range(2):
                P.add("dve", lambda e, r=r, m=m, a=a, d_=d_: e.tensor_tensor(out=sd[d_][:, m, :], in0=Sp[r][:, m * 256:(m + 1) * 256],
                                                                             in1=acst[:, 256 + a * 256:512 + a * 256], op=ALU.add),
                      [("S", r), "acst"], [("sd", d_)])
            P.add("act", lambda e, r2=r2, d_=d_: e.activation(out=PT[r2][:].rearrange("p m q -> p (m q)"), in_=sd[d_][:].rearrange("p m q -> p (m q)"),
                                                              func=AF.Exp, bias=sm[:, 4:5], scale=1.0),
                  [("sd", d_), "sm"], [("PT", r2)])

    def emit_back(t):
        qg, si, side, kb, ki, nk, ns = steps[t]
        r2 = t % 4
        for qb in range(2):
            for m in range(2):
                i = qb * 2 + m
                P.add("pe", lambda e, i=i, r2=r2, m=m, qb=qb, kb=kb, ki=ki, nk=nk: e.matmul(
                    Op[i][:, 0:257], lhsT=PT[r2][:, m, qb * 128:(qb + 1) * 128], rhs=Vt[:, kb, 0:257],
                    start=(ki == 0), stop=(ki == nk - 1)),
                    [("PT", r2), ("Vt", kb // VG), "Vt1"], [("O", i)])
        if ki != nk - 1:
            return
        for i in range(4):
            qb = i // 2
            if si == 0:
                if side == "L":
                    P.add("dve", lambda e, i=i, qb=qb: e.tensor_scalar(out=Osb[:, i, :], in0=Op[i][:, 0:257], scalar1=acst[:, 768 + qb:769 + qb],
                                                                       scalar2=None, op0=ALU.mult), [("O", i), "acst"], ["Osb"])
                else:
                    P.add("dve", lambda e, i=i: e.tensor_copy(out=Osb[:, i, :], in_=Op[i][:, 0:257]), [("O", i)], ["Osb"])
            elif side == "C":
                P.add("dve", lambda e, i=i: e.tensor_tensor(out=Osb[:, i, :], in0=Osb[:, i, :], in1=Op[i][:, 0:257], op=ALU.add),
                      [("O", i), "Osb"], ["Osb"])
            else:
                P.add("dve", lambda e, i=i, qb=qb: e.scalar_tensor_tensor(out=Osb[:, i, :], in0=Op[i][:, 0:257], scalar=acst[:, 770 + qb:771 + qb],
                                                                          in1=Osb[:, i, :], op0=ALU.mult, op1=ALU.add),
                      [("O", i), "Osb", "acst"], ["Osb"])
        if si != ns - 1:
            return
        ob = qg % 2
        for qb in range(2):
            y_ = yb4[ob * 2 + qb]
            yk = ("yb", ob * 2 + qb)
            P.add("dve", lambda e, qb=qb: e.reciprocal(out=sm[:, 11:12], in_=Osb[:, 2 * qb, 256:257]), ["Osb", "sm"], ["sm"])
            P.add("dve", lambda e, qb=qb: e.reciprocal(out=sm[:, 12:13], in_=Osb[:, 2 * qb + 1, 256:257]), ["Osb", "sm"], ["sm"])
            P.add("dve", lambda e: e.tensor_tensor(out=sm[:, 12:13], in0=sm[:, 12:13], in1=sm[:, 8:9], op=ALU.mult), ["sm"], ["sm"])
            P.add("dve", lambda e, qb=qb: e.tensor_scalar(out=res[:], in0=Osb[:, 2 * qb, 0:256], scalar1=sm[:, 11:12], scalar2=None, op0=ALU.mult),
                  ["Osb", "sm"], ["res"])
            P.add("dve", lambda e, qb=qb: e.scalar_tensor_tensor(out=res[:], in0=Osb[:, 2 * qb + 1, 0:256], scalar=sm[:, 12:13], in1=res[:],
                                                                 op0=ALU.mult, op1=ALU.add), ["Osb", "sm", "res"], ["res"])
            P.add("dve", lambda e: e.memset(sm[:, 13:14], 0.0), ["sm"], ["sm"])
            P.add("act", lambda e: e.activation(out=junk[:], in_=res[:], func=AF.Square, accum_out=sm[:, 13:14]), ["res", "sm"], ["junk", "sm"])
            P.add("act", lambda e: e.activation(out=sm[:, 14:15], in_=sm[:, 13:14], func=AF.Sqrt, bias=sm[:, 15:16], scale=1.0 / 256.0),
                  ["sm"], ["sm"])
            P.add("dve", lambda e: e.reciprocal(out=sm[:, 14:15], in_=sm[:, 14:15]), ["sm"], ["sm"])
            P.add("dve", lambda e, y_=y_: e.scalar_tensor_tensor(out=y_[:], in0=res[:], scalar=sm[:, 14:15], in1=gnb[:], op0=ALU.mult, op1=ALU.mult),
                  ["res", "sm", "gnb"], [yk])

        def tail(qg=qg, ob=ob):
            for qb in range(2):
                y_ = yb4[ob * 2 + qb]
                yk = ("yb", ob * 2 + qb)
                tslot = 512 * 0 + qb * 256
                for dc in range(2):
                    P.add("pe", lambda e, dc=dc, y_=y_, tslot=tslot: e.transpose(out=Tp[:, tslot + dc * 128:tslot + (dc + 1) * 128],
                                                                                 in_=y_[:, dc * 128:(dc + 1) * 128], identity=idb[:]),
                          [yk, "idb"], ["Tpbank"])
                P.add("act", lambda e, ob=ob, qb=qb, tslot=tslot: e.copy(out=oTs[ob][:, :, qb * 128:(qb + 1) * 128],
                                                                         in_=Tp[:, tslot:tslot + 256].rearrange("p (c q) -> p c q", c=2)),
                      ["Tpbank"], [("oTs", ob)])
            P.dma("pool", oT_rows.rearrange("(c p) t -> p c t", p=128)[:, :, qg * 256:(qg + 1) * 256], oTs[ob][:], [("oTs", ob)], [("oT", qg)])
        deferred.append((t + 10, tail))

    ns_ = len(steps)
    for t in range(ns_ + LA):
        if t < ns_:
            emit_front(t)
        if t >= LA:
            emit_back(t - LA)
        while deferred and deferred[0][0] <= t:
            deferred.pop(0)[1]()
    while deferred:
        deferred.pop(0)[1]()
    S.finish()


def hgrn_consts(SEG=512):
    cm = np.ones((128, SEG), np.float32)
    cm[:, 0::HCH] = 0.0
    s = np.arange(HCH)[:, None]
    t = np.arange(HCH)[None, :]
    mk = np.zeros((128, 2 * HCH), np.float32)
    mk[0:HCH, 0:HCH] = (s <= t)
    mk[0:HCH, HCH:2 * HCH] = (s >= t)
    return np.concatenate([cm, mk], axis=1)


def stage_hgrn(nc, projT, lbo_d, l, hcst_d, gnh_d, idb_d, o_scr, oT_rows, ntok=T, SEG=512):
    S = Stage(nc, "hg")
    P = S.P
    nseg = ntok // SEG
    nch = SEG // HCH
    hcst = S.sb([128, SEG + 2 * HCH])
    idb = S.sb([128, 128], BF16)
    lbt = S.sb([128, 2, 2, DEPTH, 16])
    noml = S.sb([128, 2, DEPTH, 16])
    gnh = S.sb([128, 1])
    onesn = S.sb([128, 128])
    epst = S.sb([128, 1])
    S32 = [S.sb([128, 128]) for _ in range(2)]
    Sbf = [S.sb([128, 128], BF16) for _ in range(2)]
    qin = [S.sb([128, 2, SEG], BF16) for _ in range(2)]
    zin = [S.sb([128, 2, SEG], BF16) for _ in range(2)]
    vin = [S.sb([128, 2, SEG], BF16) for _ in range(2)]
    gin = [S.sb([128, 2, SEG], BF16) for _ in range(2)]
    ofw = [S.sb([128, 2, SEG]) for _ in range(2)]
    sig = [S.sb([128, SEG]) for _ in range(2)]
    g = [S.sb([128, SEG]) for _ in range(2)]
    kc = [S.sb([128, SEG]) for _ in range(2)]
    bb = [S.sb([128, SEG]) for _ in range(2)]
    arg = [S.sb([128, SEG]) for _ in range(2)]
    cq = [S.sb([128, SEG]) for _ in range(2)]
    ck = [S.sb([128, SEG]) for _ in range(2)]
    ex = [S.sb([128, SEG]) for _ in range(2)]
    Qt = [[S.sb([128, SEG], BF16) for _ in range(2)] for _ in range(2)]
    Kt = [[S.sb([128, SEG], BF16) for _ in range(2)] for _ in range(2)]
    Kh = [[S.sb([128, SEG], BF16) for _ in range(2)] for _ in range(2)]
    ebl = [[S.sb([128, nch]) for _ in range(2)] for _ in range(2)]
    Am = [S.sb([HCH, HCH], BF16) for _ in range(4)]
    VK = [S.sb([HCH, 256], BF16) for _ in range(4)]
    osb = [S.sb([128, SEG]) for _ in range(2)]
    ysb = [S.sb([128, SEG]) for _ in range(2)]
    yb = [S.sb([128, SEG], BF16) for _ in range(2)]
    Ap2 = [S.ps() for _ in range(2)]
    Dp2 = [S.ps() for _ in range(2)]
    Tp2 = [S.ps([128, 1024], BF16) for _ in range(2)]
    Oa1 = [S.ps() for _ in range(2)]

    P.dma("sp", hcst[:], hcst_d, [], ["hcst"])
    P.dma("sp", idb[:], idb_d, [], ["idb"])
    P.dma("sp", lbt[:], lbo_d, [], ["lbt"])
    P.dma("sp", gnh[:], gnh_d, [], ["gnh"])
    P.add("dve", lambda e: e.tensor_scalar(out=noml[:], in0=lbt[:, 1], scalar1=-1.0, scalar2=None, op0=ALU.mult), ["lbt"], ["noml"])
    P.add("dve", lambda e: e.memset(onesn[:], 1.0 / 128.0), [], ["onesn"])
    P.add("dve", lambda e: e.memset(epst[:], EPS), [], ["eps"])
    cm = hcst[:, 0:SEG]
    pq = projT[0:256, :].rearrange("(a k) t -> k a t", k=128)
    pv = projT[768:1024, :].rearrange("(a k) t -> k a t", k=128)
    pg = projT[1024:1280, :].rearrange("(a k) t -> k a t", k=128)
    osv = o_scr.rearrange("(a k) t -> k a t", k=128)
    oTv = oT_rows.rearrange("(a k) t -> k a t", k=128)
    kk = 0
    for dirn in range(2):
        pz = projT[256 + 256 * dirn:512 + 256 * dirn, :].rearrange("(a k) t -> k a t", k=128)
        mask = hcst[0:HCH, SEG + dirn * HCH:SEG + (dirn + 1) * HCH]
        for a in range(2):
            P.add("dve", lambda e, a=a: e.memset(S32[a][:], 0.0), [("S32", a)], [("S32", a)])
            P.add("dve", lambda e, a=a: e.memset(Sbf[a][:], 0.0), [("Sbf", a)], [("Sbf", a)])
        segs = list(range(nseg)) if dirn == 0 else list(range(nseg - 1, -1, -1))

        def loads(si, dirn=dirn, pz=pz, segs=segs):
            sb_ = si % 2
            ts = slice(segs[si] * SEG, (segs[si] + 1) * SEG)
            P.dma("sp", qin[sb_][:], pq[:, :, ts], [], [("qin", sb_)])
            P.dma("sp", zin[sb_][:], pz[:, :, ts], [], [("zin", sb_)])
            P.dma("sp", vin[sb_][:], pv[:, :, ts], [], [("vin", sb_)])
            if dirn == 1:
                P.dma("sp", gin[sb_][:], pg[:, :, ts], [], [("gin", sb_)])
                P.dma("sp", ofw[sb_][:], osv[:, :, ts], [("oscr", segs[si])], [("ofw", sb_)])

        def prep(si, dirn=dirn):
            sb_ = si % 2
            pl = []
            A_ = lambda q, fn, r, w: pl.append((q, fn, r, w))
            for a in range(2):
                lb_ap = lbt[:, 0, dirn, l, a:a + 1]
                oml_ap = lbt[:, 1, dirn, l, a:a + 1]
                noml_ap = noml[:, dirn, l, a:a + 1]
                A_("act", lambda e, a=a, sb_=sb_: e.activation(out=sig[a][:], in_=zin[sb_][:, a, :], func=AF.Sigmoid), [("zin", sb_)], [("sig", a)])
                A_("act", lambda e, a=a, lb_ap=lb_ap, oml_ap=oml_ap: e.activation(out=g[a][:], in_=sig[a][:], func=AF.Ln, bias=lb_ap, scale=oml_ap),
                   [("sig", a), "lbt"], [("g", a)])
                A_("dve", lambda e, a=a, noml_ap=noml_ap, oml_ap=oml_ap: e.tensor_scalar(out=kc[a][:], in0=sig[a][:], scalar1=noml_ap, scalar2=oml_ap,
                                                                                       op0=ALU.mult, op1=ALU.add), [("sig", a), "noml", "lbt"], [("kc", a)])
                A_("dve", lambda e, a=a: e.tensor_tensor_scan(out=bb[a][:], data0=cm, data1=g[a][:], initial=0.0, op0=ALU.mult, op1=ALU.add),
                   [("g", a), "hcst"], [("bb", a)])
                b3 = bb[a][:].rearrange("p (n c) -> p n c", c=HCH)
                A_("dve", lambda e, a=a, b3=b3: e.tensor_tensor(out=arg[a][:].rearrange("p (n c) -> p n c", c=HCH),
                                                                in0=b3[:, :, HCH - 1:HCH].to_broadcast([128, nch, HCH]), in1=b3, op=ALU.subtract),
                   [("bb", a)], [("arg", a)])
                A_("act", lambda e, a=a, b3=b3, sb_=sb_: e.activation(out=ebl[a][sb_][:], in_=b3[:, :, HCH - 1], func=AF.Exp), [("bb", a)], [("ebl", a, sb_)])
                if dirn == 0:
                    cq_ap, ck_ap = bb[a], arg[a]
                    kq, kk_ = ("bb", a), ("arg", a)
                else:
                    A_("dve", lambda e, a=a: e.tensor_tensor(out=cq[a][:], in0=arg[a][:], in1=g[a][:], op=ALU.add), [("arg", a), ("g", a)], [("cq", a)])
                    A_("dve", lambda e, a=a: e.tensor_tensor(out=ck[a][:], in0=bb[a][:], in1=g[a][:], op=ALU.subtract), [("bb", a), ("g", a)], [("ck", a)])
                    cq_ap, ck_ap = cq[a], ck[a]
                    kq, kk_ = ("cq", a), ("ck", a)
                A_("act", lambda e, cq_ap=cq_ap: e.activation(out=ex[0][:], in_=cq_ap[:], func=AF.Exp), [kq], [("ex", 0)])
                A_("dve", lambda e, a=a, sb_=sb_: e.tensor_tensor(out=Qt[a][sb_][:], in0=qin[sb_][:, a, :], in1=ex[0][:], op=ALU.mult),
                   [("ex", 0), ("qin", sb_)], [("Qt", a, sb_)])
                A_("act", lambda e, cq_ap=cq_ap: e.activation(out=ex[1][:], in_=cq_ap[:], func=AF.Exp, scale=-1.0), [kq], [("ex", 1)])
                A_("dve", lambda e, a=a, sb_=sb_: e.tensor_tensor(out=Kt[a][sb_][:], in0=kc[a][:], in1=ex[1][:], op=ALU.mult), [("ex", 1), ("kc", a)], [("Kt", a, sb_)])
                A_("act", lambda e, ck_ap=ck_ap: e.activation(out=ex[0][:], in_=ck_ap[:], func=AF.Exp), [kk_, ("ex", 0)], [("ex", 0)])
                A_("dve", lambda e, a=a, sb_=sb_: e.tensor_tensor(out=Kh[a][sb_][:], in0=kc[a][:], in1=ex[0][:], op=ALU.mult), [("ex", 0), ("kc", a)], [("Kh", a, sb_)])
            return pl

        loads(0)
        for op_ in prep(0):
            P.add(*op_)
        for si, seg in enumerate(segs):
            sb_ = si % 2
            ts = slice(seg * SEG, (seg + 1) * SEG)
            pending = []
            if si + 1 < nseg:
                loads(si + 1)
                pending = prep(si + 1)
            chunks = list(range(nch)) if dirn == 0 else list(range(nch - 1, -1, -1))
            hsteps = [(n, a) for n in chunks for a in range(2)]

            def h_front(t, kk0=kk, hsteps=hsteps, sb_=sb_, mask=mask):
                n, a = hsteps[t]
                cs = slice(n * HCH, (n + 1) * HCH)
                r = (kk0 + t) % 4
                rb = (kk0 + t) % 2
                Ap, Tp = Ap2[rb], Tp2[rb]
                P.add("pe", lambda e, a=a, Ap=Ap, cs=cs: e.matmul(Ap[0:HCH, 0:HCH], lhsT=Kt[a][sb_][:, cs], rhs=Qt[a][sb_][:, cs], start=True, stop=True),
                      [("Kt", a, sb_), ("Qt", a, sb_)], [("Ap", rb)])
                P.add("dve", lambda e, r=r, Ap=Ap, mask=mask: e.tensor_tensor(out=Am[r][:], in0=Ap[0:HCH, 0:HCH], in1=mask, op=ALU.mult),
                      [("Ap", rb), "hcst"], [("Am", r)])
                P.add("pe", lambda e, a=a, Tp=Tp, cs=cs: e.transpose(out=Tp[0:HCH, 0:128], in_=vin[sb_][:, a, cs], identity=idb[:]),
                      [("vin", sb_), "idb"], [("Tp", rb)])
                P.add("pe", lambda e, a=a, Tp=Tp, cs=cs: e.transpose(out=Tp[0:HCH, 128:256], in_=Kh[a][sb_][:, cs], identity=idb[:]),
                      [("Kh", a, sb_), "idb"], [("Tp", rb)])
                P.add("act", lambda e, r=r, Tp=Tp: e.copy(out=VK[r][:], in_=Tp[0:HCH, 0:256]), [("Tp", rb)], [("VK", r)])

            def h_back(t, kk0=kk, hsteps=hsteps, sb_=sb_):
                n, a = hsteps[t]
                cs = slice(n * HCH, (n + 1) * HCH)
                r = (kk0 + t) % 4
                rb = (kk0 + t) % 2
                Dp = Dp2[rb]
                P.add("pe", lambda e, a=a, cs=cs: e.matmul(Oa1[a][:, cs], lhsT=Sbf[a][:], rhs=Qt[a][sb_][:, cs], start=True, stop=False),
                      [("Sbf", a), ("Qt", a, sb_)], [("Oa", a)])
                P.add("pe", lambda e, a=a, r=r, cs=cs: e.matmul(Oa1[a][:, cs], lhsT=VK[r][:, 0:128], rhs=Am[r][:], start=False, stop=True),
                      [("VK", r), ("Am", r)], [("Oa", a)])
                P.add("pe", lambda e, r=r, Dp=Dp: e.matmul(Dp[:, 0:128], lhsT=VK[r][:, 128:256], rhs=VK[r][:, 0:128], start=True, stop=True),
                      [("VK", r)], [("Dp", rb)])
                P.add("dve", lambda e, a=a, Dp=Dp, n=n: e.scalar_tensor_tensor(out=S32[a][:], in0=S32[a][:], scalar=ebl[a][sb_][:, n:n + 1],
                                                                              in1=Dp[:, 0:128], op0=ALU.mult, op1=ALU.add),
                      [("S32", a), ("ebl", a, sb_), ("Dp", rb)], [("S32", a)])
                P.add("act", lambda e, a=a: e.copy(out=Sbf[a][:], in_=S32[a][:]), [("S32", a)], [("Sbf", a)])

            HLA = 2
            for t in range(len(hsteps) + HLA):
                if t < len(hsteps):
                    h_front(t)
                if t >= HLA:
                    h_back(t - HLA)
                if pending:
                    P.add(*pending.pop(0))
            while pending:
                P.add(*pending.pop(0))
            kk += len(hsteps)
            for a in range(2):
                if dirn == 0:
                    P.add("dve", lambda e, a=a: e.tensor_copy(out=osb[a][:], in_=Oa1[a][:, 0:SEG]), [("Oa", a)], [("osb", a)])
                    P.dma("pool", osv[:, a, ts], osb[a][:], [("osb", a)], [("oscr", seg)])
                else:
                    P.add("dve", lambda e, a=a, sb_=sb_: e.tensor_tensor(out=osb[a][:], in0=Oa1[a][:, 0:SEG], in1=ofw[sb_][:, a, :], op=ALU.add),
                          [("Oa", a), ("ofw", sb_)], [("osb", a)])
                    P.add("act", lambda e, a=a: e.activation(out=ysb[a][:], in_=osb[a][:], func=AF.Square), [("osb", a)], [("ysb", a)])
                    P.add("pe", lambda e, a=a: e.matmul(Dp2[0][:, 0:SEG], lhsT=onesn[:], rhs=ysb[a][:], start=True, stop=True), [("ysb", a), "onesn"], [("Dp", 0)])
                    P.add("act", lambda e, a=a: e.activation(out=ysb[a][:], in_=Dp2[0][:, 0:SEG], func=AF.Sqrt, bias=epst[:, 0:1], scale=1.0),
                          [("Dp", 0), "eps"], [("ysb", a)])
                    P.add("dve", lambda e, a=a: e.reciprocal(out=ysb[a][:], in_=ysb[a][:]), [("ysb", a)], [("ysb", a)])
                    P.add("dve", lambda e, a=a: e.tensor_tensor(out=osb[a][:], in0=osb[a][:], in1=ysb[a][:], op=ALU.mult), [("osb", a), ("ysb", a)], [("osb", a)])
                    P.add("act", lambda e, a=a, sb_=sb_: e.activation(out=ysb[a][:], in_=gin[sb_][:, a, :], func=AF.Silu), [("gin", sb_), ("ysb", a)], [("ysb", a)])
                    P.add("dve", lambda e, a=a: e.scalar_tensor_tensor(out=yb[a][:], in0=osb[a][:], scalar=gnh[:, 0:1], in1=ysb[a][:], op0=ALU.mult, op1=ALU.mult),
                          [("osb", a), ("ysb", a), "gnh"], [("yb", a)])
                    P.dma("pool", oTv[:, a, ts], yb[a][:], [("yb", a)], [("oT", seg, a)])
    S.finish()


def stage_outproj(nc, oT_tok, w_out, xT_in, xT_out, par, ig, ntok=TL, TT=512):
    S = Stage(nc, "op")
    P = S.P
    G = S.sb([128, 32])
    oTr = S.sb([128, KC, ntok], BF16)
    Wg = [S.sb([128, KC, 128], BF16) for _ in range(2)]
    xg = [S.sb([128, ntok]) for _ in range(2)]
    ps = [S.ps() for _ in range(4)]
    P.dma("sp", G[:], par[:, ig, :], [], ["G"])
    ov = oT_tok.rearrange("(c p) t -> p c t", p=128)
    for q4 in range(4):
        P.dma("sp", oTr[:, q4 * 8:(q4 + 1) * 8, :], ov[:, q4 * 8:(q4 + 1) * 8, :], [], [("oTr", q4)])
    wv = w_out.rearrange("(c p) n -> p c n", p=128)
    k = 0
    for cg in range(D // 128):
        b = cg % 2
        P.dma("pool", Wg[b][:], wv[:, :, cg * 128:(cg + 1) * 128], [], [("Wg", b)])
        P.dma("sp", xg[b][:], xT_in[cg * 128:(cg + 1) * 128, :], [], [("xg", b)])
        for tt in range(ntok // TT):
            r = k % 4
            k += 1
            for c in range(KC):
                P.add("pe", lambda e, b=b, c=c, r=r, tt=tt: e.matmul(ps[r][:, 0:TT], lhsT=Wg[b][:, c, :], rhs=oTr[:, c, tt * TT:(tt + 1) * TT],
                                                                     start=(c == 0), stop=(c == KC - 1)),
                      [("Wg", b), ("oTr", c // 8)], [("ps", r)])
            P.add("dve", lambda e, b=b, r=r, tt=tt, cg=cg: e.scalar_tensor_tensor(out=xg[b][:, tt * TT:(tt + 1) * TT], in0=ps[r][:, 0:TT], scalar=G[:, cg:cg + 1],
                                                                                  in1=xg[b][:, tt * TT:(tt + 1) * TT], op0=ALU.mult, op1=ALU.add),
                  [("ps", r), "G", ("xg", b)], [("xg", b)])
        P.dma("sp", xT_out[cg * 128:(cg + 1) * 128, :], xg[b][:], [("xg", b)], [("xTo", cg)])
    S.finish()


def stage_ffn_up(nc, h2T_halo, w_up, cwp_d, mT, ntok=TL, TT=512):
    S = Stage(nc, "up")
    P = S.P
    NH = ntok + 2
    h2 = S.sb([128, KC, NH], BF16)
    Wg = [S.sb([128, KC, 256], BF16) for _ in range(2)]
    u2 = [S.sb([128, 2, NH]) for _ in range(2)]
    cwp = S.sb([128, 4, 128])
    HT = 512
    tA = [S.sb([128, HT]) for _ in range(2)]
    sg = S.sb([128, HT])
    mo = [S.sb([128, ntok], BF16)]
    ps = [S.ps() for _ in range(4)]
    P.dma("sp", cwp[:], cwp_d, [], ["cwp"])
    hv = h2T_halo.rearrange("(c p) t -> p c t", p=128)
    for q4 in range(4):
        P.dma("sp", h2[:, q4 * 8:(q4 + 1) * 8, :], hv[:, q4 * 8:(q4 + 1) * 8, :], [], [("h2", q4)])
    wv = w_up.rearrange("(c p) n -> p c n", p=128)
    mv = mT.rearrange("(g p) t -> p g t", p=128)
    k = 0
    for jg in range(DFF // 128):
        b = jg % 2
        u = u2[b]
        P.dma("pool", Wg[b][:, :, 0:128], wv[:, :, jg * 128:(jg + 1) * 128], [], [("Wg", b, 0)])
        P.dma("pool", Wg[b][:, :, 128:256], wv[:, :, DFF + jg * 128:DFF + (jg + 1) * 128], [], [("Wg", b, 1)])
        for gv in range(2):
            for ct in range(ntok // TT + 1):
                r = k % 4
                k += 1
                if ct < ntok // TT:
                    n = TT
                    rsl = slice(1 + ct * TT, 1 + (ct + 1) * TT)
                else:
                    n = 2
                    rsl = slice(0, NH, NH - 1)
                for c in range(KC):
                    P.add("pe", lambda e, b=b, c=c, r=r, gv=gv, n=n, rsl=rsl: e.matmul(ps[r][:, 0:n], lhsT=Wg[b][:, c, gv * 128:(gv + 1) * 128],
                                                                                       rhs=h2[:, c, rsl], start=(c == 0), stop=(c == KC - 1)),
                          [("Wg", b, gv), ("h2", c // 8)], [("ps", r)])
                if k % 2:
                    P.add("act", lambda e, r=r, gv=gv, n=n, rsl=rsl, u=u: e.copy(out=u[:, gv, rsl], in_=ps[r][:, 0:n]), [("ps", r)], [("u", b, gv)])
                else:
                    P.add("dve", lambda e, r=r, gv=gv, n=n, rsl=rsl, u=u: e.tensor_copy(out=u[:, gv, rsl], in_=ps[r][:, 0:n]), [("ps", r)], [("u", b, gv)])
        for th in range(ntok // HT):
            o = th * HT
            for gv in range(2):
                gi = gv * 64 + jg
                P.add("act", lambda e, gv=gv, gi=gi, o=o, u=u: e.activation(out=tA[gv][:], in_=u[:, gv, 1 + o:1 + o + HT], func=AF.Identity,
                                                                       bias=cwp[:, 3, gi:gi + 1], scale=cwp[:, 1, gi:gi + 1]),
                      [("u", b, gv), "cwp"], [("tA", gv)])
                P.add("dve", lambda e, gv=gv, gi=gi, o=o, u=u: e.scalar_tensor_tensor(out=tA[gv][:], in0=u[:, gv, o:o + HT], scalar=cwp[:, 0, gi:gi + 1],
                                                                                 in1=tA[gv][:], op0=ALU.mult, op1=ALU.add),
                      [("u", b, gv), "cwp", ("tA", gv)], [("tA", gv)])
                P.add("dve", lambda e, gv=gv, gi=gi, o=o, u=u: e.scalar_tensor_tensor(out=tA[gv][:], in0=u[:, gv, 2 + o:2 + o + HT], scalar=cwp[:, 2, gi:gi + 1],
                                                                                  in1=tA[gv][:], op0=ALU.mult, op1=ALU.add),
                      [("u", b, gv), "cwp", ("tA", gv)], [("tA", gv)])
            P.add("act", lambda e: e.activation(out=sg[:], in_=tA[0][:], func=AF.Silu), [("tA", 0)], ["sg"])
            P.add("dve", lambda e, o=o: e.tensor_tensor(out=mo[0][:, o:o + HT], in0=sg[:], in1=tA[1][:], op=ALU.mult),
                  ["sg", ("tA", 1)], [("mo", 0)])
        P.dma("sp", mv[:, jg, :], mo[0][:], [("mo", 0)], [("mT", jg)])
    S.finish()


def stage_ffn_down(nc, mT, w_down, xT_in, xT_out, par, ig, ntok=TL, TT=512):
    S = Stage(nc, "dn")
    P = S.P
    NK = DFF // 128
    TB = min(1024, ntok)
    G = S.sb([128, 32])
    mt = S.sb([128, NK, TB], BF16)
    Wg = [S.sb([128, NK, 128], BF16) for _ in range(2)]
    xg = [S.sb([128, TB]) for _ in range(3)]
    ps = [S.ps() for _ in range(4)]
    P.dma("sp", G[:], par[:, ig, :], [], ["G"])
    mv = mT.rearrange("(c p) t -> p c t", p=128)
    wv = w_down.rearrange("(c p) n -> p c n", p=128)
    k = 0
    kc_ = 0
    for th in range(ntok // TB):
        ts = slice(th * TB, (th + 1) * TB)
        for q4 in range(8):
            P.dma("sp", mt[:, q4 * 8:(q4 + 1) * 8, :], mv[:, q4 * 8:(q4 + 1) * 8, ts], [("mT", g) for g in range(q4 * 8, (q4 + 1) * 8)],
                  [("mt", q4)])
        for cg in range(D // 128):
            b = kc_ % 2
            x3 = kc_ % 3
            kc_ += 1
            P.dma("pool", Wg[b][:], wv[:, :, cg * 128:(cg + 1) * 128], [], [("Wg", b)])
            P.dma("sp", xg[x3][:], xT_in[cg * 128:(cg + 1) * 128, ts], [], [("xg", x3)])
            for ct in range(TB // TT):
                r = k % 4
                k += 1
                cs = slice(ct * TT, (ct + 1) * TT)
                for c in range(NK):
                    P.add("pe", lambda e, b=b, c=c, r=r, cs=cs: e.matmul(ps[r][:, 0:TT], lhsT=Wg[b][:, c, :], rhs=mt[:, c, cs],
                                                                         start=(c == 0), stop=(c == NK - 1)),
                          [("Wg", b), ("mt", c // 8)], [("ps", r)])
                P.add("dve", lambda e, r=r, x3=x3, cg=cg, cs=cs: e.scalar_tensor_tensor(out=xg[x3][:, cs], in0=ps[r][:, 0:TT], scalar=G[:, cg:cg + 1],
                                                                                        in1=xg[x3][:, cs], op0=ALU.mult, op1=ALU.add),
                      [("ps", r), "G", ("xg", x3)], [("xg", x3)])
            P.dma("sp", xT_out[cg * 128:(cg + 1) * 128, ts], xg[x3][:], [("xg", x3)], [("xTo", cg, th)])
    S.finish()


_PROGS = {}


def _dt(nc, n, s, d=F32, k="ExternalInput"):
    return nc.dram_tensor(n, list(s), d, kind=k).ap()


def prog_L0():
    if "L0" in _PROGS:
        return _PROGS["L0"]
    nc = bass.Bass("TRN2", target_bir_lowering=False)
    x = _dt(nc, "x", [TL, D]); vecs = _dt(nc, "vecs", [1280, 128]); w_ada = _dt(nc, "w_ada", [D, 6 * D])
    lbl = _dt(nc, "lbl", [128, 128]); idf = _dt(nc, "idf", [128, 128])
    par = _dt(nc, "par", [128, DEPTH * 6 + 1, 32], F32, "ExternalOutput")
    lbo = _dt(nc, "lbo", [128, 2, 2, DEPTH, 16], F32, "ExternalOutput")
    xT = _dt(nc, "xT", [D, TL], F32, "ExternalOutput")
    hT = _dt(nc, "hT", [D, TL], BF16, "ExternalOutput")
    stage_params(nc, vecs, w_ada, lbl, idf, par, lbo)
    stage_pre(nc, x, xT, idf)
    stage_norm(nc, xT, par, 0, 1, hT)
    _PROGS["L0"] = nc
    return nc


def prog_LB():
    if "LB" in _PROGS:
        return _PROGS["LB"]
    nc = bass.Bass("TRN2", target_bir_lowering=False)
    hT = _dt(nc, "hT", [D, T], BF16); w = _dt(nc, "w", [D, 2048]); acst = _dt(nc, "acst", [128, 772]); lam = _dt(nc, "lam", [128, 4])
    lamc = _dt(nc, "lamc", [128, 2]); gn = _dt(nc, "gn", [128, 256]); idb = _dt(nc, "idb", [128, 128], BF16)
    lbo = _dt(nc, "lbo", [128, 2, 2, DEPTH, 16]); hc = _dt(nc, "hc", [128, 512 + 64]); gnh = _dt(nc, "gnh", [128, 1])
    projT = _dt(nc, "projT", [2048, T], BF16, "Internal")
    oscr = _dt(nc, "oscr", [256, T], F32, "Internal")
    vtok = _dt(nc, "vtok", [T, 256], BF16, "Internal")
    oT = _dt(nc, "oT", [512, T], BF16, "ExternalOutput")
    stage_inproj(nc, hT, w, projT, vtok)
    stage_attn(nc, projT, vtok, acst, lam, gn, idb, oT[256:512, :], lamc)
    stage_hgrn(nc, projT, lbo, 0, hc, gnh, idb, oscr, oT[0:256, :])
    _PROGS["LB"] = nc
    return nc


def prog_LC1():
    if "LC1" in _PROGS:
        return _PROGS["LC1"]
    nc = bass.Bass("TRN2", target_bir_lowering=False)
    oT = _dt(nc, "oT", [D, TL], BF16); w_out = _dt(nc, "w_out", [D, D]); xT = _dt(nc, "xT", [D, TL]); par = _dt(nc, "par", [128, 8, 32])
    xTo = _dt(nc, "xTo", [D, TL], F32, "ExternalOutput")
    h2T = _dt(nc, "h2T", [D, TL], BF16, "ExternalOutput")
    stage_outproj(nc, oT, w_out, xT, xTo, par, 2)
    stage_norm(nc, xTo, par, 3, 4, h2T)
    _PROGS["LC1"] = nc
    return nc


def prog_LC2(final):
    key = "LC2f" if final else "LC2"
    if key in _PROGS:
        return _PROGS[key]
    nc = bass.Bass("TRN2", target_bir_lowering=False)
    h2 = _dt(nc, "h2", [D, TL + 2], BF16); w_up = _dt(nc, "w_up", [D, 2 * DFF]); cwp = _dt(nc, "cwp", [128, 4, 128])
    w_down = _dt(nc, "w_down", [DFF, D]); xT = _dt(nc, "xT", [D, TL]); par = _dt(nc, "par", [128, 8, 32])
    mT = _dt(nc, "mT", [DFF, TL], BF16, "Internal")
    xTo = _dt(nc, "xTo", [D, TL], F32, "ExternalOutput")
    stage_ffn_up(nc, h2, w_up, cwp, mT)
    stage_ffn_down(nc, mT, w_down, xT, xTo, par, 5)
    if final:
        idf = _dt(nc, "idf", [128, 128])
        y = _dt(nc, "y", [TL, D], F32, "ExternalOutput")
        stage_norm(nc, xTo, par, 6, None, y, idf_d=idf, final=True)
    else:
        hT = _dt(nc, "hT", [D, TL], BF16, "ExternalOutput")
        stage_norm(nc, xTo, par, 6, 7, hT)
    _PROGS[key] = nc
    return nc


def _run(nc, maps):
    return run_bass_kernel_spmd(nc, maps, core_ids=list(range(NCORE))).results


def kernel(x, c, w_ada, b_ada, ada_table, norm1_g, w_in, hg_lb_logits, hg_norm_g, da_lambda,
           da_norm_g, w_out, norm2_g, w_up, conv_w, conv_b, w_down, final_g):
    f32 = np.float32
    A = lambda a: np.ascontiguousarray(np.asarray(a))
    x = A(x); w_ada = A(w_ada); w_in = np.asarray(w_in); w_out = np.asarray(w_out); w_up = np.asarray(w_up); w_down = np.asarray(w_down)
    x2 = x.reshape(T, D)
    vecs = np.concatenate([np.asarray(c).reshape(1, D), np.asarray(b_ada).reshape(6, D), np.asarray(ada_table).reshape(24, D),
                           np.asarray(norm1_g), np.asarray(norm2_g), np.asarray(final_g).reshape(1, D)], 0).astype(f32).reshape(1280, 128)
    idf = np.eye(128, dtype=f32)
    idb = np.eye(128).astype(ml_dtypes.bfloat16)
    lbl4 = np.asarray(hg_lb_logits).reshape(2, DEPTH, 16, 128)
    hc = hgrn_consts()
    maps = []
    for j in range(NCORE):
        order = [2 * j, 2 * j + 1] + [h for h in range(16) if h not in (2 * j, 2 * j + 1)]
        maps.append({"x": A(x2[j * TL:(j + 1) * TL]), "vecs": vecs, "w_ada": w_ada,
                     "lbl": A(lbl4[:, :, order, :].reshape(128, 128)), "idf": idf})
    r0 = _run(prog_L0(), maps)
    par = r0[0]["par"]
    lbo = [r0[j]["lbo"] for j in range(NCORE)]
    xT = [r0[j]["xT"] for j in range(NCORE)]
    hT = [r0[j]["hT"] for j in range(NCORE)]
    del r0
    y = None
    for l in range(DEPTH):
        lam_init = 0.8 - 0.6 * math.exp(-0.3 * l)
        hT_all = np.concatenate(hT, axis=1)
        lamc = np.tile(np.array([[1.0 - lam_init, -lam_init]], f32), (128, 1))
        maps = []
        for j in range(NCORE):
            cols = np.concatenate([np.arange(g * 2048 + j * 256, g * 2048 + (j + 1) * 256) for g in range(5)] +
                                  [np.arange(10240 + g * 2048 + j * 256, 10240 + g * 2048 + (j + 1) * 256) for g in range(3)])
            lbo_l = np.repeat(lbo[j][:, :, :, l:l + 1, :], DEPTH, axis=3)
            maps.append({"hT": hT_all, "w": A(w_in[l][:, cols]), "acst": attn_consts(j), "lam": A(np.asarray(da_lambda)[l].T.astype(f32)),
                         "lamc": lamc, "gn": A(np.broadcast_to(np.asarray(da_norm_g)[l], (128, 256)).astype(f32)), "idb": idb,
                         "lbo": A(lbo_l), "hc": hc, "gnh": A(np.asarray(hg_norm_g)[l].reshape(128, 1).astype(f32))})
        rb = _run(prog_LB(), maps)
        oT_all = np.empty((D, T), dtype=ml_dtypes.bfloat16)
        for j in range(NCORE):
            oT_all[j * 256:(j + 1) * 256] = rb[j]["oT"][0:256]
            oT_all[2048 + j * 256:2048 + (j + 1) * 256] = rb[j]["oT"][256:512]
        del rb, hT_all
        nxt = par[:, 6 * (l + 1):6 * (l + 1) + 2, :] if l + 1 < DEPTH else np.repeat(par[:, 24:25, :], 2, axis=1)
        par_l = A(np.concatenate([par[:, 6 * l:6 * l + 6, :], nxt], axis=1))
        maps = [{"oT": A(oT_all[:, j * TL:(j + 1) * TL]), "w_out": A(w_out[l]), "xT": xT[j], "par": par_l} for j in range(NCORE)]
        rc = _run(prog_LC1(), maps)
        xT = [rc[j]["xTo"] for j in range(NCORE)]
        h2 = [rc[j]["h2T"] for j in range(NCORE)]
        del rc, oT_all
        zcol = np.zeros((D, 1), dtype=ml_dtypes.bfloat16)
        cwp = A(np.concatenate([np.asarray(conv_w)[l], np.asarray(conv_b)[l][None]], 0).astype(f32).reshape(4, 128, 128).transpose(2, 0, 1))
        maps = []
        for j in range(NCORE):
            left = h2[j - 1][:, -1:] if j > 0 else zcol
            right = h2[j + 1][:, :1] if j + 1 < NCORE else zcol
            m = {"h2": A(np.concatenate([left, h2[j], right], axis=1)), "w_up": A(w_up[l]), "cwp": cwp, "w_down": A(w_down[l]),
                 "xT": xT[j], "par": par_l}
            if l == DEPTH - 1:
                m["idf"] = idf
            maps.append(m)
        rd = _run(prog_LC2(l == DEPTH - 1), maps)
        xT = [rd[j]["xTo"] for j in range(NCORE)]
        if l == DEPTH - 1:
            y = np.concatenate([rd[j]["y"] for j in range(NCORE)], axis=0)
        else:
            hT = [rd[j]["hT"] for j in range(NCORE)]
        del rd
    return y.reshape(1, T, D).astype(f32)
```
